# Optimizing a Trainium2 kernel written in Bass

```python
import math
import jax, jax.numpy as jnp
from jax import lax
import numpy as np

D_MODEL = 1024
BATCH = 16
SEQ = 4096
DEPTH = 2
DEC_BATCH = 16
DEC_SEQ = 2048
PAST_LEN = 128

HEAD_DIM = 64
N_HEADS_ATTN = 6
ATTN_WIDTH = N_HEADS_ATTN * HEAD_DIM
CONV_WIDTH = 256
N_HEADS_RWKV = 6
RWKV_WIDTH = N_HEADS_RWKV * HEAD_DIM
MIX_WIDTH = ATTN_WIDTH + CONV_WIDTH + RWKV_WIDTH
DECAY_RANK = 32
ICLR_RANK = 32
GATE_RANK = 64
RWKV_IN = 3 * RWKV_WIDTH + DECAY_RANK + ICLR_RANK + GATE_RANK
IN_WIDTH = 3 * ATTN_WIDTH + 3 * CONV_WIDTH + RWKV_IN
D_FF = 4 * D_MODEL
DILATED_PATTERNS = ((128, 1), (512, 4), (2048, 16))
QUERY_BLOCK = 64
N_BUCKETS = 32
BUCKET_MAX_DIST = 1024
RMS_EPS = 1e-6
LNX_EPS = 64e-5

kernel_name = "hybrid_dilated_conv_rwkv7_encoder"


def rms_norm(x, g):
    xf = x.astype(jnp.float32)
    y = xf * lax.rsqrt(jnp.mean(xf * xf, axis=-1, keepdims=True) + RMS_EPS)
    return (y * g.astype(jnp.float32)).astype(x.dtype)


def t5_bucket(rel):
    half = N_BUCKETS // 2
    max_exact = half // 2
    ret = np.where(rel > 0, half, 0)
    n = np.abs(rel)
    large = max_exact + (np.log(np.maximum(n, 1) / max_exact)
                         / np.log(BUCKET_MAX_DIST / max_exact) * (half - max_exact)).astype(np.int32)
    large = np.minimum(large, half - 1)
    return (ret + np.where(n < max_exact, n, large)).astype(np.int32)


def dilated_branch(q, k, v, rel_bias, window, dil):
    B, S, H, Dh = q.shape
    R = window // (2 * dil)
    L = S // dil
    qb = math.gcd(L, QUERY_BLOCK)
    nb = L // qb
    W = qb + 2 * R
    qc = q.reshape(B, nb, qb, dil, H, Dh)
    kp = jnp.pad(k.reshape(B, L, dil, H, Dh), ((0, 0), (R, R), (0, 0), (0, 0), (0, 0)))
    vp = jnp.pad(v.reshape(B, L, dil, H, Dh), ((0, 0), (R, R), (0, 0), (0, 0), (0, 0)))
    win = np.arange(nb)[:, None] * qb + np.arange(W)[None, :]
    kw = kp[:, win]
    vw = vp[:, win]
    logits = jnp.einsum('bnqchd,bnwchd->bnchqw', qc, kw).astype(jnp.float32) * (Dh ** -0.5)
    rel = np.arange(W)[None, :] - R - np.arange(qb)[:, None]
    bias = jnp.transpose(rel_bias[t5_bucket(rel * dil)], (2, 0, 1)).astype(jnp.float32)
    gidx = win - R
    valid = (np.abs(rel)[None] <= R) & (gidx[:, None, :] >= 0) & (gidx[:, None, :] < L)
    logits = jnp.where(valid[None, :, None, None], logits + bias, -jnp.inf)
    m = jnp.max(logits, axis=-1)
    p = jnp.exp(logits - m[..., None])
    s = jnp.sum(p, axis=-1)
    o = jnp.einsum('bnchqw,bnwchd->bnqchd', p, vw.astype(jnp.float32))
    m = jnp.transpose(m, (0, 1, 4, 2, 3)).reshape(B, S, H)
    s = jnp.transpose(s, (0, 1, 4, 2, 3)).reshape(B, S, H)
    o = o.reshape(B, S, H, Dh) / s[..., None]
    return m, s, o


def dilated_attention(q, k, v, rel_bias):
    outs = [dilated_branch(q, k, v, rel_bias, wnd, d) for (wnd, d) in DILATED_PATTERNS]
    m_all = jnp.stack([o[0] for o in outs])
    s_all = jnp.stack([o[1] for o in outs])
    o_all = jnp.stack([o[2] for o in outs])
    den = s_all * jnp.exp(m_all - jnp.max(m_all, axis=0))
    out = jnp.sum(den[..., None] * o_all, axis=0) / jnp.sum(den, axis=0)[..., None]
    return out.astype(q.dtype)


def short_conv_mixer(b_gate, c_gate, hv, conv_w):
    u = c_gate * hv
    up = jnp.pad(u, ((0, 0), (1, 1), (0, 0)))
    c = up[:, :-2] * conv_w[0] + up[:, 1:-1] * conv_w[1] + up[:, 2:] * conv_w[2]
    return b_gate * c


def centred_shift(z):
    zp = jnp.pad(z, ((0, 0), (1, 1), (0, 0)))
    return 0.5 * (zp[:, :-2] + zp[:, 2:])


def _wkv_step(S, inp):
    r, w, k, v, a_, b_ = inp
    sa = jnp.einsum('dbhij,dbhj->dbhi', S, a_)
    S = S * w[..., None, :] + sa[..., None] * b_[..., None, :] + v[..., None] * k[..., None, :]
    y = jnp.einsum('dbhij,dbhj->dbhi', S, r)
    return S, y


def rwkv7_mixer(zc, mu, w0, w_up, a0, a_up, g_up, k_k, k_a, r_k, lnx_w, lnx_b):
    dtype = zc.dtype
    B, T, _ = zc.shape
    H, N, C = N_HEADS_RWKV, HEAD_DIM, RWKV_WIDTH
    zc = zc.astype(jnp.float32)
    zc = zc + (centred_shift(zc) - zc) * mu
    r, k, v, wd, ad, gd = jnp.split(zc, [C, 2 * C, 3 * C, 3 * C + DECAY_RANK,
                                         3 * C + DECAY_RANK + ICLR_RANK], axis=-1)
    w_log = -jax.nn.softplus(-(w0[:, None, None, :]
                               + jnp.einsum('btr,drc->dbtc', jnp.tanh(wd), w_up))) - 0.5
    decay = jnp.exp(-jnp.exp(w_log))
    a = jax.nn.sigmoid(a0[:, None, None, :] + jnp.einsum('btr,drc->dbtc', ad, a_up))
    g = jax.nn.sigmoid(gd) @ g_up
    hd = lambda t: t.reshape(t.shape[:-1] + (H, N))
    kk = hd(k * k_k)
    kk = kk / jnp.maximum(jnp.sqrt(jnp.sum(kk * kk, axis=-1, keepdims=True)), 1e-12)
    k_dir = hd(k[None] * (1.0 + (a - 1.0) * k_a))
    b_dir = kk[None] * hd(a)
    r_h, v_h = hd(r), hd(v)

    def time_major(t_dir):
        t_dir = jnp.stack([t_dir[0], jnp.flip(t_dir[1], axis=1)])
        return jnp.transpose(t_dir, (2, 0, 1, 3, 4))

    both = lambda t: jnp.broadcast_to(t[None], (2,) + t.shape)
    xs = (time_major(both(r_h)), time_major(hd(decay)), time_major(k_dir),
          time_major(both(v_h)), time_major(both(-kk)), time_major(b_dir))
    S0 = jnp.zeros((2, B, H, N, N), jnp.float32)
    _, y = lax.scan(_wkv_step, S0, xs)
    y = y[:, 0] + jnp.flip(y[:, 1], axis=0)
    y = jnp.transpose(y, (1, 0, 2, 3))
    mean = jnp.mean(y, axis=-1, keepdims=True)
    var = jnp.mean(jnp.square(y - mean), axis=-1, keepdims=True)
    yn = (y - mean) * lax.rsqrt(var + LNX_EPS) * hd(lnx_w) + hd(lnx_b)
    bonus = jnp.sum(r_h[None] * k_dir * r_k, axis=(0, -1))[..., None] * v_h
    out = (yn + bonus).reshape(B, T, C) * g
    return out.astype(dtype)


def _layer(x, rel_bias, g_mix_pre, g_mix_post, g_ffn_pre, g_ffn_post, w_in, w_out,
           attn_out_g, conv_w, conv_out_g, rwkv_mu, decay_w0, decay_up, iclr_a0, iclr_up,
           gate_up, k_k, k_a, r_k, lnx_w, lnx_b, ffn_w1, ffn_w2):
    B, T, _ = x.shape
    h = rms_norm(x, g_mix_pre)
    z = h @ w_in
    o1 = 3 * ATTN_WIDTH
    o2 = o1 + 3 * CONV_WIDTH
    q, k, v = [t.reshape(B, T, N_HEADS_ATTN, HEAD_DIM) for t in jnp.split(z[..., :o1], 3, axis=-1)]
    b_gate, c_gate, hv = jnp.split(z[..., o1:o2], 3, axis=-1)
    zc = z[..., o2:]
    ya = rms_norm(dilated_attention(q, k, v, rel_bias).reshape(B, T, ATTN_WIDTH), attn_out_g)
    yb = rms_norm(short_conv_mixer(b_gate, c_gate, hv, conv_w), conv_out_g)
    yc = rwkv7_mixer(zc, rwkv_mu, decay_w0, decay_up, iclr_a0, iclr_up, gate_up,
                     k_k, k_a, r_k, lnx_w, lnx_b)
    mix = jnp.concatenate([ya, yb, yc], axis=-1) @ w_out
    x = x + rms_norm(mix, g_mix_post)
    hf = rms_norm(x, g_ffn_pre)
    f = jnp.square(jax.nn.relu(hf @ ffn_w1)) @ ffn_w2
    return x + rms_norm(f, g_ffn_post)


def _trunk(x, rel_bias, layer_params):
    for l in range(DEPTH):
        x = _layer(x, rel_bias, *[p[l] for p in layer_params])
    return x


def setup_inputs(seed: int = 0) -> dict:
    key = jax.random.key(seed)
    ks = jax.random.split(key, 32)
    n = lambda i, shape: jax.random.normal(ks[i], shape, jnp.float32)
    L = DEPTH
    return {
        "x_prompt": n(0, (BATCH, SEQ, D_MODEL)),
        "x_sample": n(1, (DEC_BATCH, DEC_SEQ, D_MODEL)),
        "rel_bias": 0.2 * n(2, (N_BUCKETS, N_HEADS_ATTN)),
        "norm_mix_pre": 1.0 + 0.05 * n(3, (L, D_MODEL)),
        "norm_mix_post": 1.0 + 0.05 * n(4, (L, D_MODEL)),
        "norm_ffn_pre": 1.0 + 0.05 * n(5, (L, D_MODEL)),
        "norm_ffn_post": 1.0 + 0.05 * n(6, (L, D_MODEL)),
        "w_in": n(7, (L, D_MODEL, IN_WIDTH)) * D_MODEL ** -0.5,
        "w_out": n(8, (L, MIX_WIDTH, D_MODEL)) * MIX_WIDTH ** -0.5,
        "attn_out_g": 1.0 + 0.05 * n(9, (L, ATTN_WIDTH)),
        "conv_w": n(10, (L, 3, CONV_WIDTH)) * 3 ** -0.5,
        "conv_out_g": 1.0 + 0.05 * n(11, (L, CONV_WIDTH)),
        "rwkv_mu": jax.random.uniform(ks[12], (L, RWKV_IN), jnp.float32),
        "decay_w0": jnp.linspace(-6.0, -0.5, RWKV_WIDTH, dtype=jnp.float32)[None, None, :]
                    + 0.3 * n(13, (L, 2, RWKV_WIDTH)),
        "decay_up": 0.1 * n(14, (L, 2, DECAY_RANK, RWKV_WIDTH)),
        "iclr_a0": 0.5 * n(15, (L, 2, RWKV_WIDTH)),
        "iclr_up": 0.5 * n(16, (L, 2, ICLR_RANK, RWKV_WIDTH)) * ICLR_RANK ** -0.5,
        "gate_up": n(17, (L, GATE_RANK, RWKV_WIDTH)) * GATE_RANK ** -0.5,
        "k_k": 0.85 + 0.1 * n(18, (L, RWKV_WIDTH)),
        "k_a": 1.0 + 0.1 * n(19, (L, RWKV_WIDTH)),
        "r_k": 0.1 * n(20, (L, N_HEADS_RWKV, HEAD_DIM)),
        "lnx_w": 1.0 + 0.05 * n(21, (L, RWKV_WIDTH)),
        "lnx_b": 0.02 * n(22, (L, RWKV_WIDTH)),
        "ffn_w1": n(23, (L, D_MODEL, D_FF)) * D_MODEL ** -0.5,
        "ffn_w2": n(24, (L, D_FF, D_MODEL)) * D_FF ** -0.5,
    }


def reference(x_prompt, x_sample, rel_bias, norm_mix_pre, norm_mix_post, norm_ffn_pre,
              norm_ffn_post, w_in, w_out, attn_out_g, conv_w, conv_out_g, rwkv_mu, decay_w0,
              decay_up, iclr_a0, iclr_up, gate_up, k_k, k_a, r_k, lnx_w, lnx_b, ffn_w1, ffn_w2):
    layer_params = (norm_mix_pre, norm_mix_post, norm_ffn_pre, norm_ffn_post, w_in, w_out,
                    attn_out_g, conv_w, conv_out_g, rwkv_mu, decay_w0, decay_up, iclr_a0,
                    iclr_up, gate_up, k_k, k_a, r_k, lnx_w, lnx_b, ffn_w1, ffn_w2)
    y_prompt = _trunk(x_prompt, rel_bias, layer_params)
    y_sample = _trunk(x_sample, rel_bias, layer_params)
    return (y_prompt, y_sample)
```

```python
import math
import os
from contextlib import ExitStack

import numpy as np
import ml_dtypes

import concourse.bass as bass
import concourse.mybir as mybir
from concourse.bass_utils import run_bass_kernel_spmd

F32 = mybir.dt.float32
BF16 = mybir.dt.bfloat16
AF = mybir.ActivationFunctionType
ALU = mybir.AluOpType
NPBF = ml_dtypes.bfloat16

D = 1024
INW = 3200
DFF = 4096
NL = 2
RMS_EPS = 1e-6
LNX_EPS = 64e-5
DILS = (1, 4, 16)
KPAD = 1024
NEG = -30000.0
WSCALE = -math.exp(-0.5)


class Tile:
    __slots__ = ("ap", "lw", "rd", "excl")

    def __init__(self, ap, excl=False):
        self.ap = ap
        self.lw = None
        self.rd = {}
        self.excl = excl

    def __getitem__(self, idx):
        return self.ap[idx]


class Sub:
    __slots__ = ("ap", "parent")

    def __init__(self, parent, ap):
        self.parent = parent
        self.ap = ap

    def __getitem__(self, idx):
        return self.ap[idx]

    @property
    def excl(self):
        return self.parent.excl

    @property
    def lw(self):
        return self.parent.lw

    @lw.setter
    def lw(self, v):
        self.parent.lw = v

    @property
    def rd(self):
        return self.parent.rd

    @rd.setter
    def rd(self, v):
        self.parent.rd = v


class Prog:
    ENG = ("pe", "dve", "act", "pool", "sp")
    NLANES = {"sp": 20, "pool": 10}

    def __init__(self):
        self.ops = {e: [] for e in self.ENG}
        self.cnt = {e: 0 for e in self.ENG}
        self.seen = {e: {} for e in self.ENG}
        self.lane_val = {}
        self.lane_rr = {q: 0 for q in self.NLANES}
        for q, n in self.NLANES.items():
            for i in range(n):
                self.lane_val[f"d_{q}{i}"] = 0

    def sem_names(self):
        return list(self.ENG[:4]) + list(self.lane_val.keys())

    def _collect(self, eng, reads, writes, extra=()):
        w = {}
        seen = self.seen[eng]

        def add(s, v, is_rd=False):
            if s == eng and eng == "pe":
                return
            if seen.get(s, 0) >= v:
                return
            if w.get(s, 0) < v:
                w[s] = v
        for t in reads:
            if t.lw is not None:
                add(t.lw[0], t.lw[1])
            if t.excl:
                for s, v in t.rd.items():
                    if s != eng:
                        add(s, v, True)
        for t in writes:
            if t.lw is not None:
                add(t.lw[0], t.lw[1])
            for s, v in t.rd.items():
                add(s, v, True)
        for s, v in extra:
            add(s, v)
        for s, v in w.items():
            seen[s] = v
        return list(w.items())

    def _commit(self, tok, reads, writes):
        s, v = tok
        for t in reads:
            if t.rd.get(s, 0) < v:
                t.rd[s] = v
        for t in writes:
            t.lw = tok
            t.rd = {}

    def op(self, eng, fn, reads=(), writes=()):
        waits = self._collect(eng, reads, writes)
        self.cnt[eng] += 1
        tok = (eng, self.cnt[eng])
        self.ops[eng].append((waits, fn, (eng, 1)))
        self._commit(tok, reads, writes)
        return tok

    def dma(self, out_ap, in_ap, reads=(), writes=(), q="sp", **kw):
        n = self.NLANES[q]
        lane = f"d_{q}{self.lane_rr[q] % n}"
        self.lane_rr[q] += 1
        extra = []
        if self.lane_val[lane] > 0:
            extra.append((lane, self.lane_val[lane]))
        waits = self._collect(q, reads, writes, extra)
        self.lane_val[lane] += 16
        tok = (lane, self.lane_val[lane])
        self.ops[q].append((waits, lambda e: e.dma_start(out=out_ap, in_=in_ap, **kw), (lane, 16)))
        self._commit(tok, reads, writes)
        return tok

    def barrier(self):
        toks = [(e, self.cnt[e]) for e in self.ENG[:4] if self.cnt[e] > 0]
        toks += [(l, v) for l, v in self.lane_val.items() if v > 0]
        for e in self.ENG:
            waits = []
            for s, v in toks:
                if self.seen[e].get(s, 0) < v:
                    waits.append((s, v))
                    self.seen[e][s] = v
            if waits:
                self.ops[e].append((waits, None, None))

    def emit(self, nc, stack):
        semh = {}
        for s in self.sem_names():
            semh[s] = stack.enter_context(nc.semaphore(s))
        block = stack.enter_context(nc.Block())
        ops = self.ops

        def run(engobj, lst):
            for waits, fn, inc in lst:
                for s, v in waits:
                    engobj.wait_ge(semh[s], v)
                if fn is not None:
                    fn(engobj).then_inc(semh[inc[0]], inc[1])

        @block.tensor
        def _(e):
            run(e, ops["pe"])

        @block.vector
        def _(e):
            run(e, ops["dve"])

        @block.scalar
        def _(e):
            run(e, ops["act"])

        @block.gpsimd
        def _(e):
            run(e, ops["pool"])

        @block.sync
        def _(e):
            run(e, ops["sp"])


class Arena:
    def __init__(self, ap_f32, nwords):
        self.base = ap_f32
        self.n = nwords
        self.off = 0
        self.peak = 0

    def alloc(self, free, dtype=F32):
        free = list(free)
        nel = int(np.prod(free))
        words = nel if dtype == F32 else (nel + 1) // 2
        words = (words + 1) // 2 * 2
        assert self.off + words <= self.n, f"arena overflow {self.off}+{words}>{self.n}"
        v = self.base[:, self.off:self.off + words]
        self.off += words
        self.peak = max(self.peak, self.off)
        if dtype != F32:
            v = v.bitcast(dtype)[:, 0:nel]
        if len(free) == 2:
            v = v.rearrange("p (a b) -> p a b", a=free[0])
        elif len(free) == 3:
            v = v.rearrange("p (a b c) -> p a b c", a=free[0], b=free[1])
        elif len(free) == 4:
            v = v.rearrange("p (a b c d) -> p a b c d", a=free[0], b=free[1], c=free[2])
        return Tile(v)

    def mark(self):
        return self.off

    def release(self, m):
        self.off = m


def _t5_bucket(rel):
    half = 16
    max_exact = 8
    ret = np.where(rel > 0, half, 0)
    n = np.abs(rel)
    large = max_exact + (np.log(np.maximum(n, 1) / max_exact) / np.log(1024 / max_exact) * (half - max_exact)).astype(np.int32)
    large = np.minimum(large, half - 1)
    return (ret + np.where(n < max_exact, n, large)).astype(np.int32)


def _consts():
    c = {}
    c["c_identb"] = np.eye(128).astype(NPBF)
    c["c_identf"] = np.eye(128).astype(np.float32)
    blk = np.zeros((128, 128), np.float32)
    blk[:64, :64] = 1.0
    blk[64:, 64:] = 1.0
    c["c_blk"] = blk
    c["c_onesb"] = np.ones((128, 128), NPBF)
    s = np.arange(128)[:, None]
    t = np.arange(128)[None, :]
    same = (s // 64) == (t // 64)
    mat = np.zeros((2, 128, 4, 128), np.float32)
    mx0 = np.zeros((2, 128, 128), np.float32)
    for d in range(2):
        if d == 0:
            strictT = same & (s < t)
            inclT = same & (s <= t)
        else:
            strictT = same & (s > t)
            inclT = same & (s >= t)
        mat[d, :, 0] = strictT
        mat[d, :, 1] = inclT
        mat[d, :, 2] = strictT
        mat[d, :, 3] = inclT
        mx0[d] = strictT.T
    c["c_maskat"] = mat.astype(NPBF)
    c["c_maskx0"] = mx0.astype(NPBF)
    seg = np.ones((128, 512), np.float32)
    seg[:, ::64] = 0.0
    c["c_seg"] = seg
    oh = np.zeros((3, 33, 384), np.float32)
    for bi, dil in enumerate(DILS):
        for n in range(383):
            rel = 191 - n
            if abs(rel) <= 64:
                oh[bi, _t5_bucket(np.array(rel * dil)), n] = 8.0
            else:
                oh[bi, 32, n] = 8.0 * NEG
    c["c_onehot"] = oh
    misc = np.zeros((128, 8), np.float32)
    misc[:64, 1] = NEG
    misc[64:, 2] = NEG
    misc[:, 3] = RMS_EPS
    misc[:, 4] = LNX_EPS
    misc[:, 5] = 1.0
    c["c_misc"] = misc
    return c


CONST_SPECS = [
    ("c_identb", [128, 128], BF16), ("c_identf", [128, 128], F32), ("c_blk", [128, 128], F32),
    ("c_onesb", [128, 128], BF16), ("c_maskat", [2, 128, 4, 128], BF16), ("c_maskx0", [2, 128, 128], BF16),
    ("c_seg", [128, 512], F32), ("c_onehot", [3, 33, 384], F32), ("c_misc", [128, 8], F32),
]

WEIGHT_SPECS = [
    ("rel_bias", [32, 6]), ("norm_mix_pre", [NL, D]), ("norm_mix_post", [NL, D]), ("norm_ffn_pre", [NL, D]),
    ("norm_ffn_post", [NL, D]), ("w_in", [NL, D, INW]), ("w_out", [NL, D, D]), ("attn_out_g", [NL, 384]),
    ("conv_w", [NL, 3, 256]), ("conv_out_g", [NL, 256]), ("rwkv_mu", [NL, 1280]), ("decay_w0", [NL, 2, 384]),
    ("decay_up", [NL, 2, 32, 384]), ("iclr_a0", [NL, 2, 384]), ("iclr_up", [NL, 2, 32, 384]),
    ("gate_up", [NL, 64, 384]), ("k_k", [NL, 384]), ("k_a", [NL, 384]), ("r_k", [NL, 6, 64]),
    ("lnx_w", [NL, 384]), ("lnx_b", [NL, 384]), ("ffn_w1", [NL, D, DFF]), ("ffn_w2", [NL, DFF, D]),
]


class Builder:
    def __init__(self, seq_lens, debug=False, stop_after=None):
        self.seq_lens = list(seq_lens)
        self.debug = debug
        self.stop_after = stop_after
        self.nc = bass.Bass("TRN2", target_bir_lowering=False)
        self.P = Prog()
        self.rr = 0

    def dram(self, name, shape, dtype, kind="Internal"):
        if kind == "Internal" and self.debug:
            kind = "ExternalOutput"
        return self.nc.dram_tensor(name, list(shape), dtype, kind=kind).ap()

    @staticmethod
    def rawap(ap, offset, pat):
        return bass.AP(tensor=ap.tensor, offset=offset, ap=[list(x) for x in pat])

    def mm(self, ot, o, l, r, rd, start=True, stop=True):
        self.P.op("pe", lambda e: e.matmul(o, lhsT=l, rhs=r, start=start, stop=stop), reads=rd, writes=[ot])

    def tr(self, ot, o, i, rd):
        idt = self.identb if i.dtype == BF16 else self.identf
        n = i.shape[0]
        ident = idt[0:n, 0:n]
        self.P.op("pe", lambda e: e.transpose(o, i, ident), reads=list(rd) + [idt], writes=[ot])

    def act(self, ot, o, i, func, rd, bias=None, scale=None, accum=None, wr=()):
        kw = {}
        if bias is not None:
            kw["bias"] = bias
        if scale is not None:
            kw["scale"] = scale
        if accum is not None:
            kw["accum_out"] = accum
        self.P.op("act", lambda e: e.activation(out=o, in_=i, func=func, **kw), reads=rd, writes=[ot] + list(wr))

    def tt(self, eng, ot, o, a, b, op, rd):
        self.P.op(eng, lambda e: e.tensor_tensor(out=o, in0=a, in1=b, op=op), reads=rd, writes=[ot])

    def ts(self, eng, ot, o, a, s1, op0, rd, s2=None, op1=None):
        if op1 is None:
            self.P.op(eng, lambda e: e.tensor_scalar(out=o, in0=a, scalar1=s1, scalar2=None, op0=op0), reads=rd, writes=[ot])
        else:
            self.P.op(eng, lambda e: e.tensor_scalar(out=o, in0=a, scalar1=s1, scalar2=s2, op0=op0, op1=op1), reads=rd, writes=[ot])

    def stt(self, ot, o, a, s, b, op0, op1, rd):
        self.P.op("dve", lambda e: e.scalar_tensor_tensor(out=o, in0=a, scalar=s, in1=b, op0=op0, op1=op1), reads=rd, writes=[ot])

    def cp(self, eng, ot, o, i, rd):
        if eng == "act":
            self.P.op("act", lambda e: e.activation(out=o, in_=i, func=AF.Copy), reads=rd, writes=[ot])
        else:
            self.P.op(eng, lambda e: e.tensor_copy(out=o, in_=i), reads=rd, writes=[ot])

    def evac(self, ot, o, i, rd):
        self.rr += 1
        self.cp("act" if self.rr % 2 else "dve", ot, o, i, rd)

    def memset(self, eng, ot, o, val):
        self.P.op(eng, lambda e: e.memset(o, val), writes=[ot])

    def recip(self, ot, o, i, rd):
        self.P.op("dve", lambda e: e.reciprocal(out=o, in_=i), reads=rd, writes=[ot])

    def scan(self, ot, o, d0, d1, rd):
        self.P.op("dve", lambda e: e.tensor_tensor_scan(out=o, data0=d0, data1=d1, initial=0.0, op0=ALU.mult, op1=ALU.add), reads=rd, writes=[ot])

    def dma(self, o, i, rd=(), wr=(), q="sp", **kw):
        self.P.dma(o, i, reads=rd, writes=wr, q=q, **kw)

    def build(self):
        nc = self.nc
        self.xin = []
        self.yout = []
        for si, T in enumerate(self.seq_lens):
            self.xin.append(nc.dram_tensor(f"x{si}", [T, D], F32, kind="ExternalInput").ap())
            self.yout.append(nc.dram_tensor(f"y{si}", [T, D], F32, kind="ExternalOutput").ap())
        self.w = {}
        for name, shape in WEIGHT_SPECS:
            self.w[name] = nc.dram_tensor(name, shape, F32, kind="ExternalInput").ap()
        self.c = {}
        for name, shape, dt in CONST_SPECS:
            self.c[name] = nc.dram_tensor(name, shape, dt, kind="ExternalInput").ap()
        TM = max(self.seq_lens)
        self.TM = TM
        self.winb = [self.dram(f"winb{l}", [D, INW], BF16) for l in range(NL)]
        self.woutb = [self.dram(f"woutb{l}", [D, D], BF16) for l in range(NL)]
        self.w1b = [self.dram(f"w1b{l}", [D, DFF], BF16) for l in range(NL)]
        self.w2b = [self.dram(f"w2b{l}", [DFF, D], BF16) for l in range(NL)]
        self.s_x1 = self.dram("s_x1", [TM, D], F32)
        self.s_v = self.dram("s_v", [TM, 390], BF16)
        self.s_b = self.dram("s_b", [256, TM], F32)
        self.s_u = self.dram("s_u", [256, TM], F32)
        self.s_zc = self.dram("s_zc", [1280, TM], F32)
        self.s_o = self.dram("s_o", [3, TM, 390], F32)
        self.s_mix = self.dram("s_mix", [D, TM], BF16)
        self.s_yp = self.dram("s_yp", [384, TM], F32)
        self.s_bon = self.dram("s_bon", [384, TM], F32)
        self.s_g = self.dram("s_g", [384, TM], F32)
        self.s_rhb = self.dram("s_rhb", [384, TM], BF16)
        self.s_gtb = self.dram("s_gtb", [TM // 512, 128, 3 * 8 * 128], BF16)
        self.s_hb = self.dram("s_hb", [TM // 512, 128, 3 * 8 * 128], F32)
        self.s_gb = self.dram("s_gb", [18, 384], F32)
        self.s_skew = self.dram("s_skew", [18, 128 * 385], F32)
        self.s_bias = self.dram("s_bias", [128, 18 * 256], BF16)

        with ExitStack() as st:
            arena_t = st.enter_context(nc.sbuf_tensor("arena", [128, 51200], F32))
            psum_t = st.enter_context(nc.psum_tensor("psum", [128, 4096], F32))
            self.A = Arena(arena_t[:], 51200)
            self.psum = psum_t
            self.pb = [Tile(psum_t[:, i * 512:(i + 1) * 512], excl=True) for i in range(8)]
            self.setup_consts()
            m0 = self.A.mark()
            done = False
            for l in range(NL):
                self.layer_params(l)
                self.weight_prep(l)
                self.P.barrier()
            for si, T in enumerate(self.seq_lens):
                for l in range(NL):
                    self.A.release(m0)
                    src = self.xin[si] if l == 0 else self.s_x1
                    dst = self.s_x1 if l == 0 else self.yout[si]
                    self.layer_params(l)
                    self.P.barrier()
                    self.seq_layer(T, l, src, dst)
                    self.P.barrier()
                    if self.stop_after is not None:
                        done = True
                        break
                if done:
                    break
            self.P.barrier()
            self.P.emit(nc, st)
        return nc

    def setup_consts(self):
        A = self.A
        self.identb = A.alloc([128], BF16)
        self.identf = A.alloc([128], F32)
        self.blk = A.alloc([128], F32)
        self.blk64 = A.alloc([128], F32)
        self.onesb = A.alloc([128], BF16)
        self.maskat = [A.alloc([4, 128], BF16) for _ in range(2)]
        self.maskx0 = [A.alloc([128], BF16) for _ in range(2)]
        self.seg = A.alloc([512], F32)
        self.misc = A.alloc([8], F32)
        self.pv = A.alloc([128], F32)
        self.lr = A.alloc([5, 384], F32)
        c = self.c
        self.dma(self.identb[:], c["c_identb"], wr=[self.identb])
        self.dma(self.identf[:], c["c_identf"], wr=[self.identf])
        self.dma(self.blk[:], c["c_blk"], wr=[self.blk])
        self.dma(self.onesb[:], c["c_onesb"], wr=[self.onesb])
        for d in range(2):
            self.dma(self.maskat[d][:], c["c_maskat"][d], wr=[self.maskat[d]])
            self.dma(self.maskx0[d][:], c["c_maskx0"][d], wr=[self.maskx0[d]])
        self.dma(self.seg[:], c["c_seg"], wr=[self.seg])
        self.dma(self.misc[:], c["c_misc"], wr=[self.misc])
        self.ts("pool", self.blk64, self.blk64[:], self.blk[:], 1.0 / 64, ALU.mult, [self.blk])
        self.zero_c = self.misc[:, 0:1]
        self.edge_first = self.misc[:, 1:2]
        self.edge_last = self.misc[:, 2:3]
        self.eps_rms = self.misc[:, 3:4]
        self.eps_ln = self.misc[:, 4:5]
        m = A.mark()
        relb = A.alloc([6], F32)
        oh = A.alloc([3, 384], F32)
        gsb = A.alloc([3, 384], F32)
        bf = A.alloc([18, 256], F32)
        bb16 = A.alloc([18, 256], BF16)
        self.memset("pool", relb, relb[:], 1.0)
        self.dma(relb[0:32, :], self.w["rel_bias"], wr=[relb])
        self.dma(oh[0:33, :, :], c["c_onehot"].rearrange("b k n -> k b n"), wr=[oh])
        gbT = Tile(self.s_gb)
        for bi in range(3):
            ps = self.pb[bi]
            self.mm(ps, ps[0:6, 0:384], relb[0:33, 0:6], oh[0:33, bi, :], [relb, oh])
            self.cp("dve", gsb, gsb[0:6, bi, :], ps[0:6, 0:384], [ps])
            self.dma(self.s_gb[bi * 6:(bi + 1) * 6, :], gsb[0:6, bi, :], rd=[gsb], wr=[gbT])
        skT = Tile(self.s_skew)
        self.dma(self.rawap(self.s_skew, 0, [[128 * 385, 18], [385, 128], [1, 384]]),
                 self.rawap(self.s_gb, 0, [[384, 18], [0, 128], [1, 384]]), rd=[gbT], wr=[skT])
        for r in range(18):
            self.dma(bf[:, r, :].rearrange("p (a b) -> p a b", a=2),
                     self.rawap(self.s_skew, r * 128 * 385 + 255, [[384, 128], [-128, 2], [1, 128]]), rd=[skT], wr=[bf])
        self.cp("dve", bb16, bb16[:], bf[:], [bf])
        self.dma(self.s_bias, bb16[:].rearrange("p a b -> p (a b)"), rd=[bb16])
        self.P.barrier()
        A.release(m)

    PV = dict(m1=0, m2=10, kk=20, ka=23, oka=26, rk=29, lw=32, lb=35, w0=38, a0=44, cw=50, gmp=56, gfp=64, gwo=72, mu=80)

    def layer_params(self, l):
        pv, w, PV = self.pv, self.w, self.PV

        def col(dst0, src_ap, n):
            self.dma(pv[:, dst0:dst0 + n], src_ap.rearrange("(c p) -> p c", p=128), wr=[pv], allow_slow_non_contiguous=True)
        col(PV["mu"], w["rwkv_mu"][l], 10)
        col(PV["kk"], w["k_k"][l], 3)
        col(PV["ka"], w["k_a"][l], 3)
        col(PV["rk"], w["r_k"][l].rearrange("h n -> (h n)"), 3)
        col(PV["lw"], w["lnx_w"][l], 3)
        col(PV["lb"], w["lnx_b"][l], 3)
        for d in range(2):
            col(PV["w0"] + 3 * d, w["decay_w0"][l, d], 3)
            col(PV["a0"] + 3 * d, w["iclr_a0"][l, d], 3)
        for tap in range(3):
            self.dma(pv[:, PV["cw"] + tap:PV["cw"] + tap + 4:3], w["conv_w"][l, tap].rearrange("(c p) -> p c", p=128), wr=[pv],
                     allow_slow_non_contiguous=True)
        col(PV["gmp"], w["norm_mix_pre"][l], 8)
        col(PV["gfp"], w["norm_ffn_pre"][l], 8)
        self.memset("pool", pv, pv[:, PV["gwo"]:PV["gwo"] + 8], 1.0)
        col(PV["gwo"], w["attn_out_g"][l], 3)
        col(PV["gwo"] + 3, w["conv_out_g"][l], 2)
        self.ts("pool", pv, pv[:, PV["m1"]:PV["m1"] + 10], pv[:, PV["mu"]:PV["mu"] + 10], -1.0, ALU.mult, [pv], 1.0, ALU.add)
        self.ts("pool", pv, pv[:, PV["m2"]:PV["m2"] + 10], pv[:, PV["mu"]:PV["mu"] + 10], 0.5, ALU.mult, [pv])
        self.ts("pool", pv, pv[:, PV["oka"]:PV["oka"] + 3], pv[:, PV["ka"]:PV["ka"] + 3], -1.0, ALU.mult, [pv], 1.0, ALU.add)
        lr = self.lr
        self.memset("pool", lr, lr[:], 0.0)
        for d in range(2):
            self.dma(lr[0:32, d, :], w["decay_up"][l, d], wr=[lr])
            self.dma(lr[32:64, 2 + d, :], w["iclr_up"][l, d], wr=[lr])
        self.dma(lr[64:128, 4, :], w["gate_up"][l], wr=[lr])

    def weight_prep(self, l):
        A, w, PV = self.A, self.w, self.PV
        m = A.mark()
        wi = [A.alloc([4096], F32) for _ in range(2)]
        wo = [A.alloc([4096], BF16) for _ in range(2)]
        jobs = []
        for rb in range(8):
            jobs.append((w["w_in"][l, rb * 128:(rb + 1) * 128, :], self.winb[l][rb * 128:(rb + 1) * 128, :], INW, PV["gmp"] + rb))
        for rb in range(8):
            jobs.append((w["w_out"][l, rb * 128:(rb + 1) * 128, :], self.woutb[l][rb * 128:(rb + 1) * 128, :], D, PV["gwo"] + rb))
        for rb in range(8):
            jobs.append((w["ffn_w1"][l, rb * 128:(rb + 1) * 128, :], self.w1b[l][rb * 128:(rb + 1) * 128, :], DFF, PV["gfp"] + rb))
        for rb in range(8):
            src = w["ffn_w2"][l, rb * 512:(rb + 1) * 512, :].rearrange("(a p) c -> p a c", p=128)
            dst = self.w2b[l][rb * 512:(rb + 1) * 512, :].rearrange("(a p) c -> p a c", p=128)
            jobs.append((src, dst, 4096, None))
        engs = ["act", "dve", "pool"]
        for i, (src, dst, ncol, gcol) in enumerate(jobs):
            a, b = wi[i % 2], wo[i % 2]
            if gcol is None:
                self.dma(a[:].rearrange("p (a c) -> p a c", a=4), src, wr=[a])
                self.cp(engs[i % 3], b, b[:], a[:], [a])
                self.dma(dst, b[:].rearrange("p (a c) -> p a c", a=4), rd=[b])
            else:
                self.dma(a[:, 0:ncol], src, wr=[a])
                e = engs[i % 3]
                if e == "act":
                    self.act(b, b[:, 0:ncol], a[:, 0:ncol], AF.Copy, [a, self.pv], scale=self.pv[:, gcol:gcol + 1])
                else:
                    self.ts(e, b, b[:, 0:ncol], a[:, 0:ncol], self.pv[:, gcol:gcol + 1], ALU.mult, [a, self.pv])
                self.dma(dst, b[:, 0:ncol], rd=[b])
        A.release(m)

    def seq_layer(self, T, l, src, dst):
        A = self.A
        m0 = A.mark()
        self.qres = [A.alloc([3, self.TM], BF16) for _ in range(2)]
        self.kres = A.alloc([3, self.TM + 2 * KPAD], BF16)
        m1 = A.mark()
        self.phase_a(T, l, src)
        self.P.barrier()
        A.release(m1)
        if self.stop_after == "a":
            return
        self.phase_attn(T)
        self.P.barrier()
        A.release(m0)
        if self.stop_after == "attn":
            return
        self.phase_merge(T)
        self.P.barrier()
        A.release(m0)
        if self.stop_after == "merge":
            return
        self.phase_conv(T)
        self.P.barrier()
        A.release(m0)
        if self.stop_after == "conv":
            return
        self.phase_r1(T)
        self.P.barrier()
        A.release(m0)
        if self.stop_after == "r1":
            return
        self.phase_r2(T)
        self.P.barrier()
        A.release(m0)
        if self.stop_after == "r2":
            return
        self.phase_c(T, l, src, dst)
        A.release(m0)

    def norm_transpose(self, xt, ht, hT, ss, rs, junk, ptiles):
        for j in range(4):
            self.act(junk, junk[:], xt[:, j, :], AF.Square, [xt], accum=ss[:, j:j + 1], wr=[ss])
        self.act(rs, rs[:], ss[:], AF.Sqrt, [ss, self.misc], bias=self.eps_rms, scale=1.0 / D)
        self.recip(rs, rs[:], rs[:], [rs])
        for j in range(4):
            if j % 2 == 0:
                self.act(ht, ht[:, j, :], xt[:, j, :], AF.Copy, [xt, rs], scale=rs[:, j:j + 1])
            else:
                self.ts("dve", ht, ht[:, j, :], xt[:, j, :], rs[:, j:j + 1], ALU.mult, [xt, rs])
        for kc in range(8):
            ps = ptiles[kc % len(ptiles)]
            pv = ps.ap.bitcast(BF16)
            for j in range(4):
                self.tr(ps, pv[:, j * 128:(j + 1) * 128], ht[:, j, kc * 128:(kc + 1) * 128], [ht])
            self.evac(hT, hT[:, kc, :], pv[:, 0:512], [ps])

    def phase_a(self, T, l, src):
        A, pb = self.A, self.pb
        nt = T // 512
        xt = [A.alloc([4, 1024], F32) for _ in range(2)]
        ht = A.alloc([4, 1024], BF16)
        hT = [A.alloc([8, 512], BF16) for _ in range(2)]
        junk = A.alloc([1024], BF16)
        ss = A.alloc([4], F32)
        rs = A.alloc([4], F32)
        wp = [A.alloc([8, 512], BF16) for _ in range(2)]
        zst = [A.alloc([512], F32) for _ in range(4)]
        csb = [A.alloc([512], F32) for _ in range(2)]
        vst = [A.alloc([6, 65], BF16) for _ in range(4)]
        for v in vst:
            self.memset("pool", v, v[:], 1.0)
        kres, qres = self.kres, self.qres
        self.memset("pool", qres[0], qres[0][64:128, :, :], 0.0)
        self.memset("pool", qres[1], qres[1][0:64, :, :], 0.0)
        self.memset("pool", kres, kres[:, :, 0:KPAD], 0.0)
        self.memset("pool", kres, kres[:, :, KPAD + T:KPAD + T + KPAD], 0.0)
        winb = self.winb[l]
        sv, sb_, su, szc = Tile(self.s_v), Tile(self.s_b), Tile(self.s_u), Tile(self.s_zc)
        zi = 0
        wi = 0
        vi = 0
        for tt in range(nt):
            t0 = tt * 512
            x = xt[tt % 2]
            h = hT[tt % 2]
            self.dma(x[:], src[t0:t0 + 512, :].rearrange("(j p) d -> p j d", p=128), wr=[x])
            self.norm_transpose(x, ht, h, ss, rs, junk, [pb[0], pb[1]])
            for pc in range(7):
                ncol = 512 if pc < 6 else 128
                wt = wp[wi % 2]
                wi += 1
                self.dma(wt[:, :, 0:ncol], winb[:, pc * 512:pc * 512 + ncol].rearrange("(kc p) c -> p kc c", p=128), wr=[wt])
                for cc in range(ncol // 128):
                    zc = pc * 4 + cc
                    if 6 <= zc <= 8:
                        continue
                    ps = pb[2 + (zc % 4)]
                    for kc in range(8):
                        self.mm(ps, ps[:], wt[:, kc, cc * 128:(cc + 1) * 128], h[:, kc, :], [wt, h], start=(kc == 0), stop=(kc == 7))
                    if zc < 3:
                        self.cp("act", qres[0], qres[0][0:64, zc, t0:t0 + 512], ps[0:64, :], [ps])
                        self.cp("dve", qres[1], qres[1][64:128, zc, t0:t0 + 512], ps[64:128, :], [ps])
                    elif zc < 6:
                        self.evac(kres, kres[:, zc - 3, KPAD + t0:KPAD + t0 + 512], ps[:], [ps])
                    elif zc < 11:
                        z = zst[zi % 4]
                        zi += 1
                        self.evac(z, z[:], ps[:], [ps])
                        self.dma(self.s_b[(zc - 9) * 128:(zc - 8) * 128, t0:t0 + 512], z[:], rd=[z], wr=[sb_])
                    elif zc < 13:
                        self.evac(csb[zc - 11], csb[zc - 11][:], ps[:], [ps])
                    elif zc < 15:
                        z = zst[zi % 4]
                        zi += 1
                        self.tt("dve", z, z[:], ps[:], csb[zc - 13][:], ALU.mult, [ps, csb[zc - 13]])
                        self.dma(self.s_u[(zc - 13) * 128:(zc - 12) * 128, t0:t0 + 512], z[:], rd=[z], wr=[su])
                    else:
                        z = zst[zi % 4]
                        zi += 1
                        self.evac(z, z[:], ps[:], [ps])
                        self.dma(self.s_zc[(zc - 15) * 128:(zc - 14) * 128, t0:t0 + 512], z[:], rd=[z], wr=[szc])
                if pc in (1, 2):
                    c0, nv, h0 = (256, 256, 0) if pc == 1 else (0, 128, 4)
                    for j in range(4):
                        ps = pb[6 + (j % 2)]
                        for kc in range(8):
                            self.mm(ps, ps[:, 0:nv], h[:, kc, j * 128:(j + 1) * 128], wt[:, kc, c0:c0 + nv], [wt, h], start=(kc == 0), stop=(kc == 7))
                        if pc == 1:
                            v = vst[j]
                            self.evac(v, v[:, 0:4, 0:64], ps[:, 0:256].rearrange("p (a b) -> p a b", a=4), [ps])
                        else:
                            v = vst[j]
                            self.evac(v, v[:, 4:6, 0:64], ps[:, 0:128].rearrange("p (a b) -> p a b", a=2), [ps])
                            self.dma(self.s_v[t0 + j * 128:t0 + (j + 1) * 128, :], v[:].rearrange("p a b -> p (a b)"), rd=[v], wr=[sv])

    def phase_attn(self, T):
        A, pb = self.A, self.pb
        qres, kres = self.qres, self.kres
        vbuf = [A.alloc([9, 390], BF16) for _ in range(2)]
        for v in vbuf:
            self.memset("pool", v, v[:], 1.0)
        self.bias = A.alloc([18, 256], BF16)
        self.dma(self.bias[:].rearrange("p a b -> p (a b)"), self.s_bias, wr=[self.bias])
        sbl = [A.alloc([256], F32) for _ in range(3)]
        pT = [A.alloc([256], BF16) for _ in range(3)]
        ost = [A.alloc([390], F32) for _ in range(2)]
        lg = [Sub(pb[i], pb[i][:, 0:256]) for i in (0, 1, 4, 5)]
        ops_ = [pb[2], pb[3]]
        so = Tile(self.s_o)
        sv = Tile(self.s_v)
        ui = 0
        bi_ = 0
        li = 0
        for br, dil in enumerate(DILS):
            L = T // dil
            nblk = L // 128
            for c in range(dil):
                for g0 in range(0, nblk, 8):
                    nb = min(8, nblk - g0)
                    vb = vbuf[ui % 2]
                    ui += 1
                    for i in range(nb + 1):
                        kt = g0 + i
                        k0 = -64 + 128 * kt
                        lo = 64 if kt == 0 else 0
                        hi = 64 if kt == nblk else 128
                        r0 = c + dil * (k0 + lo)
                        self.dma(vb[lo:hi, i, :], self.rawap(self.s_v, r0 * 390, [[dil * 390, hi - lo], [1, 390]]), rd=[sv], wr=[vb])
                    for bl in range(nb):
                        b = g0 + bl
                        op = ops_[bi_ % 2]
                        bi_ += 1
                        q0 = c + dil * 128 * b
                        ka0 = KPAD + c + dil * (128 * b - 64)
                        kb0 = ka0 + 128 * dil
                        pend = []
                        for step in range(8):
                            if step < 6:
                                h = step
                                hp, base = h // 2, (h % 2) * 64
                                lgt = lg[li % 4]
                                sb = sbl[li % 3]
                                p = pT[li % 3]
                                li += 1
                                qm = qres[h % 2]
                                LV = int(os.environ.get("ATT_LEVEL", "9"))
                                qap = qm[:, hp, q0:q0 + 127 * dil + 1:dil]
                                if LV >= 1:
                                    self.mm(lgt, lgt[:, 0:128], kres[:, hp, ka0:ka0 + 127 * dil + 1:dil], qap, [kres, qm])
                                    self.mm(lgt, lgt[:, 128:256], kres[:, hp, kb0:kb0 + 127 * dil + 1:dil], qap, [kres, qm])
                                if LV >= 2:
                                    self.tt("dve", sb, sb[:], lgt[:], self.bias[:, br * 6 + h, :], ALU.add, [lgt, self.bias])
                                first, last = (b == 0), (b == nblk - 1)
                                if LV >= 3:
                                    if not first and not last:
                                        self.act(p, p[:], sb[:], AF.Exp, [sb, self.misc], bias=self.zero_c, scale=0.125)
                                    else:
                                        self.act(p, p[:, 0:128], sb[:, 0:128], AF.Exp, [sb, self.misc],
                                                 bias=self.edge_first if first else self.zero_c, scale=0.125)
                                        self.act(p, p[:, 128:256], sb[:, 128:256], AF.Exp, [sb, self.misc],
                                                 bias=self.edge_last if last else self.zero_c, scale=0.125)
                                pend.append((h, p))
                            if step >= 2 and LV >= 4:
                                h, p = pend[step - 2]
                                self.mm(op, op[:, h * 65:(h + 1) * 65], p[:, 0:128], vb[:, bl, h * 65:(h + 1) * 65], [p, vb], start=True, stop=False)
                                self.mm(op, op[:, h * 65:(h + 1) * 65], p[:, 128:256], vb[:, bl + 1, h * 65:(h + 1) * 65], [p, vb], start=False, stop=True)
                        if int(os.environ.get("ATT_LEVEL", "9")) < 5:
                            continue
                        o = ost[bi_ % 2]
                        self.evac(o, o[:], op[:, 0:390], [op])
                        self.dma(self.rawap(self.s_o, (br * self.TM + q0) * 390, [[dil * 390, 128], [1, 390]]), o[:], rd=[o], wr=[so])

    def phase_merge(self, T):
        A, pb = self.A, self.pb
        om = [A.alloc([4, 3, 390], F32) for _ in range(2)]
        sm = A.alloc([6, 65], F32)
        rd_ = A.alloc([6], F32)
        ya = A.alloc([6, 64], F32)
        yb = A.alloc([384], BF16)
        junk = A.alloc([384], BF16)
        ss = A.alloc([2], F32)
        mst = [A.alloc([3, 512], BF16) for _ in range(2)]
        so = Tile(self.s_o)
        smix = Tile(self.s_mix)
        for tt in range(T // 512):
            t0 = tt * 512
            o = om[tt % 2]
            ms = mst[tt % 2]
            for br in range(3):
                self.dma(o[:, :, br, :], self.s_o[br, t0:t0 + 512, :].rearrange("(j p) f -> p j f", p=128), rd=[so], wr=[o])
            ps = pb[tt % 2]
            pv = ps.ap.bitcast(BF16)
            ps2 = pb[2 + tt % 2]
            pv2 = ps2.ap.bitcast(BF16)
            for j in range(4):
                s3 = sm[:].rearrange("p a b -> p (a b)")
                self.tt("dve", sm, s3, o[:, j, 0, :], o[:, j, 1, :], ALU.add, [o])
                self.tt("dve", sm, s3, s3, o[:, j, 2, :], ALU.add, [o, sm])
                self.recip(rd_, rd_[:], sm[:, :, 64], [sm])
                self.tt("dve", ya, ya[:], sm[:, :, 0:64], rd_[:].unsqueeze(2).to_broadcast([128, 6, 64]), ALU.mult, [sm, rd_])
                yaf = ya[:].rearrange("p a b -> p (a b)")
                self.act(junk, junk[:], yaf, AF.Square, [ya], accum=ss[:, 0:1], wr=[ss])
                self.act(ss, ss[:, 1:2], ss[:, 0:1], AF.Sqrt, [ss, self.misc], bias=self.eps_rms, scale=1.0 / 384)
                self.recip(ss, ss[:, 1:2], ss[:, 1:2], [ss])
                self.act(yb, yb[:], yaf, AF.Copy, [ya, ss], scale=ss[:, 1:2])
                for i in range(3):
                    if i < 2:
                        self.tr(ps, pv[:, i * 512 + j * 128:i * 512 + (j + 1) * 128], yb[:, i * 128:(i + 1) * 128], [yb])
                    else:
                        self.tr(ps2, pv2[:, j * 128:(j + 1) * 128], yb[:, i * 128:(i + 1) * 128], [yb])
            self.evac(ms, ms[:, 0:2, :], pv[:, 0:1024].rearrange("p (a b) -> p a b", a=2), [ps])
            self.evac(ms, ms[:, 2, :], pv2[:, 0:512], [ps2])
            self.dma(self.s_mix[0:384, t0:t0 + 512].rearrange("(a p) t -> p a t", p=128), ms[:], rd=[ms], wr=[smix])

    def phase_conv(self, T):
        A, pb, PV = self.A, self.pb, self.PV
        ub = [A.alloc([2, 514], F32) for _ in range(2)]
        bb = [A.alloc([2, 512], F32) for _ in range(2)]
        c1 = A.alloc([512], F32)
        c2 = A.alloc([512], F32)
        yb = A.alloc([2, 512], F32)
        sq = A.alloc([2, 512], BF16)
        rs = A.alloc([512], F32)
        ybn = [A.alloc([2, 512], BF16) for _ in range(2)]
        su, sb_, smix = Tile(self.s_u), Tile(self.s_b), Tile(self.s_mix)
        pv = self.pv
        for tt in range(T // 512):
            t0 = tt * 512
            u, bt, yo = ub[tt % 2], bb[tt % 2], ybn[tt % 2]
            lo = 1 if tt == 0 else 0
            hi = 513 if tt == T // 512 - 1 else 514
            if lo == 1:
                self.memset("pool", u, u[:, :, 0:1], 0.0)
            if hi == 513:
                self.memset("pool", u, u[:, :, 513:514], 0.0)
            self.dma(u[:, :, lo:hi], self.s_u[:, t0 - 1 + lo:t0 - 1 + hi].rearrange("(a p) t -> p a t", p=128), rd=[su], wr=[u])
            self.dma(bt[:], self.s_b[:, t0:t0 + 512].rearrange("(a p) t -> p a t", p=128), rd=[sb_], wr=[bt])
            ps = pb[tt % 2]
            for ci in range(2):
                cw = PV["cw"] + 3 * ci
                self.ts("pool", c1, c1[:], u[:, ci, 1:513], pv[:, cw + 1:cw + 2], ALU.mult, [u, pv])
                self.stt(c2, c2[:], u[:, ci, 0:512], pv[:, cw:cw + 1], c1[:], ALU.mult, ALU.add, [u, pv, c1])
                self.stt(c1, c1[:], u[:, ci, 2:514], pv[:, cw + 2:cw + 3], c2[:], ALU.mult, ALU.add, [u, pv, c2])
                self.tt("pool", yb, yb[:, ci, :], c1[:], bt[:, ci, :], ALU.mult, [c1, bt])
                self.act(sq, sq[:, ci, :], yb[:, ci, :], AF.Square, [yb])
                self.mm(ps, ps[:], self.onesb[:], sq[:, ci, :], [self.onesb, sq], start=(ci == 0), stop=(ci == 1))
            self.act(rs, rs[:], ps[:], AF.Sqrt, [ps, self.misc], bias=self.eps_rms, scale=1.0 / 256)
            self.recip(rs, rs[:], rs[:], [rs])
            for ci in range(2):
                self.tt("dve" if ci == 0 else "pool", yo, yo[:, ci, :], yb[:, ci, :], rs[:], ALU.mult, [yb, rs])
            self.dma(self.s_mix[384:640, t0:t0 + 512].rearrange("(a p) t -> p a t", p=128), yo[:], rd=[yo], wr=[smix])

    def phase_r1(self, T):
        A, pb, PV, pv = self.A, self.pb, self.PV, self.pv
        nt = T // 512
        F = lambda: A.alloc([512], F32)
        zc = A.alloc([10, 514], F32)
        xs = A.alloc([10, 512], F32)
        t1 = [F() for _ in range(2)]
        t2 = [F() for _ in range(2)]
        kk2, rn, tmp, tk = t1[0], t1[1], t2[0], t2[1]
        tw = F()
        kk, kkn = F(), F()
        sg, aa, clw, E1, E2, enl, ta, bd = (F() for _ in range(8))
        lw = sg
        kd = [F(), F()]
        gst = F()
        bst = E1
        yst = [E2, enl]
        tot = A.alloc([8], F32)
        gC = [A.alloc([8], F32) for _ in range(2)]
        ARx = [A.alloc([2, 2, 512], BF16) for _ in range(2)]
        BT = [A.alloc([512], BF16) for _ in range(2)]
        KT = [A.alloc([512], BF16) for _ in range(2)]
        BG = A.alloc([512], BF16)
        KG = A.alloc([512], BF16)
        VT = A.alloc([512], BF16)
        tmA = [A.alloc([4, 128], BF16) for _ in range(2)]
        gx = [A.alloc([2, 4, 2, 128], BF16) for _ in range(2)]
        vtm = A.alloc([4, 128], BF16)
        atz = [A.alloc([128], BF16) for _ in range(2)]
        atk = [[[A.alloc([2, 128], BF16) for _ in range(2)] for _ in range(2)] for _ in range(4)]
        AW = [[[A.alloc([128], BF16) for _ in range(2)] for _ in range(2)] for _ in range(4)]
        Zs = [[A.alloc([128], F32) for _ in range(6)] for _ in range(2)]
        Zp = [[A.alloc([128], F32) for _ in range(5)] for _ in range(2)]
        Rk = [[A.alloc([128], F32) for _ in range(2)] for _ in range(2)]
        RHf = A.alloc([512], BF16)
        RHb = A.alloc([3, 512], BF16)
        GTf = A.alloc([8, 128], BF16)
        Hf = A.alloc([8, 128], F32)
        GTb = A.alloc([3, 8, 128], BF16)
        Hb = A.alloc([3, 8, 128], F32)
        Pst = [A.alloc([9, 128], BF16) for _ in range(3)]
        for t in ARx + gx + [GTf, GTb, Hb] + Pst:
            self.memset("pool", t, t[:], 0.0)
        self.memset("pool", Hf, Hf[:], 0.0)
        ps_at = [pb[0], pb[1]]
        ps_x = [Sub(pb[2 + i // 4], self.psum[:, 1024 + i * 128:1024 + (i + 1) * 128]) for i in range(8)]
        ps_r = [pb[4], pb[5]]
        ps_g1 = [Sub(pb[6], self.psum[hh * 64:(hh + 1) * 64, 3072:3200]) for hh in range(2)]
        ps_g2 = [[Sub(pb[6], self.psum[hh * 64:(hh + 1) * 64, 3200 + c * 64:3264 + c * 64]) for c in range(2)] for hh in range(2)]
        ps_g3 = [Sub(pb[6], self.psum[hh * 64:(hh + 1) * 64, 3328:3456]) for hh in range(2)]
        ps_c = Sub(pb[6], self.psum[:, 3456:3584])
        ps_y = pb[7]
        ps_m = ps_at
        szc, sg_, sbon, syp, srh, sgt, shb = (Tile(self.s_zc), Tile(self.s_g), Tile(self.s_bon), Tile(self.s_yp),
                                              Tile(self.s_rhb), Tile(self.s_gtb), Tile(self.s_hb))
        lr = self.lr
        c3 = lambda t: t[:].rearrange("p (a b) -> p a b", a=8)
        xi = 0
        for tt in range(nt):
            t0 = tt * 512
            lo = 1 if tt == 0 else 0
            hi = 513 if tt == nt - 1 else 514
            if lo == 1:
                self.memset("pool", zc, zc[:, :, 0:1], 0.0)
            if hi == 513:
                self.memset("pool", zc, zc[:, :, 513:514], 0.0)
            self.dma(zc[:, :, lo:hi], self.s_zc[:, t0 - 1 + lo:t0 - 1 + hi].rearrange("(a p) t -> p a t", p=128), rd=[szc], wr=[zc])
            for cc in range(10):
                a1, a2 = t1[cc % 2], t2[cc % 2]
                self.tt("pool", a1, a1[:], zc[:, cc, 0:512], zc[:, cc, 2:514], ALU.add, [zc])
                self.act(a2, a2[:], a1[:], AF.Copy, [a1, pv], scale=pv[:, PV["m2"] + cc:PV["m2"] + cc + 1])
                self.stt(xs, xs[:, cc, :], zc[:, cc, 1:513], pv[:, PV["m1"] + cc:PV["m1"] + cc + 1], a2[:], ALU.mult, ALU.add, [zc, pv, a2])
            self.act(tw, tw[0:32, :], xs[0:32, 9, :], AF.Tanh, [xs])
            self.cp("dve", tw, tw[32:64, :], xs[32:64, 9, :], [xs])
            self.act(tw, tw[64:128, :], xs[64:128, 9, :], AF.Sigmoid, [xs])
            RL = int(os.environ.get("R1_LEVEL", "9"))
            for hp in range(3):
                if RL < 1:
                    continue
                r_, k_, v_ = xs[:, hp, :], xs[:, 3 + hp, :], xs[:, 6 + hp, :]
                hc = slice(hp * 128, (hp + 1) * 128)
                pg = ps_m[0]
                self.mm(pg, pg[:], lr[:, 4, hc], tw[:], [lr, tw])
                self.evac(gst, gst[:], pg[:], [pg])
                self.dma(self.s_g[hc, t0:t0 + 512], gst[:], rd=[gst], wr=[sg_])
                self.act(kk, kk[:], k_, AF.Copy, [xs, pv], scale=pv[:, PV["kk"] + hp:PV["kk"] + hp + 1])
                self.tt("pool", kk2, kk2[:], kk[:], kk[:], ALU.mult, [kk])
                pn = ps_m[1]
                self.mm(pn, pn[:], self.blk[:], kk2[:], [self.blk, kk2])
                self.act(rn, rn[:], pn[:], AF.Sqrt, [pn])
                self.ts("dve", rn, rn[:], rn[:], 1e-12, ALU.max, [rn])
                self.recip(rn, rn[:], rn[:], [rn])
                self.tt("pool", kkn, kkn[:], kk[:], rn[:], ALU.mult, [kk, rn])
                self.cp("act", VT, VT[:], v_, [xs])
                pt = ps_m[0]
                ptv = pt.ap.bitcast(BF16)
                for np_ in range(4):
                    self.tr(pt, ptv[:, np_ * 128:(np_ + 1) * 128], VT[:, np_ * 128:(np_ + 1) * 128], [VT])
                self.evac(vtm, vtm[:].rearrange("p a b -> p (a b)"), ptv[:, 0:512], [pt])
                for d in range(2):
                    if RL < 2:
                        continue
                    arx, bt_, kt_ = ARx[d], BT[d], KT[d]
                    pw = ps_m[0]
                    self.mm(pw, pw[:], lr[:, d, hc], tw[:], [lr, tw])
                    self.act(sg, sg[:], pw[:], AF.Sigmoid, [pw, pv], bias=pv[:, PV["w0"] + 3 * d + hp:PV["w0"] + 3 * d + hp + 1])
                    pa = ps_m[1]
                    self.mm(pa, pa[:], lr[:, 2 + d, hc], tw[:], [lr, tw])
                    self.act(aa, aa[:], pa[:], AF.Sigmoid, [pa, pv], bias=pv[:, PV["a0"] + 3 * d + hp:PV["a0"] + 3 * d + hp + 1])
                    self.ts("pool", lw, lw[:], sg[:], WSCALE, ALU.mult, [sg])
                    self.scan(clw, clw[:], self.seg[:], lw[:], [self.seg, lw])
                    self.cp("dve", tot, tot[:], c3(clw)[:, :, 63], [clw])
                    if d == 1:
                        self.tt("dve", tmp, tmp[:], lw[:], clw[:], ALU.subtract, [lw, clw])
                        self.tt("dve", clw, c3(clw), c3(tmp), tot[:].unsqueeze(2).to_broadcast([128, 8, 64]), ALU.add, [tmp, tot])
                    self.act(E1, E1[:], clw[:], AF.Exp, [clw])
                    self.act(E2, E2[:], clw[:], AF.Exp, [clw], scale=-1.0)
                    self.act(enl, enl[:], lw[:], AF.Exp, [lw], scale=-1.0)
                    self.act(gC[d], gC[d][:], tot[:], AF.Exp, [tot])
                    self.tt("pool", ta, ta[:], kkn[:], enl[:], ALU.mult, [kkn, enl])
                    for hh in range(2):
                        bs = slice(hh * 64, hh * 64 + 64)
                        self.stt(arx, arx[bs, hh, 0, :], ta[bs, :], -1.0, E1[bs, :], ALU.mult, ALU.mult, [ta, E1])
                        self.tt("pool", arx, arx[bs, hh, 1, :], xs[bs, hp, :], E1[bs, :], ALU.mult, [xs, E1])
                    self.ts("dve", tk, tk[:], aa[:], pv[:, PV["ka"] + hp:PV["ka"] + hp + 1], ALU.mult, [aa, pv],
                            pv[:, PV["oka"] + hp:PV["oka"] + hp + 1], ALU.add)
                    self.tt("pool", kd[d], kd[d][:], k_, tk[:], ALU.mult, [xs, tk])
                    self.tt("pool", bd, bd[:], kkn[:], aa[:], ALU.mult, [kkn, aa])
                    self.tt("dve", bt_, bt_[:], bd[:], E2[:], ALU.mult, [bd, E2])
                    self.tt("dve", kt_, kt_[:], kd[d][:], E2[:], ALU.mult, [kd[d], E2])
                    gcb = gC[d][:].unsqueeze(2).to_broadcast([128, 8, 64])
                    self.tt("dve", BG, c3(BG), c3(bt_), gcb, ALU.mult, [bt_, gC[d]])
                    self.tt("dve", KG, c3(KG), c3(kt_), gcb, ALU.mult, [kt_, gC[d]])
                    if RL < 3:
                        continue
                    p1, p2 = ps_m[0], ps_m[1]
                    p1v, p2v = p1.ap.bitcast(BF16), p2.ap.bitcast(BF16)
                    for np_ in range(4):
                        tsl = slice(np_ * 128, (np_ + 1) * 128)
                        for hh in range(2):
                            pass
                    for np_ in range(4):
                        tsl = slice(np_ * 128, (np_ + 1) * 128)
                        self.tr(p1, p1v[0:128, np_ * 128:(np_ + 1) * 128], arx[0:128, 0, 0, tsl], [arx])
                        self.tr(p2, p2v[0:128, np_ * 128:(np_ + 1) * 128], arx[0:128, 1, 0, tsl], [arx])
                    tmd = tmA[d]
                    self.evac(tmd, tmd[:, :, 0:64], p1v[:, 0:512].rearrange("p (a b) -> p a b", a=4)[:, :, 0:64], [p1])
                    self.evac(tmd, tmd[:, :, 64:128], p2v[:, 0:512].rearrange("p (a b) -> p a b", a=4)[:, :, 64:128], [p2])
                    gxd = gx[d]
                    for q, (srct, pq, pqv) in enumerate(((BG, p1, p1v), (KG, p2, p2v))):
                        for np_ in range(4):
                            self.tr(pq, pqv[:, 512 + np_ * 128:512 + (np_ + 1) * 128], srct[:, np_ * 128:(np_ + 1) * 128], [srct])
                        v4 = pqv[:, 512:1024].rearrange("p (a b) -> p a b", a=4)
                        self.cp("act", gxd, gxd[0:64, q, :, 0, :], v4[0:64], [pq])
                        self.cp("dve", gxd, gxd[64:128, q, :, 1, :], v4[64:128], [pq])
                    for np_ in range(4):
                        if RL < 4:
                            continue
                        tsl = slice(np_ * 128, (np_ + 1) * 128)
                        for hh in range(2):
                            bs = slice(hh * 64, hh * 64 + 64)
                            slot = xi % 2
                            xi += 1
                            az, ak, aw = atz[slot], atk[np_][hh][d], AW[np_][hh][d]
                            pat = ps_at[slot]
                            px0 = ps_x[slot * 4]
                            self.mm(pat, pat[:, 0:256], bt_[:, tsl], arx[:, hh, :, tsl], [bt_, arx])
                            self.mm(pat, pat[:, 256:512], kt_[:, tsl], arx[:, hh, :, tsl], [kt_, arx])
                            self.mm(px0, px0[:], arx[:, hh, 0, tsl], bt_[:, tsl], [arx, bt_])
                            p4 = pat[:].rearrange("p (a b) -> p a b", a=4)
                            m4 = self.maskat[d]
                            Z, ZP = Zs[slot], Zp[slot]
                            self.tt("dve", Z[0], Z[0][:], p4[:, 0, :], m4[:, 0, :], ALU.mult, [pat, m4])
                            self.tt("dve", az, az[:], p4[:, 2, :], m4[:, 2, :], ALU.mult, [pat, m4])
                            self.tt("dve", ak, ak[:], p4[:, 1:4:2, :], m4[:, 1:4:2, :], ALU.mult, [pat, m4])
                            self.tt("dve", ZP[0], ZP[0][:], px0[:], self.maskx0[d][:], ALU.mult, [px0, self.maskx0[d]])
                            UL = int(os.environ.get("UNIT_LEVEL", "9"))
                            if UL < 2:
                                continue
                            pr = ps_r[slot]
                            R0 = Rk[slot][0]
                            self.mm(pr, pr[:, 0:64], az[:], vtm[:, np_, bs], [az, vtm])
                            self.cp("pool", R0, R0[:, 0:64], tmd[:, np_, bs], [tmd])
                            self.cp("act", R0, R0[:, 64:128], pr[:, 0:64], [pr])
                            for kq in range(6):
                                zk_t = Z[kq]
                                zk = Z[kq][:]
                                rk = Rk[slot][kq % 2]
                                rnx = Rk[slot][(kq + 1) % 2] if kq < 5 else aw
                                self.mm(pr, pr[:, 0:128], zk, rk[:], [zk_t, rk])
                                self.tt("dve", rnx, rnx[:], pr[:, 0:128], rk[:], ALU.add, [pr, rk])
                                if kq < 5:
                                    pz = ps_x[slot * 4 + 1 + (kq % 2)]
                                    self.mm(pz, pz[:], ZP[kq][:], zk, [ZP[kq], zk_t])
                                    self.cp("act", Z[kq + 1], Z[kq + 1][:], pz[:], [pz])
                                    if kq < 4:
                                        pz2 = ps_x[slot * 4 + 3]
                                        self.mm(pz2, pz2[:], zk, ZP[kq][:], [ZP[kq], zk_t])
                                        self.cp("act", ZP[kq + 1], ZP[kq + 1][:], pz2[:], [pz2])
                            if UL < 3:
                                continue
                            pg1, pg3 = ps_g1[hh], ps_g3[hh]
                            self.mm(pg1, pg1[:], aw[:, 0:64], gxd[:, 0, np_, :, bs], [aw, gxd])
                            self.mm(pg3, pg3[:], aw[:, 0:64], ak[:, 0, :], [aw, ak])
                            for c in range(2):
                                pg2 = ps_g2[hh][c]
                                self.mm(pg2, pg2[:], gxd[:, 0, np_, c, bs], aw[:, 64:128], [aw, gxd], start=True, stop=False)
                                self.mm(pg2, pg2[:], gxd[:, 1, np_, c, bs], vtm[:, np_, bs], [gxd, vtm], start=False, stop=True)
                            if UL < 4:
                                continue
                            tokp = slice(np_ * 128, (np_ + 1) * 128)
                            for c in range(2):
                                ch = np_ * 2 + c
                                if d == 0:
                                    gdst_t, gdst = GTf, GTf[bs, ch, bs]
                                    hdst_t, hdst = Hf, Hf[bs, ch, bs]
                                else:
                                    gdst_t, gdst = GTb, GTb[bs, hp, ch, bs]
                                    hdst_t, hdst = Hb, Hb[bs, hp, ch, bs]
                                self.stt(gdst_t, gdst, self.identf[bs, bs], gC[d][bs, ch:ch + 1], pg1[:, c * 64:(c + 1) * 64], ALU.mult, ALU.add,
                                         [self.identf, gC[d], pg1])
                                self.cp("act", hdst_t, hdst, ps_g2[hh][c][:], [ps_g2[hh][c]])
                            if d == 0:
                                rdst_t, rdst = RHf, RHf[bs, tokp]
                            else:
                                rdst_t, rdst = RHb, RHb[bs, hp, tokp]
                            self.tt("dve", rdst_t, rdst, pg3[:], arx[bs, hh, 1, tokp], ALU.add, [pg3, arx])
                if RL < 5:
                    continue
                self.tt("pool", tmp, tmp[:], kd[0][:], kd[1][:], ALU.add, [kd[0], kd[1]])
                self.stt(ta, ta[:], r_, pv[:, PV["rk"] + hp:PV["rk"] + hp + 1], tmp[:], ALU.mult, ALU.mult, [xs, pv, tmp])
                pbn = ps_m[0]
                self.mm(pbn, pbn[:], self.blk[:], ta[:], [self.blk, ta])
                self.tt("dve", bst, bst[:], pbn[:], v_, ALU.mult, [pbn, xs])
                self.dma(self.s_bon[hc, t0:t0 + 512], bst[:], rd=[bst], wr=[sbon])
                Pc = Pst[hp]
                if tt > 0:
                    self.cp("pool", Pc, Pc[:, 0, :], Pc[:, 8, :], [Pc])
                for ch in range(8):
                    self.mm(ps_c, ps_c[:], GTf[:, ch, :], Pc[:, ch, :], [GTf, Pc])
                    self.tt("dve", Pc, Pc[:, ch + 1, :], ps_c[:], Hf[:, ch, :], ALU.add, [ps_c, Hf])
                for np_ in range(4):
                    tokp = slice(np_ * 128, (np_ + 1) * 128)
                    for c in range(2):
                        ch = np_ * 2 + c
                        tokc = slice(ch * 64, (ch + 1) * 64)
                        self.mm(ps_y, ps_y[:, tokc], Pc[:, ch, :], RHf[:, tokc], [Pc, RHf], start=(c == 0), stop=False)
                    for hh in range(2):
                        bs = slice(hh * 64, hh * 64 + 64)
                        for d in range(2):
                            ak, aw = atk[np_][hh][d], AW[np_][hh][d]
                            self.mm(ps_y, ps_y[bs, tokp], aw[:, 64:128], ak[:, 0, :], [aw, ak], start=False, stop=False)
                            self.mm(ps_y, ps_y[bs, tokp], vtm[:, np_, bs], ak[:, 1, :], [vtm, ak], start=False, stop=(d == 1))
                ys = yst[hp % 2]
                self.evac(ys, ys[:], ps_y[:], [ps_y])
                self.dma(self.s_yp[hc, t0:t0 + 512], ys[:], rd=[ys], wr=[syp])
            self.dma(self.s_gtb[tt], GTb[:].rearrange("p a b c -> p (a b c)"), rd=[GTb], wr=[sgt])
            self.dma(self.s_hb[tt], Hb[:].rearrange("p a b c -> p (a b c)"), rd=[Hb], wr=[shb])
            self.dma(self.s_rhb[:, t0:t0 + 512].rearrange("(a p) t -> p a t", p=128), RHb[:], rd=[RHb], wr=[srh])

    def phase_r2(self, T):
        A, pb, PV, pv = self.A, self.pb, self.PV, self.pv
        nt = T // 512
        gtb = [A.alloc([3, 8, 128], BF16) for _ in range(2)]
        hb = [A.alloc([3, 8, 128], F32) for _ in range(2)]
        rhb = [A.alloc([3, 512], BF16) for _ in range(2)]
        yp = [A.alloc([3, 512], F32) for _ in range(2)]
        bon = [A.alloc([3, 512], F32) for _ in range(2)]
        gg = [A.alloc([3, 512], F32) for _ in range(2)]
        Pst = [A.alloc([9, 128], BF16) for _ in range(3)]
        y = A.alloc([512], F32)
        yc = A.alloc([512], F32)
        sq = A.alloc([512], F32)
        sd = A.alloc([512], F32)
        yo = [A.alloc([3, 512], BF16) for _ in range(2)]
        ps_c = Sub(pb[6], self.psum[:, 3072:3200])
        sg_, sbon, syp, srh, sgt, shb, smix = (Tile(self.s_g), Tile(self.s_bon), Tile(self.s_yp),
                                               Tile(self.s_rhb), Tile(self.s_gtb), Tile(self.s_hb), Tile(self.s_mix))
        for hp in range(3):
            self.memset("pool", Pst[hp], Pst[hp][:, 8, :], 0.0)
        for it, tt in enumerate(range(nt - 1, -1, -1)):
            t0 = tt * 512
            par = it % 2
            G, H, RH, YP, BO, GG, YO = gtb[par], hb[par], rhb[par], yp[par], bon[par], gg[par], yo[par]
            self.dma(G[:].rearrange("p a b c -> p (a b c)"), self.s_gtb[tt], rd=[sgt], wr=[G])
            self.dma(H[:].rearrange("p a b c -> p (a b c)"), self.s_hb[tt], rd=[shb], wr=[H])
            fm = lambda s_: s_[0:384, t0:t0 + 512].rearrange("(a p) t -> p a t", p=128)
            self.dma(RH[:], fm(self.s_rhb), rd=[srh], wr=[RH])
            self.dma(YP[:], fm(self.s_yp), rd=[syp], wr=[YP])
            self.dma(BO[:], fm(self.s_bon), rd=[sbon], wr=[BO])
            self.dma(GG[:], fm(self.s_g), rd=[sg_], wr=[GG])
            for hp in range(3):
                Pc = Pst[hp]
                if it > 0:
                    self.cp("pool", Pc, Pc[:, 8, :], Pc[:, 0, :], [Pc])
                for ch in range(7, -1, -1):
                    self.mm(ps_c, ps_c[:], G[:, hp, ch, :], Pc[:, ch + 1, :], [G, Pc])
                    self.tt("dve", Pc, Pc[:, ch, :], ps_c[:], H[:, hp, ch, :], ALU.add, [ps_c, H])
                py = pb[hp % 2]
                for ch in range(8):
                    tokc = slice(ch * 64, (ch + 1) * 64)
                    self.mm(py, py[:, tokc], Pc[:, ch + 1, :], RH[:, hp, tokc], [Pc, RH])
                self.tt("dve", y, y[:], py[:], YP[:, hp, :], ALU.add, [py, YP])
                pm = pb[2 + hp % 2]
                self.mm(pm, pm[:], self.blk64[:], y[:], [self.blk64, y])
                self.tt("dve", yc, yc[:], y[:], pm[:], ALU.subtract, [y, pm])
                self.act(sq, sq[:], yc[:], AF.Square, [yc])
                pvr = pb[4 + hp % 2]
                self.mm(pvr, pvr[:], self.blk64[:], sq[:], [self.blk64, sq])
                self.act(sd, sd[:], pvr[:], AF.Sqrt, [pvr, self.misc], bias=self.eps_ln)
                self.recip(sd, sd[:], sd[:], [sd])
                self.tt("pool", yc, yc[:], yc[:], sd[:], ALU.mult, [yc, sd])
                self.ts("dve", yc, yc[:], yc[:], pv[:, PV["lw"] + hp:PV["lw"] + hp + 1], ALU.mult, [yc, pv],
                        pv[:, PV["lb"] + hp:PV["lb"] + hp + 1], ALU.add)
                self.tt("pool", yc, yc[:], yc[:], BO[:, hp, :], ALU.add, [yc, BO])
                self.tt("dve", YO, YO[:, hp, :], yc[:], GG[:, hp, :], ALU.mult, [yc, GG])
            self.dma(self.s_mix[640:1024, t0:t0 + 512].rearrange("(a p) t -> p a t", p=128), YO[:], rd=[YO], wr=[smix])

    def phase_c(self, T, l, src, dst):
        A, pb = self.A, self.pb
        nt = T // 512
        mt = [A.alloc([8, 512], BF16) for _ in range(2)]
        xt = [A.alloc([4, 1024], F32) for _ in range(2)]
        mo = A.alloc([4, 1024], F32)
        ht = A.alloc([4, 1024], BF16)
        hT = A.alloc([8, 512], BF16)
        f1 = A.alloc([32, 512], BF16)
        rl = [A.alloc([512], BF16) for _ in range(2)]
        wq = [A.alloc([8, 512], BF16) for _ in range(3)]
        junk = A.alloc([1024], BF16)
        ss = A.alloc([4], F32)
        rs = A.alloc([4], F32)
        tmp = A.alloc([1024], F32)
        smix = Tile(self.s_mix)
        dT = Tile(dst)
        wi = 0
        gp = A.alloc([2, 1024], F32)
        self.dma(gp[:, 0, :], self.rawap(self.w["norm_mix_post"], l * D, [[0, 128], [1, D]]), wr=[gp])
        self.dma(gp[:, 1, :], self.rawap(self.w["norm_ffn_post"], l * D, [[0, 128], [1, D]]), wr=[gp])

        def post_norm_add(x, which):
            for j in range(4):
                self.act(junk, junk[:], mo[:, j, :], AF.Square, [mo], accum=ss[:, j:j + 1], wr=[ss])
            self.act(rs, rs[:], ss[:], AF.Sqrt, [ss, self.misc], bias=self.eps_rms, scale=1.0 / D)
            self.recip(rs, rs[:], rs[:], [rs])
            for j in range(4):
                self.stt(tmp, tmp[:], mo[:, j, :], rs[:, j:j + 1], gp[:, which, :], ALU.mult, ALU.mult, [mo, rs, gp])
                self.tt("pool", x, x[:, j, :], x[:, j, :], tmp[:], ALU.add, [x, tmp])

        for tt in range(nt):
            t0 = tt * 512
            m, x = mt[tt % 2], xt[tt % 2]
            self.dma(m[:], self.s_mix[:, t0:t0 + 512].rearrange("(a p) t -> p a t", p=128), rd=[smix], wr=[m])
            self.dma(x[:], src[t0:t0 + 512, :].rearrange("(j p) d -> p j d", p=128), wr=[x])
            for dh in range(2):
                wt = wq[wi % 3]
                wi += 1
                self.dma(wt[:], self.woutb[l][:, dh * 512:(dh + 1) * 512].rearrange("(kc p) c -> p kc c", p=128), wr=[wt])
                for j in range(4):
                    ps = pb[j]
                    for kc in range(8):
                        self.mm(ps, ps[:], m[:, kc, j * 128:(j + 1) * 128], wt[:, kc, :], [m, wt], start=(kc == 0), stop=(kc == 7))
                    self.evac(mo, mo[:, j, dh * 512:(dh + 1) * 512], ps[:], [ps])
            post_norm_add(x, 0)
            self.norm_transpose(x, ht, hT, ss, rs, junk, [pb[0], pb[1]])
            for fp in range(8):
                wt = wq[wi % 3]
                wi += 1
                self.dma(wt[:], self.w1b[l][:, fp * 512:(fp + 1) * 512].rearrange("(kc p) c -> p kc c", p=128), wr=[wt])
                for fc in range(4):
                    ps = pb[fc]
                    for kc in range(8):
                        self.mm(ps, ps[:], wt[:, kc, fc * 128:(fc + 1) * 128], hT[:, kc, :], [wt, hT], start=(kc == 0), stop=(kc == 7))
                    r = rl[fc % 2]
                    self.act(r, r[:], ps[:], AF.Relu, [ps])
                    self.tt("pool", f1, f1[:, fp * 4 + fc, :], r[:], r[:], ALU.mult, [r])
            for dh in range(2):
                for kp in range(4):
                    wt = wq[wi % 3]
                    wi += 1
                    self.dma(wt[:], self.w2b[l][kp * 1024:(kp + 1) * 1024, dh * 512:(dh + 1) * 512].rearrange("(kc p) c -> p kc c", p=128), wr=[wt])
                    for j in range(4):
                        ps = pb[4 + j]
                        for kc in range(8):
                            kg = kp * 8 + kc
                            self.mm(ps, ps[:], f1[:, kg, j * 128:(j + 1) * 128], wt[:, kc, :], [f1, wt], start=(kg == 0), stop=(kg == 31))
                for j in range(4):
                    self.evac(mo, mo[:, j, dh * 512:(dh + 1) * 512], pb[4 + j][:], [pb[4 + j]])
            post_norm_add(x, 1)
            self.dma(dst[t0:t0 + 512, :].rearrange("(j p) d -> p j d", p=128), x[:], rd=[x], wr=[dT])


_WNAMES = [n for n, _ in WEIGHT_SPECS]


def kernel(**inputs):
    xp = np.ascontiguousarray(np.asarray(inputs["x_prompt"], dtype=np.float32))
    xs = np.ascontiguousarray(np.asarray(inputs["x_sample"], dtype=np.float32))
    ncores = 8
    npp, nsp = xp.shape[0] // ncores, xs.shape[0] // ncores
    seq_lens = [xp.shape[1]] * npp + [xs.shape[1]] * nsp
    b = Builder(seq_lens)
    nc = b.build()
    consts = _consts()
    shared = {n: np.ascontiguousarray(np.asarray(inputs[n], dtype=np.float32)) for n in _WNAMES}
    shared.update(consts)
    in_maps = []
    for c in range(ncores):
        m = dict(shared)
        for i in range(npp):
            m[f"x{i}"] = xp[c * npp + i]
        for i in range(nsp):
            m[f"x{npp + i}"] = xs[c * nsp + i]
        in_maps.append(m)
    res = run_bass_kernel_spmd(nc, in_maps, core_ids=list(range(ncores)))
    yp = np.empty_like(xp)
    ys = np.empty_like(xs)
    for c in range(ncores):
        r = res.results[c]
        for i in range(npp):
            yp[c * npp + i] = r[f"y{i}"]
        for i in range(nsp):
            ys[c * nsp + i] = r[f"y{npp + i}"]
    return (yp, ys)
```

```python
import math
import os
from contextlib import ExitStack

import numpy as np
import ml_dtypes

import concourse.bass as bass
import concourse.mybir as mybir
from concourse.bass_utils import run_bass_kernel_spmd

F32 = mybir.dt.float32
BF16 = mybir.dt.bfloat16
AF = mybir.ActivationFunctionType
ALU = mybir.AluOpType
NPBF = ml_dtypes.bfloat16

D = 1024
INW = 3200
DFF = 4096
NL = 2
RMS_EPS = 1e-6
LNX_EPS = 64e-5
DILS = (1, 4, 16)
KPAD = 1024
NEG = -30000.0
WSCALE = -math.exp(-0.5)


class Tile:
    __slots__ = ("ap", "lw", "rd", "excl")

    def __init__(self, ap, excl=False):
        self.ap = ap
        self.lw = None
        self.rd = {}
        self.excl = excl

    def __getitem__(self, idx):
        return self.ap[idx]


class Sub:
    __slots__ = ("ap", "parent")

    def __init__(self, parent, ap):
        self.parent = parent
        self.ap = ap

    def __getitem__(self, idx):
        return self.ap[idx]

    @property
    def excl(self):
        return self.parent.excl

    @property
    def lw(self):
        return self.parent.lw

    @lw.setter
    def lw(self, v):
        self.parent.lw = v

    @property
    def rd(self):
        return self.parent.rd

    @rd.setter
    def rd(self, v):
        self.parent.rd = v


class Prog:
    ENG = ("pe", "dve", "act", "pool", "sp")
    NLANES = {"sp": 20, "pool": 10}

    def __init__(self):
        self.ops = {e: [] for e in self.ENG}
        self.cnt = {e: 0 for e in self.ENG}
        self.seen = {e: {} for e in self.ENG}
        self.lane_val = {}
        self.lane_rr = {q: 0 for q in self.NLANES}
        for q, n in self.NLANES.items():
            for i in range(n):
                self.lane_val[f"d_{q}{i}"] = 0

    def sem_names(self):
        return list(self.ENG[:4]) + list(self.lane_val.keys())

    def _collect(self, eng, reads, writes, extra=()):
        w = {}
        seen = self.seen[eng]

        def add(s, v, is_rd=False):
            if s == eng and eng == "pe":
                return
            if seen.get(s, 0) >= v:
                return
            if w.get(s, 0) < v:
                w[s] = v
        for t in reads:
            if t.lw is not None:
                add(t.lw[0], t.lw[1])
            if t.excl:
                for s, v in t.rd.items():
                    if s != eng:
                        add(s, v, True)
        for t in writes:
            if t.lw is not None:
                add(t.lw[0], t.lw[1])
            for s, v in t.rd.items():
                add(s, v, True)
        for s, v in extra:
            add(s, v)
        for s, v in w.items():
            seen[s] = v
        return list(w.items())

    def _commit(self, tok, reads, writes):
        s, v = tok
        for t in reads:
            if t.rd.get(s, 0) < v:
                t.rd[s] = v
        for t in writes:
            t.lw = tok
            t.rd = {}

    def op(self, eng, fn, reads=(), writes=()):
        waits = self._collect(eng, reads, writes)
        self.cnt[eng] += 1
        tok = (eng, self.cnt[eng])
        self.ops[eng].append((waits, fn, (eng, 1)))
        self._commit(tok, reads, writes)
        return tok

    def dma(self, out_ap, in_ap, reads=(), writes=(), q="sp", **kw):
        n = self.NLANES[q]
        lane = f"d_{q}{self.lane_rr[q] % n}"
        self.lane_rr[q] += 1
        extra = []
        if self.lane_val[lane] > 0:
            extra.append((lane, self.lane_val[lane]))
        waits = self._collect(q, reads, writes, extra)
        self.lane_val[lane] += 16
        tok = (lane, self.lane_val[lane])
        self.ops[q].append((waits, lambda e: e.dma_start(out=out_ap, in_=in_ap, **kw), (lane, 16)))
        self._commit(tok, reads, writes)
        return tok

    def barrier(self):
        toks = [(e, self.cnt[e]) for e in self.ENG[:4] if self.cnt[e] > 0]
        toks += [(l, v) for l, v in self.lane_val.items() if v > 0]
        for e in self.ENG:
            waits = []
            for s, v in toks:
                if self.seen[e].get(s, 0) < v:
                    waits.append((s, v))
                    self.seen[e][s] = v
            if waits:
                self.ops[e].append((waits, None, None))

    def emit(self, nc, stack):
        semh = {}
        for s in self.sem_names():
            semh[s] = stack.enter_context(nc.semaphore(s))
        block = stack.enter_context(nc.Block())
        ops = self.ops

        def run(engobj, lst):
            for waits, fn, inc in lst:
                for s, v in waits:
                    engobj.wait_ge(semh[s], v)
                if fn is not None:
                    fn(engobj).then_inc(semh[inc[0]], inc[1])

        @block.tensor
        def _(e):
            run(e, ops["pe"])

        @block.vector
        def _(e):
            run(e, ops["dve"])

        @block.scalar
        def _(e):
            run(e, ops["act"])

        @block.gpsimd
        def _(e):
            run(e, ops["pool"])

        @block.sync
        def _(e):
            run(e, ops["sp"])


class Arena:
    def __init__(self, ap_f32, nwords):
        self.base = ap_f32
        self.n = nwords
        self.off = 0
        self.peak = 0

    def alloc(self, free, dtype=F32):
        free = list(free)
        nel = int(np.prod(free))
        words = nel if dtype == F32 else (nel + 1) // 2
        words = (words + 1) // 2 * 2
        assert self.off + words <= self.n, f"arena overflow {self.off}+{words}>{self.n}"
        v = self.base[:, self.off:self.off + words]
        self.off += words
        self.peak = max(self.peak, self.off)
        if dtype != F32:
            v = v.bitcast(dtype)[:, 0:nel]
        if len(free) == 2:
            v = v.rearrange("p (a b) -> p a b", a=free[0])
        elif len(free) == 3:
            v = v.rearrange("p (a b c) -> p a b c", a=free[0], b=free[1])
        elif len(free) == 4:
            v = v.rearrange("p (a b c d) -> p a b c d", a=free[0], b=free[1], c=free[2])
        return Tile(v)

    def mark(self):
        return self.off

    def release(self, m):
        self.off = m


def _t5_bucket(rel):
    half = 16
    max_exact = 8
    ret = np.where(rel > 0, half, 0)
    n = np.abs(rel)
    large = max_exact + (np.log(np.maximum(n, 1) / max_exact) / np.log(1024 / max_exact) * (half - max_exact)).astype(np.int32)
    large = np.minimum(large, half - 1)
    return (ret + np.where(n < max_exact, n, large)).astype(np.int32)


def _consts():
    c = {}
    c["c_identb"] = np.eye(128).astype(NPBF)
    c["c_identf"] = np.eye(128).astype(np.float32)
    blk = np.zeros((128, 128), np.float32)
    blk[:64, :64] = 1.0
    blk[64:, 64:] = 1.0
    c["c_blk"] = blk
    c["c_onesb"] = np.ones((128, 128), NPBF)
    s = np.arange(128)[:, None]
    t = np.arange(128)[None, :]
    same = (s // 64) == (t // 64)
    mat = np.zeros((2, 128, 4, 128), np.float32)
    mx0 = np.zeros((2, 128, 128), np.float32)
    for d in range(2):
        if d == 0:
            strictT = same & (s < t)
            inclT = same & (s <= t)
        else:
            strictT = same & (s > t)
            inclT = same & (s >= t)
        mat[d, :, 0] = strictT
        mat[d, :, 1] = inclT
        mat[d, :, 2] = strictT
        mat[d, :, 3] = inclT
        mx0[d] = strictT.T
    c["c_maskat"] = mat.astype(NPBF)
    c["c_maskx0"] = mx0.astype(NPBF)
    seg = np.ones((128, 512), np.float32)
    seg[:, ::64] = 0.0
    c["c_seg"] = seg
    oh = np.zeros((3, 33, 384), np.float32)
    for bi, dil in enumerate(DILS):
        for n in range(383):
            rel = 191 - n
            if abs(rel) <= 64:
                oh[bi, _t5_bucket(np.array(rel * dil)), n] = 8.0
            else:
                oh[bi, 32, n] = 8.0 * NEG
    c["c_onehot"] = oh
    misc = np.zeros((128, 8), np.float32)
    misc[:64, 1] = NEG
    misc[64:, 2] = NEG
    misc[:, 3] = RMS_EPS
    misc[:, 4] = LNX_EPS
    misc[:, 5] = 1.0
    c["c_misc"] = misc
    return c


CONST_SPECS = [
    ("c_identb", [128, 128], BF16), ("c_identf", [128, 128], F32), ("c_blk", [128, 128], F32),
    ("c_onesb", [128, 128], BF16), ("c_maskat", [2, 128, 4, 128], BF16), ("c_maskx0", [2, 128, 128], BF16),
    ("c_seg", [128, 512], F32), ("c_onehot", [3, 33, 384], F32), ("c_misc", [128, 8], F32),
]

WEIGHT_SPECS = [
    ("rel_bias", [32, 6]), ("norm_mix_pre", [NL, D]), ("norm_mix_post", [NL, D]), ("norm_ffn_pre", [NL, D]),
    ("norm_ffn_post", [NL, D]), ("w_in", [NL, D, INW]), ("w_out", [NL, D, D]), ("attn_out_g", [NL, 384]),
    ("conv_w", [NL, 3, 256]), ("conv_out_g", [NL, 256]), ("rwkv_mu", [NL, 1280]), ("decay_w0", [NL, 2, 384]),
    ("decay_up", [NL, 2, 32, 384]), ("iclr_a0", [NL, 2, 384]), ("iclr_up", [NL, 2, 32, 384]),
    ("gate_up", [NL, 64, 384]), ("k_k", [NL, 384]), ("k_a", [NL, 384]), ("r_k", [NL, 6, 64]),
    ("lnx_w", [NL, 384]), ("lnx_b", [NL, 384]), ("ffn_w1", [NL, D, DFF]), ("ffn_w2", [NL, DFF, D]),
]


class Builder:
    def __init__(self, seq_lens, debug=False, stop_after=None):
        self.seq_lens = list(seq_lens)
        self.debug = debug
        self.stop_after = stop_after
        self.nc = bass.Bass("TRN2", target_bir_lowering=False)
        self.P = Prog()
        self.rr = 0

    def dram(self, name, shape, dtype, kind="Internal"):
        if kind == "Internal" and self.debug:
            kind = "ExternalOutput"
        return self.nc.dram_tensor(name, list(shape), dtype, kind=kind).ap()

    @staticmethod
    def rawap(ap, offset, pat):
        return bass.AP(tensor=ap.tensor, offset=offset, ap=[list(x) for x in pat])

    def mm(self, ot, o, l, r, rd, start=True, stop=True):
        self.P.op("pe", lambda e: e.matmul(o, lhsT=l, rhs=r, start=start, stop=stop), reads=rd, writes=[ot])

    def tr(self, ot, o, i, rd):
        idt = self.identb if i.dtype == BF16 else self.identf
        n = i.shape[0]
        ident = idt[0:n, 0:n]
        self.P.op("pe", lambda e: e.transpose(o, i, ident), reads=list(rd) + [idt], writes=[ot])

    def act(self, ot, o, i, func, rd, bias=None, scale=None, accum=None, wr=()):
        kw = {}
        if bias is not None:
            kw["bias"] = bias
        if scale is not None:
            kw["scale"] = scale
        if accum is not None:
            kw["accum_out"] = accum
        self.P.op("act", lambda e: e.activation(out=o, in_=i, func=func, **kw), reads=rd, writes=[ot] + list(wr))

    def tt(self, eng, ot, o, a, b, op, rd):
        self.P.op(eng, lambda e: e.tensor_tensor(out=o, in0=a, in1=b, op=op), reads=rd, writes=[ot])

    def ts(self, eng, ot, o, a, s1, op0, rd, s2=None, op1=None):
        if op1 is None:
            self.P.op(eng, lambda e: e.tensor_scalar(out=o, in0=a, scalar1=s1, scalar2=None, op0=op0), reads=rd, writes=[ot])
        else:
            self.P.op(eng, lambda e: e.tensor_scalar(out=o, in0=a, scalar1=s1, scalar2=s2, op0=op0, op1=op1), reads=rd, writes=[ot])

    def stt(self, ot, o, a, s, b, op0, op1, rd):
        self.P.op("dve", lambda e: e.scalar_tensor_tensor(out=o, in0=a, scalar=s, in1=b, op0=op0, op1=op1), reads=rd, writes=[ot])

    def cp(self, eng, ot, o, i, rd):
        if eng == "act":
            self.P.op("act", lambda e: e.activation(out=o, in_=i, func=AF.Copy), reads=rd, writes=[ot])
        else:
            self.P.op(eng, lambda e: e.tensor_copy(out=o, in_=i), reads=rd, writes=[ot])

    def evac(self, ot, o, i, rd):
        self.rr += 1
        self.cp("act" if self.rr % 2 else "dve", ot, o, i, rd)

    def memset(self, eng, ot, o, val):
        self.P.op(eng, lambda e: e.memset(o, val), writes=[ot])

    def recip(self, ot, o, i, rd):
        self.P.op("dve", lambda e: e.reciprocal(out=o, in_=i), reads=rd, writes=[ot])

    def scan(self, ot, o, d0, d1, rd):
        self.P.op("dve", lambda e: e.tensor_tensor_scan(out=o, data0=d0, data1=d1, initial=0.0, op0=ALU.mult, op1=ALU.add), reads=rd, writes=[ot])

    def dma(self, o, i, rd=(), wr=(), q="sp", **kw):
        self.P.dma(o, i, reads=rd, writes=wr, q=q, **kw)

    def build(self):
        nc = self.nc
        self.xin = []
        self.yout = []
        for si, T in enumerate(self.seq_lens):
            self.xin.append(nc.dram_tensor(f"x{si}", [T, D], F32, kind="ExternalInput").ap())
            self.yout.append(nc.dram_tensor(f"y{si}", [T, D], F32, kind="ExternalOutput").ap())
        self.w = {}
        for name, shape in WEIGHT_SPECS:
            self.w[name] = nc.dram_tensor(name, shape, F32, kind="ExternalInput").ap()
        self.c = {}
        for name, shape, dt in CONST_SPECS:
            self.c[name] = nc.dram_tensor(name, shape, dt, kind="ExternalInput").ap()
        TM = max(self.seq_lens)
        self.TM = TM
        self.winb = [self.dram(f"winb{l}", [D, INW], BF16) for l in range(NL)]
        self.woutb = [self.dram(f"woutb{l}", [D, D], BF16) for l in range(NL)]
        self.w1b = [self.dram(f"w1b{l}", [D, DFF], BF16) for l in range(NL)]
        self.w2b = [self.dram(f"w2b{l}", [DFF, D], BF16) for l in range(NL)]
        self.s_x1 = self.dram("s_x1", [TM, D], F32)
        self.s_v = self.dram("s_v", [TM, 390], BF16)
        self.s_b = self.dram("s_b", [256, TM], F32)
        self.s_u = self.dram("s_u", [256, TM], F32)
        self.s_zc = self.dram("s_zc", [1280, TM], F32)
        self.s_o = self.dram("s_o", [3, TM, 390], F32)
        self.s_mix = self.dram("s_mix", [D, TM], BF16)
        self.s_yp = self.dram("s_yp", [384, TM], F32)
        self.s_bon = self.dram("s_bon", [384, TM], F32)
        self.s_g = self.dram("s_g", [384, TM], F32)
        self.s_rhb = self.dram("s_rhb", [384, TM], BF16)
        self.s_gtb = self.dram("s_gtb", [TM // 512, 128, 3 * 8 * 128], BF16)
        self.s_hb = self.dram("s_hb", [TM // 512, 128, 3 * 8 * 128], F32)
        self.s_gb = self.dram("s_gb", [18, 384], F32)
        self.s_skew = self.dram("s_skew", [18, 128 * 385], F32)
        self.s_bias = self.dram("s_bias", [128, 18 * 256], BF16)

        with ExitStack() as st:
            arena_t = st.enter_context(nc.sbuf_tensor("arena", [128, 51200], F32))
            psum_t = st.enter_context(nc.psum_tensor("psum", [128, 4096], F32))
            self.A = Arena(arena_t[:], 51200)
            self.psum = psum_t
            self.pb = [Tile(psum_t[:, i * 512:(i + 1) * 512], excl=True) for i in range(8)]
            self.setup_consts()
            m0 = self.A.mark()
            done = False
            for l in range(NL):
                self.layer_params(l)
                self.weight_prep(l)
                self.P.barrier()
            for si, T in enumerate(self.seq_lens):
                for l in range(NL):
                    self.A.release(m0)
                    src = self.xin[si] if l == 0 else self.s_x1
                    dst = self.s_x1 if l == 0 else self.yout[si]
                    self.layer_params(l)
                    self.P.barrier()
                    self.seq_layer(T, l, src, dst)
                    self.P.barrier()
                    if self.stop_after is not None:
                        done = True
                        break
                if done:
                    break
            self.P.barrier()
            self.P.emit(nc, st)
        return nc

    def setup_consts(self):
        A = self.A
        self.identb = A.alloc([128], BF16)
        self.identf = A.alloc([128], F32)
        self.blk = A.alloc([128], F32)
        self.blk64 = A.alloc([128], F32)
        self.onesb = A.alloc([128], BF16)
        self.maskat = [A.alloc([4, 128], BF16) for _ in range(2)]
        self.maskx0 = [A.alloc([128], BF16) for _ in range(2)]
        self.seg = A.alloc([512], F32)
        self.misc = A.alloc([8], F32)
        self.pv = A.alloc([128], F32)
        self.lr = A.alloc([5, 384], F32)
        c = self.c
        self.dma(self.identb[:], c["c_identb"], wr=[self.identb])
        self.dma(self.identf[:], c["c_identf"], wr=[self.identf])
        self.dma(self.blk[:], c["c_blk"], wr=[self.blk])
        self.dma(self.onesb[:], c["c_onesb"], wr=[self.onesb])
        for d in range(2):
            self.dma(self.maskat[d][:], c["c_maskat"][d], wr=[self.maskat[d]])
            self.dma(self.maskx0[d][:], c["c_maskx0"][d], wr=[self.maskx0[d]])
        self.dma(self.seg[:], c["c_seg"], wr=[self.seg])
        self.dma(self.misc[:], c["c_misc"], wr=[self.misc])
        self.ts("pool", self.blk64, self.blk64[:], self.blk[:], 1.0 / 64, ALU.mult, [self.blk])
        self.zero_c = self.misc[:, 0:1]
        self.edge_first = self.misc[:, 1:2]
        self.edge_last = self.misc[:, 2:3]
        self.eps_rms = self.misc[:, 3:4]
        self.eps_ln = self.misc[:, 4:5]
        m = A.mark()
        relb = A.alloc([6], F32)
        oh = A.alloc([3, 384], F32)
        gsb = A.alloc([3, 384], F32)
        bf = A.alloc([18, 256], F32)
        bb16 = A.alloc([18, 256], BF16)
        self.memset("pool", relb, relb[:], 1.0)
        self.dma(relb[0:32, :], self.w["rel_bias"], wr=[relb])
        self.dma(oh[0:33, :, :], c["c_onehot"].rearrange("b k n -> k b n"), wr=[oh])
        gbT = Tile(self.s_gb)
        for bi in range(3):
            ps = self.pb[bi]
            self.mm(ps, ps[0:6, 0:384], relb[0:33, 0:6], oh[0:33, bi, :], [relb, oh])
            self.cp("dve", gsb, gsb[0:6, bi, :], ps[0:6, 0:384], [ps])
            self.dma(self.s_gb[bi * 6:(bi + 1) * 6, :], gsb[0:6, bi, :], rd=[gsb], wr=[gbT])
        skT = Tile(self.s_skew)
        self.dma(self.rawap(self.s_skew, 0, [[128 * 385, 18], [385, 128], [1, 384]]),
                 self.rawap(self.s_gb, 0, [[384, 18], [0, 128], [1, 384]]), rd=[gbT], wr=[skT])
        for r in range(18):
            self.dma(bf[:, r, :].rearrange("p (a b) -> p a b", a=2),
                     self.rawap(self.s_skew, r * 128 * 385 + 255, [[384, 128], [-128, 2], [1, 128]]), rd=[skT], wr=[bf])
        self.cp("dve", bb16, bb16[:], bf[:], [bf])
        self.dma(self.s_bias, bb16[:].rearrange("p a b -> p (a b)"), rd=[bb16])
        self.P.barrier()
        A.release(m)

    PV = dict(m1=0, m2=10, kk=20, ka=23, oka=26, rk=29, lw=32, lb=35, w0=38, a0=44, cw=50, gmp=56, gfp=64, gwo=72, mu=80)

    def layer_params(self, l):
        pv, w, PV = self.pv, self.w, self.PV

        def col(dst0, src_ap, n):
            self.dma(pv[:, dst0:dst0 + n], src_ap.rearrange("(c p) -> p c", p=128), wr=[pv], allow_slow_non_contiguous=True)
        col(PV["mu"], w["rwkv_mu"][l], 10)
        col(PV["kk"], w["k_k"][l], 3)
        col(PV["ka"], w["k_a"][l], 3)
        col(PV["rk"], w["r_k"][l].rearrange("h n -> (h n)"), 3)
        col(PV["lw"], w["lnx_w"][l], 3)
        col(PV["lb"], w["lnx_b"][l], 3)
        for d in range(2):
            col(PV["w0"] + 3 * d, w["decay_w0"][l, d], 3)
            col(PV["a0"] + 3 * d, w["iclr_a0"][l, d], 3)
        for tap in range(3):
            self.dma(pv[:, PV["cw"] + tap:PV["cw"] + tap + 4:3], w["conv_w"][l, tap].rearrange("(c p) -> p c", p=128), wr=[pv],
                     allow_slow_non_contiguous=True)
        col(PV["gmp"], w["norm_mix_pre"][l], 8)
        col(PV["gfp"], w["norm_ffn_pre"][l], 8)
        self.memset("pool", pv, pv[:, PV["gwo"]:PV["gwo"] + 8], 1.0)
        col(PV["gwo"], w["attn_out_g"][l], 3)
        col(PV["gwo"] + 3, w["conv_out_g"][l], 2)
        self.ts("pool", pv, pv[:, PV["m1"]:PV["m1"] + 10], pv[:, PV["mu"]:PV["mu"] + 10], -1.0, ALU.mult, [pv], 1.0, ALU.add)
        self.ts("pool", pv, pv[:, PV["m2"]:PV["m2"] + 10], pv[:, PV["mu"]:PV["mu"] + 10], 0.5, ALU.mult, [pv])
        self.ts("pool", pv, pv[:, PV["oka"]:PV["oka"] + 3], pv[:, PV["ka"]:PV["ka"] + 3], -1.0, ALU.mult, [pv], 1.0, ALU.add)
        lr = self.lr
        self.memset("pool", lr, lr[:], 0.0)
        for d in range(2):
            self.dma(lr[0:32, d, :], w["decay_up"][l, d], wr=[lr])
            self.dma(lr[32:64, 2 + d, :], w["iclr_up"][l, d], wr=[lr])
        self.dma(lr[64:128, 4, :], w["gate_up"][l], wr=[lr])

    def weight_prep(self, l):
        A, w, PV = self.A, self.w, self.PV
        m = A.mark()
        wi = [A.alloc([4096], F32) for _ in range(2)]
        wo = [A.alloc([4096], BF16) for _ in range(2)]
        jobs = []
        for rb in range(8):
            jobs.append((w["w_in"][l, rb * 128:(rb + 1) * 128, :], self.winb[l][rb * 128:(rb + 1) * 128, :], INW, PV["gmp"] + rb))
        for rb in range(8):
            jobs.append((w["w_out"][l, rb * 128:(rb + 1) * 128, :], self.woutb[l][rb * 128:(rb + 1) * 128, :], D, PV["gwo"] + rb))
        for rb in range(8):
            jobs.append((w["ffn_w1"][l, rb * 128:(rb + 1) * 128, :], self.w1b[l][rb * 128:(rb + 1) * 128, :], DFF, PV["gfp"] + rb))
        for rb in range(8):
            src = w["ffn_w2"][l, rb * 512:(rb + 1) * 512, :].rearrange("(a p) c -> p a c", p=128)
            dst = self.w2b[l][rb * 512:(rb + 1) * 512, :].rearrange("(a p) c -> p a c", p=128)
            jobs.append((src, dst, 4096, None))
        engs = ["act", "dve", "pool"]
        for i, (src, dst, ncol, gcol) in enumerate(jobs):
            a, b = wi[i % 2], wo[i % 2]
            if gcol is None:
                self.dma(a[:].rearrange("p (a c) -> p a c", a=4), src, wr=[a])
                self.cp(engs[i % 3], b, b[:], a[:], [a])
                self.dma(dst, b[:].rearrange("p (a c) -> p a c", a=4), rd=[b])
            else:
                self.dma(a[:, 0:ncol], src, wr=[a])
                e = engs[i % 3]
                if e == "act":
                    self.act(b, b[:, 0:ncol], a[:, 0:ncol], AF.Copy, [a, self.pv], scale=self.pv[:, gcol:gcol + 1])
                else:
                    self.ts(e, b, b[:, 0:ncol], a[:, 0:ncol], self.pv[:, gcol:gcol + 1], ALU.mult, [a, self.pv])
                self.dma(dst, b[:, 0:ncol], rd=[b])
        A.release(m)

    def seq_layer(self, T, l, src, dst):
        A = self.A
        m0 = A.mark()
        self.qres = [A.alloc([3, self.TM], BF16) for _ in range(2)]
        self.kres = A.alloc([3, self.TM + 2 * KPAD], BF16)
        m1 = A.mark()
        self.phase_a(T, l, src)
        self.P.barrier()
        A.release(m1)
        if self.stop_after == "a":
            return
        self.phase_attn(T)
        self.P.barrier()
        A.release(m0)
        if self.stop_after == "attn":
            return
        self.phase_merge(T)
        self.P.barrier()
        A.release(m0)
        if self.stop_after == "merge":
            return
        self.phase_conv(T)
        self.P.barrier()
        A.release(m0)
        if self.stop_after == "conv":
            return
        self.phase_r1(T)
        self.P.barrier()
        A.release(m0)
        if self.stop_after == "r1":
            return
        self.phase_r2(T)
        self.P.barrier()
        A.release(m0)
        if self.stop_after == "r2":
            return
        self.phase_c(T, l, src, dst)
        A.release(m0)

    def norm_transpose(self, xt, ht, hT, ss, rs, junk, ptiles):
        for j in range(4):
            self.act(junk, junk[:], xt[:, j, :], AF.Square, [xt], accum=ss[:, j:j + 1], wr=[ss])
        self.act(rs, rs[:], ss[:], AF.Sqrt, [ss, self.misc], bias=self.eps_rms, scale=1.0 / D)
        self.recip(rs, rs[:], rs[:], [rs])
        for j in range(4):
            if j % 2 == 0:
                self.act(ht, ht[:, j, :], xt[:, j, :], AF.Copy, [xt, rs], scale=rs[:, j:j + 1])
            else:
                self.ts("dve", ht, ht[:, j, :], xt[:, j, :], rs[:, j:j + 1], ALU.mult, [xt, rs])
        for kc in range(8):
            ps = ptiles[kc % len(ptiles)]
            pv = ps.ap.bitcast(BF16)
            for j in range(4):
                self.tr(ps, pv[:, j * 128:(j + 1) * 128], ht[:, j, kc * 128:(kc + 1) * 128], [ht])
            self.evac(hT, hT[:, kc, :], pv[:, 0:512], [ps])

    def phase_a(self, T, l, src):
        A, pb = self.A, self.pb
        nt = T // 512
        xt = [A.alloc([4, 1024], F32) for _ in range(2)]
        ht = A.alloc([4, 1024], BF16)
        hT = [A.alloc([8, 512], BF16) for _ in range(2)]
        junk = A.alloc([1024], BF16)
        ss = A.alloc([4], F32)
        rs = A.alloc([4], F32)
        wp = [A.alloc([8, 512], BF16) for _ in range(2)]
        zst = [A.alloc([512], F32) for _ in range(4)]
        csb = [A.alloc([512], F32) for _ in range(2)]
        vst = [A.alloc([6, 65], BF16) for _ in range(4)]
        for v in vst:
            self.memset("pool", v, v[:], 1.0)
        kres, qres = self.kres, self.qres
        self.memset("pool", qres[0], qres[0][64:128, :, :], 0.0)
        self.memset("pool", qres[1], qres[1][0:64, :, :], 0.0)
        self.memset("pool", kres, kres[:, :, 0:KPAD], 0.0)
        self.memset("pool", kres, kres[:, :, KPAD + T:KPAD + T + KPAD], 0.0)
        winb = self.winb[l]
        sv, sb_, su, szc = Tile(self.s_v), Tile(self.s_b), Tile(self.s_u), Tile(self.s_zc)
        zi = 0
        wi = 0
        vi = 0
        for tt in range(nt):
            t0 = tt * 512
            x = xt[tt % 2]
            h = hT[tt % 2]
            self.dma(x[:], src[t0:t0 + 512, :].rearrange("(j p) d -> p j d", p=128), wr=[x])
            self.norm_transpose(x, ht, h, ss, rs, junk, [pb[0], pb[1]])
            for pc in range(7):
                ncol = 512 if pc < 6 else 128
                wt = wp[wi % 2]
                wi += 1
                self.dma(wt[:, :, 0:ncol], winb[:, pc * 512:pc * 512 + ncol].rearrange("(kc p) c -> p kc c", p=128), wr=[wt])
                for cc in range(ncol // 128):
                    zc = pc * 4 + cc
                    if 6 <= zc <= 8:
                        continue
                    ps = pb[2 + (zc % 4)]
                    for kc in range(8):
                        self.mm(ps, ps[:], wt[:, kc, cc * 128:(cc + 1) * 128], h[:, kc, :], [wt, h], start=(kc == 0), stop=(kc == 7))
                    if zc < 3:
                        self.cp("act", qres[0], qres[0][0:64, zc, t0:t0 + 512], ps[0:64, :], [ps])
                        self.cp("dve", qres[1], qres[1][64:128, zc, t0:t0 + 512], ps[64:128, :], [ps])
                    elif zc < 6:
                        self.evac(kres, kres[:, zc - 3, KPAD + t0:KPAD + t0 + 512], ps[:], [ps])
                    elif zc < 11:
                        z = zst[zi % 4]
                        zi += 1
                        self.evac(z, z[:], ps[:], [ps])
                        self.dma(self.s_b[(zc - 9) * 128:(zc - 8) * 128, t0:t0 + 512], z[:], rd=[z], wr=[sb_])
                    elif zc < 13:
                        self.evac(csb[zc - 11], csb[zc - 11][:], ps[:], [ps])
                    elif zc < 15:
                        z = zst[zi % 4]
                        zi += 1
                        self.tt("dve", z, z[:], ps[:], csb[zc - 13][:], ALU.mult, [ps, csb[zc - 13]])
                        self.dma(self.s_u[(zc - 13) * 128:(zc - 12) * 128, t0:t0 + 512], z[:], rd=[z], wr=[su])
                    else:
                        z = zst[zi % 4]
                        zi += 1
                        self.evac(z, z[:], ps[:], [ps])
                        self.dma(self.s_zc[(zc - 15) * 128:(zc - 14) * 128, t0:t0 + 512], z[:], rd=[z], wr=[szc])
                if pc in (1, 2):
                    c0, nv, h0 = (256, 256, 0) if pc == 1 else (0, 128, 4)
                    for j in range(4):
                        ps = pb[6 + (j % 2)]
                        for kc in range(8):
                            self.mm(ps, ps[:, 0:nv], h[:, kc, j * 128:(j + 1) * 128], wt[:, kc, c0:c0 + nv], [wt, h], start=(kc == 0), stop=(kc == 7))
                        if pc == 1:
                            v = vst[j]
                            self.evac(v, v[:, 0:4, 0:64], ps[:, 0:256].rearrange("p (a b) -> p a b", a=4), [ps])
                        else:
                            v = vst[j]
                            self.evac(v, v[:, 4:6, 0:64], ps[:, 0:128].rearrange("p (a b) -> p a b", a=2), [ps])
                            self.dma(self.s_v[t0 + j * 128:t0 + (j + 1) * 128, :], v[:].rearrange("p a b -> p (a b)"), rd=[v], wr=[sv])

    def phase_attn(self, T):
        A, pb = self.A, self.pb
        qres, kres = self.qres, self.kres
        vbuf = [A.alloc([9, 390], BF16) for _ in range(2)]
        for v in vbuf:
            self.memset("pool", v, v[:], 1.0)
        self.bias = A.alloc([18, 256], BF16)
        self.dma(self.bias[:].rearrange("p a b -> p (a b)"), self.s_bias, wr=[self.bias])
        sbl = [A.alloc([256], F32) for _ in range(3)]
        pT = [A.alloc([256], BF16) for _ in range(3)]
        ost = [A.alloc([390], F32) for _ in range(2)]
        lg = [Sub(pb[i], pb[i][:, 0:256]) for i in (0, 1, 4, 5)]
        ops_ = [pb[2], pb[3]]
        so = Tile(self.s_o)
        sv = Tile(self.s_v)
        ui = 0
        bi_ = 0
        li = 0
        for br, dil in enumerate(DILS):
            L = T // dil
            nblk = L // 128
            for c in range(dil):
                for g0 in range(0, nblk, 8):
                    nb = min(8, nblk - g0)
                    vb = vbuf[ui % 2]
                    ui += 1
                    for i in range(nb + 1):
                        kt = g0 + i
                        k0 = -64 + 128 * kt
                        lo = 64 if kt == 0 else 0
                        hi = 64 if kt == nblk else 128
                        r0 = c + dil * (k0 + lo)
                        self.dma(vb[lo:hi, i, :], self.rawap(self.s_v, r0 * 390, [[dil * 390, hi - lo], [1, 390]]), rd=[sv], wr=[vb])
                    for bl in range(nb):
                        b = g0 + bl
                        op = ops_[bi_ % 2]
                        bi_ += 1
                        q0 = c + dil * 128 * b
                        ka0 = KPAD + c + dil * (128 * b - 64)
                        kb0 = ka0 + 128 * dil
                        pend = []
                        for step in range(8):
                            if step < 6:
                                h = step
                                hp, base = h // 2, (h % 2) * 64
                                lgt = lg[li % 4]
                                sb = sbl[li % 3]
                                p = pT[li % 3]
                                li += 1
                                qm = qres[h % 2]
                                LV = int(os.environ.get("ATT_LEVEL", "9"))
                                qap = qm[:, hp, q0:q0 + 127 * dil + 1:dil]
                                if LV >= 1:
                                    self.mm(lgt, lgt[:, 0:128], kres[:, hp, ka0:ka0 + 127 * dil + 1:dil], qap, [kres, qm])
                                    self.mm(lgt, lgt[:, 128:256], kres[:, hp, kb0:kb0 + 127 * dil + 1:dil], qap, [kres, qm])
                                if LV >= 2:
                                    self.tt("dve", sb, sb[:], lgt[:], self.bias[:, br * 6 + h, :], ALU.add, [lgt, self.bias])
                                first, last = (b == 0), (b == nblk - 1)
                                if LV >= 3:
                                    if not first and not last:
                                        self.act(p, p[:], sb[:], AF.Exp, [sb, self.misc], bias=self.zero_c, scale=0.125)
                                    else:
                                        self.act(p, p[:, 0:128], sb[:, 0:128], AF.Exp, [sb, self.misc],
                                                 bias=self.edge_first if first else self.zero_c, scale=0.125)
                                        self.act(p, p[:, 128:256], sb[:, 128:256], AF.Exp, [sb, self.misc],
                                                 bias=self.edge_last if last else self.zero_c, scale=0.125)
                                pend.append((h, p))
                            if step >= 2 and LV >= 4:
                                h, p = pend[step - 2]
                                self.mm(op, op[:, h * 65:(h + 1) * 65], p[:, 0:128], vb[:, bl, h * 65:(h + 1) * 65], [p, vb], start=True, stop=False)
                                self.mm(op, op[:, h * 65:(h + 1) * 65], p[:, 128:256], vb[:, bl + 1, h * 65:(h + 1) * 65], [p, vb], start=False, stop=True)
                        if int(os.environ.get("ATT_LEVEL", "9")) < 5:
                            continue
                        o = ost[bi_ % 2]
                        self.evac(o, o[:], op[:, 0:390], [op])
                        self.dma(self.rawap(self.s_o, (br * self.TM + q0) * 390, [[dil * 390, 128], [1, 390]]), o[:], rd=[o], wr=[so])

    def phase_merge(self, T):
        A, pb = self.A, self.pb
        om = [A.alloc([4, 3, 390], F32) for _ in range(2)]
        sm = A.alloc([6, 65], F32)
        rd_ = A.alloc([6], F32)
        ya = A.alloc([6, 64], F32)
        yb = A.alloc([384], BF16)
        junk = A.alloc([384], BF16)
        ss = A.alloc([2], F32)
        mst = [A.alloc([3, 512], BF16) for _ in range(2)]
        so = Tile(self.s_o)
        smix = Tile(self.s_mix)
        for tt in range(T // 512):
            t0 = tt * 512
            o = om[tt % 2]
            ms = mst[tt % 2]
            for br in range(3):
                self.dma(o[:, :, br, :], self.s_o[br, t0:t0 + 512, :].rearrange("(j p) f -> p j f", p=128), rd=[so], wr=[o])
            ps = pb[tt % 2]
            pv = ps.ap.bitcast(BF16)
            ps2 = pb[2 + tt % 2]
            pv2 = ps2.ap.bitcast(BF16)
            for j in range(4):
                s3 = sm[:].rearrange("p a b -> p (a b)")
                self.tt("dve", sm, s3, o[:, j, 0, :], o[:, j, 1, :], ALU.add, [o])
                self.tt("dve", sm, s3, s3, o[:, j, 2, :], ALU.add, [o, sm])
                self.recip(rd_, rd_[:], sm[:, :, 64], [sm])
                self.tt("dve", ya, ya[:], sm[:, :, 0:64], rd_[:].unsqueeze(2).to_broadcast([128, 6, 64]), ALU.mult, [sm, rd_])
                yaf = ya[:].rearrange("p a b -> p (a b)")
                self.act(junk, junk[:], yaf, AF.Square, [ya], accum=ss[:, 0:1], wr=[ss])
                self.act(ss, ss[:, 1:2], ss[:, 0:1], AF.Sqrt, [ss, self.misc], bias=self.eps_rms, scale=1.0 / 384)
                self.recip(ss, ss[:, 1:2], ss[:, 1:2], [ss])
                self.act(yb, yb[:], yaf, AF.Copy, [ya, ss], scale=ss[:, 1:2])
                for i in range(3):
                    if i < 2:
                        self.tr(ps, pv[:, i * 512 + j * 128:i * 512 + (j + 1) * 128], yb[:, i * 128:(i + 1) * 128], [yb])
                    else:
                        self.tr(ps2, pv2[:, j * 128:(j + 1) * 128], yb[:, i * 128:(i + 1) * 128], [yb])
            self.evac(ms, ms[:, 0:2, :], pv[:, 0:1024].rearrange("p (a b) -> p a b", a=2), [ps])
            self.evac(ms, ms[:, 2, :], pv2[:, 0:512], [ps2])
            self.dma(self.s_mix[0:384, t0:t0 + 512].rearrange("(a p) t -> p a t", p=128), ms[:], rd=[ms], wr=[smix])

    def phase_conv(self, T):
        A, pb, PV = self.A, self.pb, self.PV
        ub = [A.alloc([2, 514], F32) for _ in range(2)]
        bb = [A.alloc([2, 512], F32) for _ in range(2)]
        c1 = A.alloc([512], F32)
        c2 = A.alloc([512], F32)
        yb = A.alloc([2, 512], F32)
        sq = A.alloc([2, 512], BF16)
        rs = A.alloc([512], F32)
        ybn = [A.alloc([2, 512], BF16) for _ in range(2)]
        su, sb_, smix = Tile(self.s_u), Tile(self.s_b), Tile(self.s_mix)
        pv = self.pv
        for tt in range(T // 512):
            t0 = tt * 512
            u, bt, yo = ub[tt % 2], bb[tt % 2], ybn[tt % 2]
            lo = 1 if tt == 0 else 0
            hi = 513 if tt == T // 512 - 1 else 514
            if lo == 1:
                self.memset("pool", u, u[:, :, 0:1], 0.0)
            if hi == 513:
                self.memset("pool", u, u[:, :, 513:514], 0.0)
            self.dma(u[:, :, lo:hi], self.s_u[:, t0 - 1 + lo:t0 - 1 + hi].rearrange("(a p) t -> p a t", p=128), rd=[su], wr=[u])
            self.dma(bt[:], self.s_b[:, t0:t0 + 512].rearrange("(a p) t -> p a t", p=128), rd=[sb_], wr=[bt])
            ps = pb[tt % 2]
            for ci in range(2):
                cw = PV["cw"] + 3 * ci
                self.ts("pool", c1, c1[:], u[:, ci, 1:513], pv[:, cw + 1:cw + 2], ALU.mult, [u, pv])
                self.stt(c2, c2[:], u[:, ci, 0:512], pv[:, cw:cw + 1], c1[:], ALU.mult, ALU.add, [u, pv, c1])
                self.stt(c1, c1[:], u[:, ci, 2:514], pv[:, cw + 2:cw + 3], c2[:], ALU.mult, ALU.add, [u, pv, c2])
                self.tt("pool", yb, yb[:, ci, :], c1[:], bt[:, ci, :], ALU.mult, [c1, bt])
                self.act(sq, sq[:, ci, :], yb[:, ci, :], AF.Square, [yb])
                self.mm(ps, ps[:], self.onesb[:], sq[:, ci, :], [self.onesb, sq], start=(ci == 0), stop=(ci == 1))
            self.act(rs, rs[:], ps[:], AF.Sqrt, [ps, self.misc], bias=self.eps_rms, scale=1.0 / 256)
            self.recip(rs, rs[:], rs[:], [rs])
            for ci in range(2):
                self.tt("dve" if ci == 0 else "pool", yo, yo[:, ci, :], yb[:, ci, :], rs[:], ALU.mult, [yb, rs])
            self.dma(self.s_mix[384:640, t0:t0 + 512].rearrange("(a p) t -> p a t", p=128), yo[:], rd=[yo], wr=[smix])

    def phase_r1(self, T):
        A, pb, PV, pv = self.A, self.pb, self.PV, self.pv
        nt = T // 512
        F = lambda: A.alloc([512], F32)
        zc = A.alloc([10, 514], F32)
        xs = A.alloc([10, 512], F32)
        t1 = [F() for _ in range(2)]
        t2 = [F() for _ in range(2)]
        kk2, rn, tmp, tk = t1[0], t1[1], t2[0], t2[1]
        tw = F()
        kk, kkn = F(), F()
        sg, aa, clw, E1, E2, enl, ta, bd = (F() for _ in range(8))
        lw = sg
        kd = [F(), F()]
        gst = F()
        bst = E1
        yst = [E2, enl]
        tot = A.alloc([8], F32)
        gC = [A.alloc([8], F32) for _ in range(2)]
        ARx = [A.alloc([2, 2, 512], BF16) for _ in range(2)]
        BT = [A.alloc([512], BF16) for _ in range(2)]
        KT = [A.alloc([512], BF16) for _ in range(2)]
        BG = A.alloc([512], BF16)
        KG = A.alloc([512], BF16)
        VT = A.alloc([512], BF16)
        tmA = [A.alloc([4, 128], BF16) for _ in range(2)]
        gx = [A.alloc([2, 4, 2, 128], BF16) for _ in range(2)]
        vtm = A.alloc([4, 128], BF16)
        atz = [A.alloc([128], BF16) for _ in range(2)]
        atk = [[[A.alloc([2, 128], BF16) for _ in range(2)] for _ in range(2)] for _ in range(4)]
        AW = [[[A.alloc([128], BF16) for _ in range(2)] for _ in range(2)] for _ in range(4)]
        Zs = [[A.alloc([128], F32) for _ in range(6)] for _ in range(2)]
        Zp = [[A.alloc([128], F32) for _ in range(5)] for _ in range(2)]
        Rk = [[A.alloc([128], F32) for _ in range(2)] for _ in range(2)]
        RHf = A.alloc([512], BF16)
        RHb = A.alloc([3, 512], BF16)
        GTf = A.alloc([8, 128], BF16)
        Hf = A.alloc([8, 128], F32)
        GTb = A.alloc([3, 8, 128], BF16)
        Hb = A.alloc([3, 8, 128], F32)
        Pst = [A.alloc([9, 128], BF16) for _ in range(3)]
        for t in ARx + gx + [GTf, GTb, Hb] + Pst:
            self.memset("pool", t, t[:], 0.0)
        self.memset("pool", Hf, Hf[:], 0.0)
        ps_at = [pb[0], pb[1]]
        ps_x = [Sub(pb[2 + i // 4], self.psum[:, 1024 + i * 128:1024 + (i + 1) * 128]) for i in range(8)]
        ps_r = [pb[4], pb[5]]
        ps_g1 = [Sub(pb[6], self.psum[hh * 64:(hh + 1) * 64, 3072:3200]) for hh in range(2)]
        ps_g2 = [[Sub(pb[6], self.psum[hh * 64:(hh + 1) * 64, 3200 + c * 64:3264 + c * 64]) for c in range(2)] for hh in range(2)]
        ps_g3 = [Sub(pb[6], self.psum[hh * 64:(hh + 1) * 64, 3328:3456]) for hh in range(2)]
        ps_c = Sub(pb[6], self.psum[:, 3456:3584])
        ps_y = pb[7]
        ps_m = ps_at
        szc, sg_, sbon, syp, srh, sgt, shb = (Tile(self.s_zc), Tile(self.s_g), Tile(self.s_bon), Tile(self.s_yp),
                                              Tile(self.s_rhb), Tile(self.s_gtb), Tile(self.s_hb))
        lr = self.lr
        c3 = lambda t: t[:].rearrange("p (a b) -> p a b", a=8)
        xi = 0
        for tt in range(nt):
            t0 = tt * 512
            lo = 1 if tt == 0 else 0
            hi = 513 if tt == nt - 1 else 514
            if lo == 1:
                self.memset("pool", zc, zc[:, :, 0:1], 0.0)
            if hi == 513:
                self.memset("pool", zc, zc[:, :, 513:514], 0.0)
            self.dma(zc[:, :, lo:hi], self.s_zc[:, t0 - 1 + lo:t0 - 1 + hi].rearrange("(a p) t -> p a t", p=128), rd=[szc], wr=[zc])
            for cc in range(10):
                a1, a2 = t1[cc % 2], t2[cc % 2]
                self.tt("pool", a1, a1[:], zc[:, cc, 0:512], zc[:, cc, 2:514], ALU.add, [zc])
                self.act(a2, a2[:], a1[:], AF.Copy, [a1, pv], scale=pv[:, PV["m2"] + cc:PV["m2"] + cc + 1])
                self.stt(xs, xs[:, cc, :], zc[:, cc, 1:513], pv[:, PV["m1"] + cc:PV["m1"] + cc + 1], a2[:], ALU.mult, ALU.add, [zc, pv, a2])
            self.act(tw, tw[0:32, :], xs[0:32, 9, :], AF.Tanh, [xs])
            self.cp("dve", tw, tw[32:64, :], xs[32:64, 9, :], [xs])
            self.act(tw, tw[64:128, :], xs[64:128, 9, :], AF.Sigmoid, [xs])
            RL = int(os.environ.get("R1_LEVEL", "9"))
            for hp in range(3):
                if RL < 1:
                    continue
                r_, k_, v_ = xs[:, hp, :], xs[:, 3 + hp, :], xs[:, 6 + hp, :]
                hc = slice(hp * 128, (hp + 1) * 128)
                pg = ps_m[0]
                self.mm(pg, pg[:], lr[:, 4, hc], tw[:], [lr, tw])
                self.evac(gst, gst[:], pg[:], [pg])
                self.dma(self.s_g[hc, t0:t0 + 512], gst[:], rd=[gst], wr=[sg_])
                self.act(kk, kk[:], k_, AF.Copy, [xs, pv], scale=pv[:, PV["kk"] + hp:PV["kk"] + hp + 1])
                self.tt("pool", kk2, kk2[:], kk[:], kk[:], ALU.mult, [kk])
                pn = ps_m[1]
                self.mm(pn, pn[:], self.blk[:], kk2[:], [self.blk, kk2])
                self.act(rn, rn[:], pn[:], AF.Sqrt, [pn])
                self.ts("dve", rn, rn[:], rn[:], 1e-12, ALU.max, [rn])
                self.recip(rn, rn[:], rn[:], [rn])
                self.tt("pool", kkn, kkn[:], kk[:], rn[:], ALU.mult, [kk, rn])
                self.cp("act", VT, VT[:], v_, [xs])
                pt = ps_m[0]
                ptv = pt.ap.bitcast(BF16)
                for np_ in range(4):
                    self.tr(pt, ptv[:, np_ * 128:(np_ + 1) * 128], VT[:, np_ * 128:(np_ + 1) * 128], [VT])
                self.evac(vtm, vtm[:].rearrange("p a b -> p (a b)"), ptv[:, 0:512], [pt])
                for d in range(2):
                    if RL < 2:
                        continue
                    arx, bt_, kt_ = ARx[d], BT[d], KT[d]
                    pw = ps_m[0]
                    self.mm(pw, pw[:], lr[:, d, hc], tw[:], [lr, tw])
                    self.act(sg, sg[:], pw[:], AF.Sigmoid, [pw, pv], bias=pv[:, PV["w0"] + 3 * d + hp:PV["w0"] + 3 * d + hp + 1])
                    pa = ps_m[1]
                    self.mm(pa, pa[:], lr[:, 2 + d, hc], tw[:], [lr, tw])
                    self.act(aa, aa[:], pa[:], AF.Sigmoid, [pa, pv], bias=pv[:, PV["a0"] + 3 * d + hp:PV["a0"] + 3 * d + hp + 1])
                    self.ts("pool", lw, lw[:], sg[:], WSCALE, ALU.mult, [sg])
                    self.scan(clw, clw[:], self.seg[:], lw[:], [self.seg, lw])
                    self.cp("dve", tot, tot[:], c3(clw)[:, :, 63], [clw])
                    if d == 1:
                        self.tt("dve", tmp, tmp[:], lw[:], clw[:], ALU.subtract, [lw, clw])
                        self.tt("dve", clw, c3(clw), c3(tmp), tot[:].unsqueeze(2).to_broadcast([128, 8, 64]), ALU.add, [tmp, tot])
                    self.act(E1, E1[:], clw[:], AF.Exp, [clw])
                    self.act(E2, E2[:], clw[:], AF.Exp, [clw], scale=-1.0)
                    self.act(enl, enl[:], lw[:], AF.Exp, [lw], scale=-1.0)
                    self.act(gC[d], gC[d][:], tot[:], AF.Exp, [tot])
                    self.tt("pool", ta, ta[:], kkn[:], enl[:], ALU.mult, [kkn, enl])
                    for hh in range(2):
                        bs = slice(hh * 64, hh * 64 + 64)
                        self.stt(arx, arx[bs, hh, 0, :], ta[bs, :], -1.0, E1[bs, :], ALU.mult, ALU.mult, [ta, E1])
                        self.tt("pool", arx, arx[bs, hh, 1, :], xs[bs, hp, :], E1[bs, :], ALU.mult, [xs, E1])
                    self.ts("dve", tk, tk[:], aa[:], pv[:, PV["ka"] + hp:PV["ka"] + hp + 1], ALU.mult, [aa, pv],
                            pv[:, PV["oka"] + hp:PV["oka"] + hp + 1], ALU.add)
                    self.tt("pool", kd[d], kd[d][:], k_, tk[:], ALU.mult, [xs, tk])
                    self.tt("pool", bd, bd[:], kkn[:], aa[:], ALU.mult, [kkn, aa])
                    self.tt("dve", bt_, bt_[:], bd[:], E2[:], ALU.mult, [bd, E2])
                    self.tt("dve", kt_, kt_[:], kd[d][:], E2[:], ALU.mult, [kd[d], E2])
                    gcb = gC[d][:].unsqueeze(2).to_broadcast([128, 8, 64])
                    self.tt("dve", BG, c3(BG), c3(bt_), gcb, ALU.mult, [bt_, gC[d]])
                    self.tt("dve", KG, c3(KG), c3(kt_), gcb, ALU.mult, [kt_, gC[d]])
                    if RL < 3:
                        continue
                    p1, p2 = ps_m[0], ps_m[1]
                    p1v, p2v = p1.ap.bitcast(BF16), p2.ap.bitcast(BF16)
                    for np_ in range(4):
                        tsl = slice(np_ * 128, (np_ + 1) * 128)
                        for hh in range(2):
                            pass
                    for np_ in range(4):
                        tsl = slice(np_ * 128, (np_ + 1) * 128)
                        self.tr(p1, p1v[0:128, np_ * 128:(np_ + 1) * 128], arx[0:128, 0, 0, tsl], [arx])
                        self.tr(p2, p2v[0:128, np_ * 128:(np_ + 1) * 128], arx[0:128, 1, 0, tsl], [arx])
                    tmd = tmA[d]
                    self.evac(tmd, tmd[:, :, 0:64], p1v[:, 0:512].rearrange("p (a b) -> p a b", a=4)[:, :, 0:64], [p1])
                    self.evac(tmd, tmd[:, :, 64:128], p2v[:, 0:512].rearrange("p (a b) -> p a b", a=4)[:, :, 64:128], [p2])
                    gxd = gx[d]
                    for q, (srct, pq, pqv) in enumerate(((BG, p1, p1v), (KG, p2, p2v))):
                        for np_ in range(4):
                            self.tr(pq, pqv[:, 512 + np_ * 128:512 + (np_ + 1) * 128], srct[:, np_ * 128:(np_ + 1) * 128], [srct])
                        v4 = pqv[:, 512:1024].rearrange("p (a b) -> p a b", a=4)
                        self.cp("act", gxd, gxd[0:64, q, :, 0, :], v4[0:64], [pq])
                        self.cp("dve", gxd, gxd[64:128, q, :, 1, :], v4[64:128], [pq])
                    def unit(np_, hh, slot):
                        tsl = slice(np_ * 128, (np_ + 1) * 128)
                        tokp = tsl
                        bs = slice(hh * 64, hh * 64 + 64)
                        az, ak, aw = atz[slot], atk[np_][hh][d], AW[np_][hh][d]
                        pat = ps_at[slot]
                        px0 = ps_x[slot * 4]
                        self.mm(pat, pat[:, 0:256], bt_[:, tsl], arx[:, hh, :, tsl], [bt_, arx])
                        self.mm(pat, pat[:, 256:512], kt_[:, tsl], arx[:, hh, :, tsl], [kt_, arx])
                        self.mm(px0, px0[:], arx[:, hh, 0, tsl], bt_[:, tsl], [arx, bt_])
                        p4 = pat[:].rearrange("p (a b) -> p a b", a=4)
                        m4 = self.maskat[d]
                        Z, ZP = Zs[slot], Zp[slot]
                        self.tt("dve", Z[0], Z[0][:], p4[:, 0, :], m4[:, 0, :], ALU.mult, [pat, m4])
                        self.tt("dve", az, az[:], p4[:, 2, :], m4[:, 2, :], ALU.mult, [pat, m4])
                        self.tt("dve", ak, ak[:], p4[:, 1:4:2, :], m4[:, 1:4:2, :], ALU.mult, [pat, m4])
                        self.tt("dve", ZP[0], ZP[0][:], px0[:], self.maskx0[d][:], ALU.mult, [px0, self.maskx0[d]])
                        yield
                        pr = ps_r[slot]
                        R0 = Rk[slot][0]
                        self.mm(pr, pr[:, 0:64], az[:], vtm[:, np_, bs], [az, vtm])
                        self.cp("pool", R0, R0[:, 0:64], tmd[:, np_, bs], [tmd])
                        self.cp("act", R0, R0[:, 64:128], pr[:, 0:64], [pr])
                        yield
                        pz = ps_x[slot * 4 + 1]
                        pz2 = Sub(pat, pat[:, 0:128])
                        for kq in range(6):
                            rk = Rk[slot][kq % 2]
                            rnx = Rk[slot][(kq + 1) % 2] if kq < 5 else aw
                            if kq < 5:
                                self.mm(pz, pz[:], ZP[kq][:], Z[kq][:], [ZP[kq], Z[kq]])
                            if kq < 4:
                                self.mm(pz2, pz2[:], Z[kq][:], ZP[kq][:], [ZP[kq], Z[kq]])
                            self.mm(pr, pr[:, 0:128], Z[kq][:], rk[:], [Z[kq], rk])
                            if kq < 5:
                                self.cp("act", Z[kq + 1], Z[kq + 1][:], pz[:], [pz])
                            if kq < 4:
                                self.cp("act", ZP[kq + 1], ZP[kq + 1][:], pz2[:], [pz2])
                            self.tt("dve", rnx, rnx[:], pr[:, 0:128], rk[:], ALU.add, [pr, rk])
                            yield
                        pg1, pg3 = ps_g1[hh], ps_g3[hh]
                        self.mm(pg1, pg1[:], aw[:, 0:64], gxd[:, 0, np_, :, bs], [aw, gxd])
                        self.mm(pg3, pg3[:], aw[:, 0:64], ak[:, 0, :], [aw, ak])
                        for c in range(2):
                            pg2 = ps_g2[hh][c]
                            self.mm(pg2, pg2[:], gxd[:, 0, np_, c, bs], aw[:, 64:128], [aw, gxd], start=True, stop=False)
                            self.mm(pg2, pg2[:], gxd[:, 1, np_, c, bs], vtm[:, np_, bs], [gxd, vtm], start=False, stop=True)
                        for c in range(2):
                            ch = np_ * 2 + c
                            if d == 0:
                                gdst_t, gdst = GTf, GTf[bs, ch, bs]
                                hdst_t, hdst = Hf, Hf[bs, ch, bs]
                            else:
                                gdst_t, gdst = GTb, GTb[bs, hp, ch, bs]
                                hdst_t, hdst = Hb, Hb[bs, hp, ch, bs]
                            self.stt(gdst_t, gdst, self.identf[bs, bs], gC[d][bs, ch:ch + 1], pg1[:, c * 64:(c + 1) * 64], ALU.mult, ALU.add,
                                     [self.identf, gC[d], pg1])
                            self.cp("dve", hdst_t, hdst, ps_g2[hh][c][:], [ps_g2[hh][c]])
                        if d == 0:
                            rdst_t, rdst = RHf, RHf[bs, tokp]
                        else:
                            rdst_t, rdst = RHb, RHb[bs, hp, tokp]
                        self.tt("dve", rdst_t, rdst, pg3[:], arx[bs, hh, 1, tokp], ALU.add, [pg3, arx])

                    for np_ in range(4):
                        if RL < 4:
                            continue
                        gens = [unit(np_, 0, 0), unit(np_, 1, 1)]
                        while gens:
                            for g in list(gens):
                                try:
                                    next(g)
                                except StopIteration:
                                    gens.remove(g)
                if RL < 5:
                    continue
                self.tt("pool", tmp, tmp[:], kd[0][:], kd[1][:], ALU.add, [kd[0], kd[1]])
                self.stt(ta, ta[:], r_, pv[:, PV["rk"] + hp:PV["rk"] + hp + 1], tmp[:], ALU.mult, ALU.mult, [xs, pv, tmp])
                pbn = ps_m[0]
                self.mm(pbn, pbn[:], self.blk[:], ta[:], [self.blk, ta])
                self.tt("dve", bst, bst[:], pbn[:], v_, ALU.mult, [pbn, xs])
                self.dma(self.s_bon[hc, t0:t0 + 512], bst[:], rd=[bst], wr=[sbon])
                Pc = Pst[hp]
                if tt > 0:
                    self.cp("pool", Pc, Pc[:, 0, :], Pc[:, 8, :], [Pc])
                for ch in range(8):
                    self.mm(ps_c, ps_c[:], GTf[:, ch, :], Pc[:, ch, :], [GTf, Pc])
                    self.tt("dve", Pc, Pc[:, ch + 1, :], ps_c[:], Hf[:, ch, :], ALU.add, [ps_c, Hf])
                for np_ in range(4):
                    tokp = slice(np_ * 128, (np_ + 1) * 128)
                    for c in range(2):
                        ch = np_ * 2 + c
                        tokc = slice(ch * 64, (ch + 1) * 64)
                        self.mm(ps_y, ps_y[:, tokc], Pc[:, ch, :], RHf[:, tokc], [Pc, RHf], start=(c == 0), stop=False)
                    for hh in range(2):
                        bs = slice(hh * 64, hh * 64 + 64)
                        for d in range(2):
                            ak, aw = atk[np_][hh][d], AW[np_][hh][d]
                            self.mm(ps_y, ps_y[bs, tokp], aw[:, 64:128], ak[:, 0, :], [aw, ak], start=False, stop=False)
                            self.mm(ps_y, ps_y[bs, tokp], vtm[:, np_, bs], ak[:, 1, :], [vtm, ak], start=False, stop=(d == 1))
                ys = yst[hp % 2]
                self.evac(ys, ys[:], ps_y[:], [ps_y])
                self.dma(self.s_yp[hc, t0:t0 + 512], ys[:], rd=[ys], wr=[syp])
            self.dma(self.s_gtb[tt], GTb[:].rearrange("p a b c -> p (a b c)"), rd=[GTb], wr=[sgt])
            self.dma(self.s_hb[tt], Hb[:].rearrange("p a b c -> p (a b c)"), rd=[Hb], wr=[shb])
            self.dma(self.s_rhb[:, t0:t0 + 512].rearrange("(a p) t -> p a t", p=128), RHb[:], rd=[RHb], wr=[srh])

    def phase_r2(self, T):
        A, pb, PV, pv = self.A, self.pb, self.PV, self.pv
        nt = T // 512
        gtb = [A.alloc([3, 8, 128], BF16) for _ in range(2)]
        hb = [A.alloc([3, 8, 128], F32) for _ in range(2)]
        rhb = [A.alloc([3, 512], BF16) for _ in range(2)]
        yp = [A.alloc([3, 512], F32) for _ in range(2)]
        bon = [A.alloc([3, 512], F32) for _ in range(2)]
        gg = [A.alloc([3, 512], F32) for _ in range(2)]
        Pst = [A.alloc([9, 128], BF16) for _ in range(3)]
        y = A.alloc([512], F32)
        yc = A.alloc([512], F32)
        sq = A.alloc([512], F32)
        sd = A.alloc([512], F32)
        yo = [A.alloc([3, 512], BF16) for _ in range(2)]
        ps_c = Sub(pb[6], self.psum[:, 3072:3200])
        sg_, sbon, syp, srh, sgt, shb, smix = (Tile(self.s_g), Tile(self.s_bon), Tile(self.s_yp),
                                               Tile(self.s_rhb), Tile(self.s_gtb), Tile(self.s_hb), Tile(self.s_mix))
        for hp in range(3):
            self.memset("pool", Pst[hp], Pst[hp][:, 8, :], 0.0)
        for it, tt in enumerate(range(nt - 1, -1, -1)):
            t0 = tt * 512
            par = it % 2
            G, H, RH, YP, BO, GG, YO = gtb[par], hb[par], rhb[par], yp[par], bon[par], gg[par], yo[par]
            self.dma(G[:].rearrange("p a b c -> p (a b c)"), self.s_gtb[tt], rd=[sgt], wr=[G])
            self.dma(H[:].rearrange("p a b c -> p (a b c)"), self.s_hb[tt], rd=[shb], wr=[H])
            fm = lambda s_: s_[0:384, t0:t0 + 512].rearrange("(a p) t -> p a t", p=128)
            self.dma(RH[:], fm(self.s_rhb), rd=[srh], wr=[RH])
            self.dma(YP[:], fm(self.s_yp), rd=[syp], wr=[YP])
            self.dma(BO[:], fm(self.s_bon), rd=[sbon], wr=[BO])
            self.dma(GG[:], fm(self.s_g), rd=[sg_], wr=[GG])
            for hp in range(3):
                Pc = Pst[hp]
                if it > 0:
                    self.cp("pool", Pc, Pc[:, 8, :], Pc[:, 0, :], [Pc])
                for ch in range(7, -1, -1):
                    self.mm(ps_c, ps_c[:], G[:, hp, ch, :], Pc[:, ch + 1, :], [G, Pc])
                    self.tt("dve", Pc, Pc[:, ch, :], ps_c[:], H[:, hp, ch, :], ALU.add, [ps_c, H])
                py = pb[hp % 2]
                for ch in range(8):
                    tokc = slice(ch * 64, (ch + 1) * 64)
                    self.mm(py, py[:, tokc], Pc[:, ch + 1, :], RH[:, hp, tokc], [Pc, RH])
                self.tt("dve", y, y[:], py[:], YP[:, hp, :], ALU.add, [py, YP])
                pm = pb[2 + hp % 2]
                self.mm(pm, pm[:], self.blk64[:], y[:], [self.blk64, y])
                self.tt("dve", yc, yc[:], y[:], pm[:], ALU.subtract, [y, pm])
                self.act(sq, sq[:], yc[:], AF.Square, [yc])
                pvr = pb[4 + hp % 2]
                self.mm(pvr, pvr[:], self.blk64[:], sq[:], [self.blk64, sq])
                self.act(sd, sd[:], pvr[:], AF.Sqrt, [pvr, self.misc], bias=self.eps_ln)
                self.recip(sd, sd[:], sd[:], [sd])
                self.tt("pool", yc, yc[:], yc[:], sd[:], ALU.mult, [yc, sd])
                self.ts("dve", yc, yc[:], yc[:], pv[:, PV["lw"] + hp:PV["lw"] + hp + 1], ALU.mult, [yc, pv],
                        pv[:, PV["lb"] + hp:PV["lb"] + hp + 1], ALU.add)
                self.tt("pool", yc, yc[:], yc[:], BO[:, hp, :], ALU.add, [yc, BO])
                self.tt("dve", YO, YO[:, hp, :], yc[:], GG[:, hp, :], ALU.mult, [yc, GG])
            self.dma(self.s_mix[640:1024, t0:t0 + 512].rearrange("(a p) t -> p a t", p=128), YO[:], rd=[YO], wr=[smix])

    def phase_c(self, T, l, src, dst):
        A, pb = self.A, self.pb
        nt = T // 512
        mt = [A.alloc([8, 512], BF16) for _ in range(2)]
        xt = [A.alloc([4, 1024], F32) for _ in range(2)]
        mo = A.alloc([4, 1024], F32)
        ht = A.alloc([4, 1024], BF16)
        hT = A.alloc([8, 512], BF16)
        f1 = A.alloc([32, 512], BF16)
        rl = [A.alloc([512], BF16) for _ in range(2)]
        wq = [A.alloc([8, 512], BF16) for _ in range(3)]
        junk = A.alloc([1024], BF16)
        ss = A.alloc([4], F32)
        rs = A.alloc([4], F32)
        tmp = A.alloc([1024], F32)
        smix = Tile(self.s_mix)
        dT = Tile(dst)
        wi = 0
        gp = A.alloc([2, 1024], F32)
        self.dma(gp[:, 0, :], self.rawap(self.w["norm_mix_post"], l * D, [[0, 128], [1, D]]), wr=[gp])
        self.dma(gp[:, 1, :], self.rawap(self.w["norm_ffn_post"], l * D, [[0, 128], [1, D]]), wr=[gp])

        def post_norm_add(x, which):
            for j in range(4):
                self.act(junk, junk[:], mo[:, j, :], AF.Square, [mo], accum=ss[:, j:j + 1], wr=[ss])
            self.act(rs, rs[:], ss[:], AF.Sqrt, [ss, self.misc], bias=self.eps_rms, scale=1.0 / D)
            self.recip(rs, rs[:], rs[:], [rs])
            for j in range(4):
                self.stt(tmp, tmp[:], mo[:, j, :], rs[:, j:j + 1], gp[:, which, :], ALU.mult, ALU.mult, [mo, rs, gp])
                self.tt("pool", x, x[:, j, :], x[:, j, :], tmp[:], ALU.add, [x, tmp])

        for tt in range(nt):
            t0 = tt * 512
            m, x = mt[tt % 2], xt[tt % 2]
            self.dma(m[:], self.s_mix[:, t0:t0 + 512].rearrange("(a p) t -> p a t", p=128), rd=[smix], wr=[m])
            self.dma(x[:], src[t0:t0 + 512, :].rearrange("(j p) d -> p j d", p=128), wr=[x])
            for dh in range(2):
                wt = wq[wi % 3]
                wi += 1
                self.dma(wt[:], self.woutb[l][:, dh * 512:(dh + 1) * 512].rearrange("(kc p) c -> p kc c", p=128), wr=[wt])
                for j in range(4):
                    ps = pb[j]
                    for kc in range(8):
                        self.mm(ps, ps[:], m[:, kc, j * 128:(j + 1) * 128], wt[:, kc, :], [m, wt], start=(kc == 0), stop=(kc == 7))
                    self.evac(mo, mo[:, j, dh * 512:(dh + 1) * 512], ps[:], [ps])
            post_norm_add(x, 0)
            self.norm_transpose(x, ht, hT, ss, rs, junk, [pb[0], pb[1]])
            for fp in range(8):
                wt = wq[wi % 3]
                wi += 1
                self.dma(wt[:], self.w1b[l][:, fp * 512:(fp + 1) * 512].rearrange("(kc p) c -> p kc c", p=128), wr=[wt])
                for fc in range(4):
                    ps = pb[fc]
                    for kc in range(8):
                        self.mm(ps, ps[:], wt[:, kc, fc * 128:(fc + 1) * 128], hT[:, kc, :], [wt, hT], start=(kc == 0), stop=(kc == 7))
                    r = rl[fc % 2]
                    self.act(r, r[:], ps[:], AF.Relu, [ps])
                    self.tt("pool", f1, f1[:, fp * 4 + fc, :], r[:], r[:], ALU.mult, [r])
            for dh in range(2):
                for kp in range(4):
                    wt = wq[wi % 3]
                    wi += 1
                    self.dma(wt[:], self.w2b[l][kp * 1024:(kp + 1) * 1024, dh * 512:(dh + 1) * 512].rearrange("(kc p) c -> p kc c", p=128), wr=[wt])
                    for j in range(4):
                        ps = pb[4 + j]
                        for kc in range(8):
                            kg = kp * 8 + kc
                            self.mm(ps, ps[:], f1[:, kg, j * 128:(j + 1) * 128], wt[:, kc, :], [f1, wt], start=(kg == 0), stop=(kg == 31))
                for j in range(4):
                    self.evac(mo, mo[:, j, dh * 512:(dh + 1) * 512], pb[4 + j][:], [pb[4 + j]])
            post_norm_add(x, 1)
            self.dma(dst[t0:t0 + 512, :].rearrange("(j p) d -> p j d", p=128), x[:], rd=[x], wr=[dT])


_WNAMES = [n for n, _ in WEIGHT_SPECS]


def kernel(**inputs):
    xp = np.ascontiguousarray(np.asarray(inputs["x_prompt"], dtype=np.float32))
    xs = np.ascontiguousarray(np.asarray(inputs["x_sample"], dtype=np.float32))
    ncores = 8
    npp, nsp = xp.shape[0] // ncores, xs.shape[0] // ncores
    seq_lens = [xp.shape[1]] * npp + [xs.shape[1]] * nsp
    b = Builder(seq_lens)
    nc = b.build()
    consts = _consts()
    shared = {n: np.ascontiguousarray(np.asarray(inputs[n], dtype=np.float32)) for n in _WNAMES}
    shared.update(consts)
    in_maps = []
    for c in range(ncores):
        m = dict(shared)
        for i in range(npp):
            m[f"x{i}"] = xp[c * npp + i]
        for i in range(nsp):
            m[f"x{npp + i}"] = xs[c * nsp + i]
        in_maps.append(m)
    res = run_bass_kernel_spmd(nc, in_maps, core_ids=list(range(ncores)))
    yp = np.empty_like(xp)
    ys = np.empty_like(xs)
    for c in range(ncores):
        r = res.results[c]
        for i in range(npp):
            yp[c * npp + i] = r[f"y{i}"]
        for i in range(nsp):
            ys[c * nsp + i] = r[f"y{npp + i}"]
    return (yp, ys)
```

```python
import math
from contextlib import ExitStack

import numpy as np
import ml_dtypes

import concourse.bass as bass
import concourse.mybir as mybir
from concourse.bass_utils import run_bass_kernel_spmd

F32 = mybir.dt.float32
BF16 = mybir.dt.bfloat16
AF = mybir.ActivationFunctionType
ALU = mybir.AluOpType
NPBF = ml_dtypes.bfloat16

D = 1024
INW = 3200
DFF = 4096
NL = 2
RMS_EPS = 1e-6
LNX_EPS = 64e-5
DILS = (1, 4, 16)
KPAD = 1024
NEG = -30000.0
WSCALE = -math.exp(-0.5)


class Tile:
    __slots__ = ("ap", "lw", "rd", "excl")

    def __init__(self, ap, excl=False):
        self.ap = ap
        self.lw = None
        self.rd = {}
        self.excl = excl

    def __getitem__(self, idx):
        return self.ap[idx]


class Sub:
    __slots__ = ("ap", "parent")

    def __init__(self, parent, ap):
        self.parent = parent
        self.ap = ap

    def __getitem__(self, idx):
        return self.ap[idx]

    @property
    def excl(self):
        return self.parent.excl

    @property
    def lw(self):
        return self.parent.lw

    @lw.setter
    def lw(self, v):
        self.parent.lw = v

    @property
    def rd(self):
        return self.parent.rd

    @rd.setter
    def rd(self, v):
        self.parent.rd = v


class Prog:
    ENG = ("pe", "dve", "act", "pool", "sp")
    NLANES = {"sp": 20, "pool": 10}

    def __init__(self):
        self.ops = {e: [] for e in self.ENG}
        self.cnt = {e: 0 for e in self.ENG}
        self.seen = {e: {} for e in self.ENG}
        self.lane_val = {}
        self.lane_rr = {q: 0 for q in self.NLANES}
        for q, n in self.NLANES.items():
            for i in range(n):
                self.lane_val[f"d_{q}{i}"] = 0

    def sem_names(self):
        return list(self.ENG[:4]) + list(self.lane_val.keys())

    def _collect(self, eng, reads, writes, extra=()):
        w = {}
        seen = self.seen[eng]

        def add(s, v, is_rd=False):
            if s == eng and eng == "pe":
                return
            if seen.get(s, 0) >= v:
                return
            if w.get(s, 0) < v:
                w[s] = v
        for t in reads:
            if t.lw is not None:
                add(t.lw[0], t.lw[1])
            if t.excl:
                for s, v in t.rd.items():
                    if s != eng:
                        add(s, v, True)
        for t in writes:
            if t.lw is not None:
                add(t.lw[0], t.lw[1])
            for s, v in t.rd.items():
                add(s, v, True)
        for s, v in extra:
            add(s, v)
        for s, v in w.items():
            seen[s] = v
        return list(w.items())

    def _commit(self, tok, reads, writes):
        s, v = tok
        for t in reads:
            if t.rd.get(s, 0) < v:
                t.rd[s] = v
        for t in writes:
            t.lw = tok
            t.rd = {}

    def op(self, eng, fn, reads=(), writes=()):
        waits = self._collect(eng, reads, writes)
        self.cnt[eng] += 1
        tok = (eng, self.cnt[eng])
        self.ops[eng].append((waits, fn, (eng, 1)))
        self._commit(tok, reads, writes)
        return tok

    def dma(self, out_ap, in_ap, reads=(), writes=(), q="sp", **kw):
        n = self.NLANES[q]
        lane = f"d_{q}{self.lane_rr[q] % n}"
        self.lane_rr[q] += 1
        extra = []
        if self.lane_val[lane] > 0:
            extra.append((lane, self.lane_val[lane]))
        waits = self._collect(q, reads, writes, extra)
        self.lane_val[lane] += 16
        tok = (lane, self.lane_val[lane])
        self.ops[q].append((waits, lambda e: e.dma_start(out=out_ap, in_=in_ap, **kw), (lane, 16)))
        self._commit(tok, reads, writes)
        return tok

    def barrier(self):
        toks = [(e, self.cnt[e]) for e in self.ENG[:4] if self.cnt[e] > 0]
        toks += [(l, v) for l, v in self.lane_val.items() if v > 0]
        for e in self.ENG:
            waits = []
            for s, v in toks:
                if self.seen[e].get(s, 0) < v:
                    waits.append((s, v))
                    self.seen[e][s] = v
            if waits:
                self.ops[e].append((waits, None, None))

    def emit(self, nc, stack):
        semh = {}
        for s in self.sem_names():
            semh[s] = stack.enter_context(nc.semaphore(s))
        block = stack.enter_context(nc.Block())
        ops = self.ops

        def run(engobj, lst):
            for waits, fn, inc in lst:
                for s, v in waits:
                    engobj.wait_ge(semh[s], v)
                if fn is not None:
                    fn(engobj).then_inc(semh[inc[0]], inc[1])

        @block.tensor
        def _(e):
            run(e, ops["pe"])

        @block.vector
        def _(e):
            run(e, ops["dve"])

        @block.scalar
        def _(e):
            run(e, ops["act"])

        @block.gpsimd
        def _(e):
            run(e, ops["pool"])

        @block.sync
        def _(e):
            run(e, ops["sp"])


class Arena:
    def __init__(self, ap_f32, nwords):
        self.base = ap_f32
        self.n = nwords
        self.off = 0
        self.peak = 0

    def alloc(self, free, dtype=F32):
        free = list(free)
        nel = int(np.prod(free))
        words = nel if dtype == F32 else (nel + 1) // 2
        words = (words + 1) // 2 * 2
        assert self.off + words <= self.n, f"arena overflow {self.off}+{words}>{self.n}"
        v = self.base[:, self.off:self.off + words]
        self.off += words
        self.peak = max(self.peak, self.off)
        if dtype != F32:
            v = v.bitcast(dtype)[:, 0:nel]
        if len(free) == 2:
            v = v.rearrange("p (a b) -> p a b", a=free[0])
        elif len(free) == 3:
            v = v.rearrange("p (a b c) -> p a b c", a=free[0], b=free[1])
        elif len(free) == 4:
            v = v.rearrange("p (a b c d) -> p a b c d", a=free[0], b=free[1], c=free[2])
        return Tile(v)

    def mark(self):
        return self.off

    def release(self, m):
        self.off = m


def _t5_bucket(rel):
    half = 16
    max_exact = 8
    ret = np.where(rel > 0, half, 0)
    n = np.abs(rel)
    large = max_exact + (np.log(np.maximum(n, 1) / max_exact) / np.log(1024 / max_exact) * (half - max_exact)).astype(np.int32)
    large = np.minimum(large, half - 1)
    return (ret + np.where(n < max_exact, n, large)).astype(np.int32)


def _consts():
    c = {}
    c["c_identb"] = np.eye(128).astype(NPBF)
    c["c_identf"] = np.eye(128).astype(np.float32)
    blk = np.zeros((128, 128), np.float32)
    blk[:64, :64] = 1.0
    blk[64:, 64:] = 1.0
    c["c_blk"] = blk
    c["c_onesb"] = np.ones((128, 128), NPBF)
    s = np.arange(128)[:, None]
    t = np.arange(128)[None, :]
    same = (s // 64) == (t // 64)
    mat = np.zeros((2, 128, 4, 128), np.float32)
    mx0 = np.zeros((2, 128, 128), np.float32)
    for d in range(2):
        if d == 0:
            strictT = same & (s < t)
            inclT = same & (s <= t)
        else:
            strictT = same & (s > t)
            inclT = same & (s >= t)
        mat[d, :, 0] = strictT
        mat[d, :, 1] = inclT
        mat[d, :, 2] = strictT
        mat[d, :, 3] = inclT
        mx0[d] = strictT.T
    c["c_maskat"] = mat.astype(NPBF)
    c["c_maskx0"] = mx0.astype(NPBF)
    seg = np.ones((128, 512), np.float32)
    seg[:, ::64] = 0.0
    c["c_seg"] = seg
    oh = np.zeros((3, 33, 384), np.float32)
    for bi, dil in enumerate(DILS):
        for n in range(383):
            rel = 191 - n
            if abs(rel) <= 64:
                oh[bi, _t5_bucket(np.array(rel * dil)), n] = 8.0
            else:
                oh[bi, 32, n] = 8.0 * NEG
    c["c_onehot"] = oh
    misc = np.zeros((128, 8), np.float32)
    misc[:64, 1] = NEG
    misc[64:, 2] = NEG
    misc[:, 3] = RMS_EPS
    misc[:, 4] = LNX_EPS
    misc[:, 5] = 1.0
    c["c_misc"] = misc
    return c


CONST_SPECS = [
    ("c_identb", [128, 128], BF16), ("c_identf", [128, 128], F32), ("c_blk", [128, 128], F32),
    ("c_onesb", [128, 128], BF16), ("c_maskat", [2, 128, 4, 128], BF16), ("c_maskx0", [2, 128, 128], BF16),
    ("c_seg", [128, 512], F32), ("c_onehot", [3, 33, 384], F32), ("c_misc", [128, 8], F32),
]

WEIGHT_SPECS = [
    ("rel_bias", [32, 6]), ("norm_mix_pre", [NL, D]), ("norm_mix_post", [NL, D]), ("norm_ffn_pre", [NL, D]),
    ("norm_ffn_post", [NL, D]), ("w_in", [NL, D, INW]), ("w_out", [NL, D, D]), ("attn_out_g", [NL, 384]),
    ("conv_w", [NL, 3, 256]), ("conv_out_g", [NL, 256]), ("rwkv_mu", [NL, 1280]), ("decay_w0", [NL, 2, 384]),
    ("decay_up", [NL, 2, 32, 384]), ("iclr_a0", [NL, 2, 384]), ("iclr_up", [NL, 2, 32, 384]),
    ("gate_up", [NL, 64, 384]), ("k_k", [NL, 384]), ("k_a", [NL, 384]), ("r_k", [NL, 6, 64]),
    ("lnx_w", [NL, 384]), ("lnx_b", [NL, 384]), ("ffn_w1", [NL, D, DFF]), ("ffn_w2", [NL, DFF, D]),
]


class Builder:
    def __init__(self, seq_lens, debug=False, stop_after=None):
        self.seq_lens = list(seq_lens)
        self.debug = debug
        self.stop_after = stop_after
        self.nc = bass.Bass("TRN2", target_bir_lowering=False)
        self.P = Prog()
        self.rr = 0

    def dram(self, name, shape, dtype, kind="Internal"):
        if kind == "Internal" and self.debug:
            kind = "ExternalOutput"
        return self.nc.dram_tensor(name, list(shape), dtype, kind=kind).ap()

    @staticmethod
    def rawap(ap, offset, pat):
        return bass.AP(tensor=ap.tensor, offset=offset, ap=[list(x) for x in pat])

    def mm(self, ot, o, l, r, rd, start=True, stop=True):
        self.P.op("pe", lambda e: e.matmul(o, lhsT=l, rhs=r, start=start, stop=stop), reads=rd, writes=[ot])

    def tr(self, ot, o, i, rd):
        idt = self.identb if i.dtype == BF16 else self.identf
        n = i.shape[0]
        ident = idt[0:n, 0:n]
        self.P.op("pe", lambda e: e.transpose(o, i, ident), reads=list(rd) + [idt], writes=[ot])

    def act(self, ot, o, i, func, rd, bias=None, scale=None, accum=None, wr=()):
        kw = {}
        if bias is not None:
            kw["bias"] = bias
        if scale is not None:
            kw["scale"] = scale
        if accum is not None:
            kw["accum_out"] = accum
        self.P.op("act", lambda e: e.activation(out=o, in_=i, func=func, **kw), reads=rd, writes=[ot] + list(wr))

    def tt(self, eng, ot, o, a, b, op, rd):
        self.P.op(eng, lambda e: e.tensor_tensor(out=o, in0=a, in1=b, op=op), reads=rd, writes=[ot])

    def ts(self, eng, ot, o, a, s1, op0, rd, s2=None, op1=None):
        if op1 is None:
            self.P.op(eng, lambda e: e.tensor_scalar(out=o, in0=a, scalar1=s1, scalar2=None, op0=op0), reads=rd, writes=[ot])
        else:
            self.P.op(eng, lambda e: e.tensor_scalar(out=o, in0=a, scalar1=s1, scalar2=s2, op0=op0, op1=op1), reads=rd, writes=[ot])

    def stt(self, ot, o, a, s, b, op0, op1, rd):
        self.P.op("dve", lambda e: e.scalar_tensor_tensor(out=o, in0=a, scalar=s, in1=b, op0=op0, op1=op1), reads=rd, writes=[ot])

    def cp(self, eng, ot, o, i, rd):
        if eng == "act":
            self.P.op("act", lambda e: e.activation(out=o, in_=i, func=AF.Copy), reads=rd, writes=[ot])
        else:
            self.P.op(eng, lambda e: e.tensor_copy(out=o, in_=i), reads=rd, writes=[ot])

    def evac(self, ot, o, i, rd):
        self.rr += 1
        self.cp("act" if self.rr % 2 else "dve", ot, o, i, rd)

    def memset(self, eng, ot, o, val):
        self.P.op(eng, lambda e: e.memset(o, val), writes=[ot])

    def recip(self, ot, o, i, rd):
        self.P.op("dve", lambda e: e.reciprocal(out=o, in_=i), reads=rd, writes=[ot])

    def scan(self, ot, o, d0, d1, rd):
        self.P.op("dve", lambda e: e.tensor_tensor_scan(out=o, data0=d0, data1=d1, initial=0.0, op0=ALU.mult, op1=ALU.add), reads=rd, writes=[ot])

    def dma(self, o, i, rd=(), wr=(), q="sp", **kw):
        self.P.dma(o, i, reads=rd, writes=wr, q=q, **kw)

    def build(self):
        nc = self.nc
        self.xin = []
        self.yout = []
        for si, T in enumerate(self.seq_lens):
            self.xin.append(nc.dram_tensor(f"x{si}", [T, D], F32, kind="ExternalInput").ap())
            self.yout.append(nc.dram_tensor(f"y{si}", [T, D], F32, kind="ExternalOutput").ap())
        self.w = {}
        for name, shape in WEIGHT_SPECS:
            self.w[name] = nc.dram_tensor(name, shape, F32, kind="ExternalInput").ap()
        self.c = {}
        for name, shape, dt in CONST_SPECS:
            self.c[name] = nc.dram_tensor(name, shape, dt, kind="ExternalInput").ap()
        TM = max(self.seq_lens)
        self.TM = TM
        self.winb = [self.dram(f"winb{l}", [D, INW], BF16) for l in range(NL)]
        self.woutb = [self.dram(f"woutb{l}", [D, D], BF16) for l in range(NL)]
        self.w1b = [self.dram(f"w1b{l}", [D, DFF], BF16) for l in range(NL)]
        self.w2b = [self.dram(f"w2b{l}", [DFF, D], BF16) for l in range(NL)]
        self.s_x1 = self.dram("s_x1", [TM, D], F32)
        self.s_v = self.dram("s_v", [TM, 390], BF16)
        self.s_b = self.dram("s_b", [256, TM], F32)
        self.s_u = self.dram("s_u", [256, TM], F32)
        self.s_zc = self.dram("s_zc", [1280, TM], F32)
        self.s_o = self.dram("s_o", [3, TM, 390], F32)
        self.s_mix = self.dram("s_mix", [D, TM], BF16)
        self.s_yp = self.dram("s_yp", [384, TM], F32)
        self.s_bon = self.dram("s_bon", [384, TM], F32)
        self.s_g = self.dram("s_g", [384, TM], F32)
        self.s_rhb = self.dram("s_rhb", [384, TM], BF16)
        self.s_gtb = self.dram("s_gtb", [TM // 512, 128, 3 * 8 * 128], BF16)
        self.s_hb = self.dram("s_hb", [TM // 512, 128, 3 * 8 * 128], F32)
        self.s_gb = self.dram("s_gb", [18, 384], F32)
        self.s_skew = self.dram("s_skew", [18, 128 * 385], F32)
        self.s_bias = self.dram("s_bias", [128, 18 * 256], BF16)

        with ExitStack() as st:
            arena_t = st.enter_context(nc.sbuf_tensor("arena", [128, 51200], F32))
            psum_t = st.enter_context(nc.psum_tensor("psum", [128, 4096], F32))
            self.A = Arena(arena_t[:], 51200)
            self.psum = psum_t
            self.pb = [Tile(psum_t[:, i * 512:(i + 1) * 512], excl=True) for i in range(8)]
            self.setup_consts()
            m0 = self.A.mark()
            done = False
            for l in range(NL):
                self.layer_params(l)
                self.weight_prep(l)
                self.P.barrier()
            for si, T in enumerate(self.seq_lens):
                for l in range(NL):
                    self.A.release(m0)
                    src = self.xin[si] if l == 0 else self.s_x1
                    dst = self.s_x1 if l == 0 else self.yout[si]
                    self.layer_params(l)
                    self.P.barrier()
                    self.seq_layer(T, l, src, dst)
                    self.P.barrier()
                    if self.stop_after is not None:
                        done = True
                        break
                if done:
                    break
            self.P.barrier()
            self.P.emit(nc, st)
        return nc

    def setup_consts(self):
        A = self.A
        self.identb = A.alloc([128], BF16)
        self.identf = A.alloc([128], F32)
        self.blk = A.alloc([128], F32)
        self.blk64 = A.alloc([128], F32)
        self.onesb = A.alloc([128], BF16)
        self.maskat = [A.alloc([4, 128], BF16) for _ in range(2)]
        self.maskx0 = [A.alloc([128], BF16) for _ in range(2)]
        self.seg = A.alloc([512], F32)
        self.misc = A.alloc([8], F32)
        self.pv = A.alloc([128], F32)
        self.lr = A.alloc([5, 384], F32)
        c = self.c
        self.dma(self.identb[:], c["c_identb"], wr=[self.identb])
        self.dma(self.identf[:], c["c_identf"], wr=[self.identf])
        self.dma(self.blk[:], c["c_blk"], wr=[self.blk])
        self.dma(self.onesb[:], c["c_onesb"], wr=[self.onesb])
        for d in range(2):
            self.dma(self.maskat[d][:], c["c_maskat"][d], wr=[self.maskat[d]])
            self.dma(self.maskx0[d][:], c["c_maskx0"][d], wr=[self.maskx0[d]])
        self.dma(self.seg[:], c["c_seg"], wr=[self.seg])
        self.dma(self.misc[:], c["c_misc"], wr=[self.misc])
        self.ts("pool", self.blk64, self.blk64[:], self.blk[:], 1.0 / 64, ALU.mult, [self.blk])
        self.zero_c = self.misc[:, 0:1]
        self.edge_first = self.misc[:, 1:2]
        self.edge_last = self.misc[:, 2:3]
        self.eps_rms = self.misc[:, 3:4]
        self.eps_ln = self.misc[:, 4:5]
        m = A.mark()
        relb = A.alloc([6], F32)
        oh = A.alloc([3, 384], F32)
        gsb = A.alloc([3, 384], F32)
        bf = A.alloc([18, 256], F32)
        bb16 = A.alloc([18, 256], BF16)
        self.memset("pool", relb, relb[:], 1.0)
        self.dma(relb[0:32, :], self.w["rel_bias"], wr=[relb])
        self.dma(oh[0:33, :, :], c["c_onehot"].rearrange("b k n -> k b n"), wr=[oh])
        gbT = Tile(self.s_gb)
        for bi in range(3):
            ps = self.pb[bi]
            self.mm(ps, ps[0:6, 0:384], relb[0:33, 0:6], oh[0:33, bi, :], [relb, oh])
            self.cp("dve", gsb, gsb[0:6, bi, :], ps[0:6, 0:384], [ps])
            self.dma(self.s_gb[bi * 6:(bi + 1) * 6, :], gsb[0:6, bi, :], rd=[gsb], wr=[gbT])
        skT = Tile(self.s_skew)
        self.dma(self.rawap(self.s_skew, 0, [[128 * 385, 18], [385, 128], [1, 384]]),
                 self.rawap(self.s_gb, 0, [[384, 18], [0, 128], [1, 384]]), rd=[gbT], wr=[skT])
        for r in range(18):
            self.dma(bf[:, r, :].rearrange("p (a b) -> p a b", a=2),
                     self.rawap(self.s_skew, r * 128 * 385 + 255, [[384, 128], [-128, 2], [1, 128]]), rd=[skT], wr=[bf])
        self.cp("dve", bb16, bb16[:], bf[:], [bf])
        self.dma(self.s_bias, bb16[:].rearrange("p a b -> p (a b)"), rd=[bb16])
        self.P.barrier()
        A.release(m)

    PV = dict(m1=0, m2=10, kk=20, ka=23, oka=26, rk=29, lw=32, lb=35, w0=38, a0=44, cw=50, gmp=56, gfp=64, gwo=72, mu=80)

    def layer_params(self, l):
        pv, w, PV = self.pv, self.w, self.PV

        def col(dst0, src_ap, n):
            self.dma(pv[:, dst0:dst0 + n], src_ap.rearrange("(c p) -> p c", p=128), wr=[pv], allow_slow_non_contiguous=True)
        col(PV["mu"], w["rwkv_mu"][l], 10)
        col(PV["kk"], w["k_k"][l], 3)
        col(PV["ka"], w["k_a"][l], 3)
        col(PV["rk"], w["r_k"][l].rearrange("h n -> (h n)"), 3)
        col(PV["lw"], w["lnx_w"][l], 3)
        col(PV["lb"], w["lnx_b"][l], 3)
        for d in range(2):
            col(PV["w0"] + 3 * d, w["decay_w0"][l, d], 3)
            col(PV["a0"] + 3 * d, w["iclr_a0"][l, d], 3)
        for tap in range(3):
            self.dma(pv[:, PV["cw"] + tap:PV["cw"] + tap + 4:3], w["conv_w"][l, tap].rearrange("(c p) -> p c", p=128), wr=[pv],
                     allow_slow_non_contiguous=True)
        col(PV["gmp"], w["norm_mix_pre"][l], 8)
        col(PV["gfp"], w["norm_ffn_pre"][l], 8)
        self.memset("pool", pv, pv[:, PV["gwo"]:PV["gwo"] + 8], 1.0)
        col(PV["gwo"], w["attn_out_g"][l], 3)
        col(PV["gwo"] + 3, w["conv_out_g"][l], 2)
        self.ts("pool", pv, pv[:, PV["m1"]:PV["m1"] + 10], pv[:, PV["mu"]:PV["mu"] + 10], -1.0, ALU.mult, [pv], 1.0, ALU.add)
        self.ts("pool", pv, pv[:, PV["m2"]:PV["m2"] + 10], pv[:, PV["mu"]:PV["mu"] + 10], 0.5, ALU.mult, [pv])
        self.ts("pool", pv, pv[:, PV["oka"]:PV["oka"] + 3], pv[:, PV["ka"]:PV["ka"] + 3], -1.0, ALU.mult, [pv], 1.0, ALU.add)
        lr = self.lr
        self.memset("pool", lr, lr[:], 0.0)
        for d in range(2):
            self.dma(lr[0:32, d, :], w["decay_up"][l, d], wr=[lr])
            self.dma(lr[32:64, 2 + d, :], w["iclr_up"][l, d], wr=[lr])
        self.dma(lr[64:128, 4, :], w["gate_up"][l], wr=[lr])

    def weight_prep(self, l):
        A, w, PV = self.A, self.w, self.PV
        m = A.mark()
        wi = [A.alloc([4096], F32) for _ in range(2)]
        wo = [A.alloc([4096], BF16) for _ in range(2)]
        jobs = []
        for rb in range(8):
            jobs.append((w["w_in"][l, rb * 128:(rb + 1) * 128, :], self.winb[l][rb * 128:(rb + 1) * 128, :], INW, PV["gmp"] + rb))
        for rb in range(8):
            jobs.append((w["w_out"][l, rb * 128:(rb + 1) * 128, :], self.woutb[l][rb * 128:(rb + 1) * 128, :], D, PV["gwo"] + rb))
        for rb in range(8):
            jobs.append((w["ffn_w1"][l, rb * 128:(rb + 1) * 128, :], self.w1b[l][rb * 128:(rb + 1) * 128, :], DFF, PV["gfp"] + rb))
        for rb in range(8):
            src = w["ffn_w2"][l, rb * 512:(rb + 1) * 512, :].rearrange("(a p) c -> p a c", p=128)
            dst = self.w2b[l][rb * 512:(rb + 1) * 512, :].rearrange("(a p) c -> p a c", p=128)
            jobs.append((src, dst, 4096, None))
        engs = ["act", "dve", "pool"]
        for i, (src, dst, ncol, gcol) in enumerate(jobs):
            a, b = wi[i % 2], wo[i % 2]
            if gcol is None:
                self.dma(a[:].rearrange("p (a c) -> p a c", a=4), src, wr=[a])
                self.cp(engs[i % 3], b, b[:], a[:], [a])
                self.dma(dst, b[:].rearrange("p (a c) -> p a c", a=4), rd=[b])
            else:
                self.dma(a[:, 0:ncol], src, wr=[a])
                e = engs[i % 3]
                if e == "act":
                    self.act(b, b[:, 0:ncol], a[:, 0:ncol], AF.Copy, [a, self.pv], scale=self.pv[:, gcol:gcol + 1])
                else:
                    self.ts(e, b, b[:, 0:ncol], a[:, 0:ncol], self.pv[:, gcol:gcol + 1], ALU.mult, [a, self.pv])
                self.dma(dst, b[:, 0:ncol], rd=[b])
        A.release(m)

    def seq_layer(self, T, l, src, dst):
        A = self.A
        m0 = A.mark()
        self.qres = [A.alloc([3, self.TM], BF16) for _ in range(2)]
        self.kres = A.alloc([3, self.TM + 2 * KPAD], BF16)
        m1 = A.mark()
        self.phase_a(T, l, src)
        self.P.barrier()
        A.release(m1)
        if self.stop_after == "a":
            return
        self.phase_attn(T)
        self.P.barrier()
        A.release(m0)
        if self.stop_after == "attn":
            return
        self.phase_merge(T)
        self.P.barrier()
        A.release(m0)
        if self.stop_after == "merge":
            return
        self.phase_conv(T)
        self.P.barrier()
        A.release(m0)
        if self.stop_after == "conv":
            return
        self.phase_r1(T)
        self.P.barrier()
        A.release(m0)
        if self.stop_after == "r1":
            return
        self.phase_r2(T)
        self.P.barrier()
        A.release(m0)
        if self.stop_after == "r2":
            return
        self.phase_c(T, l, src, dst)
        A.release(m0)

    def norm_transpose(self, xt, ht, hT, ss, rs, junk, ptiles):
        for j in range(4):
            self.act(junk, junk[:], xt[:, j, :], AF.Square, [xt], accum=ss[:, j:j + 1], wr=[ss])
        self.act(rs, rs[:], ss[:], AF.Sqrt, [ss, self.misc], bias=self.eps_rms, scale=1.0 / D)
        self.recip(rs, rs[:], rs[:], [rs])
        for j in range(4):
            if j % 2 == 0:
                self.act(ht, ht[:, j, :], xt[:, j, :], AF.Copy, [xt, rs], scale=rs[:, j:j + 1])
            else:
                self.ts("dve", ht, ht[:, j, :], xt[:, j, :], rs[:, j:j + 1], ALU.mult, [xt, rs])
        for kc in range(8):
            ps = ptiles[kc % len(ptiles)]
            pv = ps.ap.bitcast(BF16)
            for j in range(4):
                self.tr(ps, pv[:, j * 128:(j + 1) * 128], ht[:, j, kc * 128:(kc + 1) * 128], [ht])
            self.evac(hT, hT[:, kc, :], pv[:, 0:512], [ps])

    def phase_a(self, T, l, src):
        A, pb = self.A, self.pb
        nt = T // 512
        xt = [A.alloc([4, 1024], F32) for _ in range(2)]
        ht = A.alloc([4, 1024], BF16)
        hT = [A.alloc([8, 512], BF16) for _ in range(2)]
        junk = A.alloc([1024], BF16)
        ss = A.alloc([4], F32)
        rs = A.alloc([4], F32)
        wp = [A.alloc([8, 512], BF16) for _ in range(2)]
        zst = [A.alloc([512], F32) for _ in range(4)]
        csb = [A.alloc([512], F32) for _ in range(2)]
        vst = [A.alloc([6, 65], BF16) for _ in range(4)]
        for v in vst:
            self.memset("pool", v, v[:], 1.0)
        kres, qres = self.kres, self.qres
        self.memset("pool", qres[0], qres[0][64:128, :, :], 0.0)
        self.memset("pool", qres[1], qres[1][0:64, :, :], 0.0)
        self.memset("pool", kres, kres[:, :, 0:KPAD], 0.0)
        self.memset("pool", kres, kres[:, :, KPAD + T:KPAD + T + KPAD], 0.0)
        winb = self.winb[l]
        sv, sb_, su, szc = Tile(self.s_v), Tile(self.s_b), Tile(self.s_u), Tile(self.s_zc)
        zi = 0
        wi = 0
        vi = 0
        for tt in range(nt):
            t0 = tt * 512
            x = xt[tt % 2]
            h = hT[tt % 2]
            self.dma(x[:], src[t0:t0 + 512, :].rearrange("(j p) d -> p j d", p=128), wr=[x])
            self.norm_transpose(x, ht, h, ss, rs, junk, [pb[0], pb[1]])
            for pc in range(7):
                ncol = 512 if pc < 6 else 128
                wt = wp[wi % 2]
                wi += 1
                self.dma(wt[:, :, 0:ncol], winb[:, pc * 512:pc * 512 + ncol].rearrange("(kc p) c -> p kc c", p=128), wr=[wt])
                for cc in range(ncol // 128):
                    zc = pc * 4 + cc
                    if 6 <= zc <= 8:
                        continue
                    ps = pb[2 + (zc % 4)]
                    for kc in range(8):
                        self.mm(ps, ps[:], wt[:, kc, cc * 128:(cc + 1) * 128], h[:, kc, :], [wt, h], start=(kc == 0), stop=(kc == 7))
                    if zc < 3:
                        self.cp("act", qres[0], qres[0][0:64, zc, t0:t0 + 512], ps[0:64, :], [ps])
                        self.cp("dve", qres[1], qres[1][64:128, zc, t0:t0 + 512], ps[64:128, :], [ps])
                    elif zc < 6:
                        self.evac(kres, kres[:, zc - 3, KPAD + t0:KPAD + t0 + 512], ps[:], [ps])
                    elif zc < 11:
                        z = zst[zi % 4]
                        zi += 1
                        self.evac(z, z[:], ps[:], [ps])
                        self.dma(self.s_b[(zc - 9) * 128:(zc - 8) * 128, t0:t0 + 512], z[:], rd=[z], wr=[sb_])
                    elif zc < 13:
                        self.evac(csb[zc - 11], csb[zc - 11][:], ps[:], [ps])
                    elif zc < 15:
                        z = zst[zi % 4]
                        zi += 1
                        self.tt("dve", z, z[:], ps[:], csb[zc - 13][:], ALU.mult, [ps, csb[zc - 13]])
                        self.dma(self.s_u[(zc - 13) * 128:(zc - 12) * 128, t0:t0 + 512], z[:], rd=[z], wr=[su])
                    else:
                        z = zst[zi % 4]
                        zi += 1
                        self.evac(z, z[:], ps[:], [ps])
                        self.dma(self.s_zc[(zc - 15) * 128:(zc - 14) * 128, t0:t0 + 512], z[:], rd=[z], wr=[szc])
                if pc in (1, 2):
                    c0, nv, h0 = (256, 256, 0) if pc == 1 else (0, 128, 4)
                    for j in range(4):
                        ps = pb[6 + (j % 2)]
                        for kc in range(8):
                            self.mm(ps, ps[:, 0:nv], h[:, kc, j * 128:(j + 1) * 128], wt[:, kc, c0:c0 + nv], [wt, h], start=(kc == 0), stop=(kc == 7))
                        if pc == 1:
                            v = vst[j]
                            self.evac(v, v[:, 0:4, 0:64], ps[:, 0:256].rearrange("p (a b) -> p a b", a=4), [ps])
                        else:
                            v = vst[j]
                            self.evac(v, v[:, 4:6, 0:64], ps[:, 0:128].rearrange("p (a b) -> p a b", a=2), [ps])
                            self.dma(self.s_v[t0 + j * 128:t0 + (j + 1) * 128, :], v[:].rearrange("p a b -> p (a b)"), rd=[v], wr=[sv])

    def phase_attn(self, T):
        A, pb = self.A, self.pb
        qres, kres = self.qres, self.kres
        vbuf = [A.alloc([9, 390], BF16) for _ in range(2)]
        for v in vbuf:
            self.memset("pool", v, v[:], 1.0)
        self.bias = A.alloc([18, 256], BF16)
        self.dma(self.bias[:].rearrange("p a b -> p (a b)"), self.s_bias, wr=[self.bias])
        sbl = [A.alloc([256], F32) for _ in range(3)]
        pT = [A.alloc([256], BF16) for _ in range(3)]
        ost = [A.alloc([390], F32) for _ in range(2)]
        lg = [Sub(pb[i], pb[i][:, 0:256]) for i in (0, 1, 4, 5)]
        ops_ = [pb[2], pb[3]]
        so = Tile(self.s_o)
        sv = Tile(self.s_v)
        ui = 0
        bi_ = 0
        li = 0
        for br, dil in enumerate(DILS):
            L = T // dil
            nblk = L // 128
            for c in range(dil):
                for g0 in range(0, nblk, 8):
                    nb = min(8, nblk - g0)
                    vb = vbuf[ui % 2]
                    ui += 1
                    for i in range(nb + 1):
                        kt = g0 + i
                        k0 = -64 + 128 * kt
                        lo = 64 if kt == 0 else 0
                        hi = 64 if kt == nblk else 128
                        r0 = c + dil * (k0 + lo)
                        self.dma(vb[lo:hi, i, :], self.rawap(self.s_v, r0 * 390, [[dil * 390, hi - lo], [1, 390]]), rd=[sv], wr=[vb])
                    for bl in range(nb):
                        b = g0 + bl
                        op = ops_[bi_ % 2]
                        bi_ += 1
                        q0 = c + dil * 128 * b
                        ka0 = KPAD + c + dil * (128 * b - 64)
                        kb0 = ka0 + 128 * dil
                        pend = []
                        for step in range(8):
                            if step < 6:
                                h = step
                                hp, base = h // 2, (h % 2) * 64
                                lgt = lg[li % 4]
                                sb = sbl[li % 3]
                                p = pT[li % 3]
                                li += 1
                                qm = qres[h % 2]
                                LV = 9
                                qap = qm[:, hp, q0:q0 + 127 * dil + 1:dil]
                                if LV >= 1:
                                    self.mm(lgt, lgt[:, 0:128], kres[:, hp, ka0:ka0 + 127 * dil + 1:dil], qap, [kres, qm])
                                    self.mm(lgt, lgt[:, 128:256], kres[:, hp, kb0:kb0 + 127 * dil + 1:dil], qap, [kres, qm])
                                if LV >= 2:
                                    self.tt("dve", sb, sb[:], lgt[:], self.bias[:, br * 6 + h, :], ALU.add, [lgt, self.bias])
                                first, last = (b == 0), (b == nblk - 1)
                                if LV >= 3:
                                    if not first and not last:
                                        self.act(p, p[:], sb[:], AF.Exp, [sb, self.misc], bias=self.zero_c, scale=0.125)
                                    else:
                                        self.act(p, p[:, 0:128], sb[:, 0:128], AF.Exp, [sb, self.misc],
                                                 bias=self.edge_first if first else self.zero_c, scale=0.125)
                                        self.act(p, p[:, 128:256], sb[:, 128:256], AF.Exp, [sb, self.misc],
                                                 bias=self.edge_last if last else self.zero_c, scale=0.125)
                                pend.append((h, p))
                            if step >= 2 and LV >= 4:
                                h, p = pend[step - 2]
                                self.mm(op, op[:, h * 65:(h + 1) * 65], p[:, 0:128], vb[:, bl, h * 65:(h + 1) * 65], [p, vb], start=True, stop=False)
                                self.mm(op, op[:, h * 65:(h + 1) * 65], p[:, 128:256], vb[:, bl + 1, h * 65:(h + 1) * 65], [p, vb], start=False, stop=True)
                        o = ost[bi_ % 2]
                        self.evac(o, o[:], op[:, 0:390], [op])
                        self.dma(self.rawap(self.s_o, (br * self.TM + q0) * 390, [[dil * 390, 128], [1, 390]]), o[:], rd=[o], wr=[so])

    def phase_merge(self, T):
        A, pb = self.A, self.pb
        om = [A.alloc([4, 3, 390], F32) for _ in range(2)]
        sm = A.alloc([6, 65], F32)
        rd_ = A.alloc([6], F32)
        ya = A.alloc([6, 64], F32)
        yb = A.alloc([384], BF16)
        junk = A.alloc([384], BF16)
        ss = A.alloc([2], F32)
        mst = [A.alloc([3, 512], BF16) for _ in range(2)]
        so = Tile(self.s_o)
        smix = Tile(self.s_mix)
        for tt in range(T // 512):
            t0 = tt * 512
            o = om[tt % 2]
            ms = mst[tt % 2]
            for br in range(3):
                self.dma(o[:, :, br, :], self.s_o[br, t0:t0 + 512, :].rearrange("(j p) f -> p j f", p=128), rd=[so], wr=[o])
            ps = pb[tt % 2]
            pv = ps.ap.bitcast(BF16)
            ps2 = pb[2 + tt % 2]
            pv2 = ps2.ap.bitcast(BF16)
            for j in range(4):
                s3 = sm[:].rearrange("p a b -> p (a b)")
                self.tt("dve", sm, s3, o[:, j, 0, :], o[:, j, 1, :], ALU.add, [o])
                self.tt("dve", sm, s3, s3, o[:, j, 2, :], ALU.add, [o, sm])
                self.recip(rd_, rd_[:], sm[:, :, 64], [sm])
                self.tt("dve", ya, ya[:], sm[:, :, 0:64], rd_[:].unsqueeze(2).to_broadcast([128, 6, 64]), ALU.mult, [sm, rd_])
                yaf = ya[:].rearrange("p a b -> p (a b)")
                self.act(junk, junk[:], yaf, AF.Square, [ya], accum=ss[:, 0:1], wr=[ss])
                self.act(ss, ss[:, 1:2], ss[:, 0:1], AF.Sqrt, [ss, self.misc], bias=self.eps_rms, scale=1.0 / 384)
                self.recip(ss, ss[:, 1:2], ss[:, 1:2], [ss])
                self.act(yb, yb[:], yaf, AF.Copy, [ya, ss], scale=ss[:, 1:2])
                for i in range(3):
                    if i < 2:
                        self.tr(ps, pv[:, i * 512 + j * 128:i * 512 + (j + 1) * 128], yb[:, i * 128:(i + 1) * 128], [yb])
                    else:
                        self.tr(ps2, pv2[:, j * 128:(j + 1) * 128], yb[:, i * 128:(i + 1) * 128], [yb])
            self.evac(ms, ms[:, 0:2, :], pv[:, 0:1024].rearrange("p (a b) -> p a b", a=2), [ps])
            self.evac(ms, ms[:, 2, :], pv2[:, 0:512], [ps2])
            self.dma(self.s_mix[0:384, t0:t0 + 512].rearrange("(a p) t -> p a t", p=128), ms[:], rd=[ms], wr=[smix])

    def phase_conv(self, T):
        A, pb, PV = self.A, self.pb, self.PV
        ub = [A.alloc([2, 514], F32) for _ in range(2)]
        bb = [A.alloc([2, 512], F32) for _ in range(2)]
        c1 = A.alloc([512], F32)
        c2 = A.alloc([512], F32)
        yb = A.alloc([2, 512], F32)
        sq = A.alloc([2, 512], BF16)
        rs = A.alloc([512], F32)
        ybn = [A.alloc([2, 512], BF16) for _ in range(2)]
        su, sb_, smix = Tile(self.s_u), Tile(self.s_b), Tile(self.s_mix)
        pv = self.pv
        for tt in range(T // 512):
            t0 = tt * 512
            u, bt, yo = ub[tt % 2], bb[tt % 2], ybn[tt % 2]
            lo = 1 if tt == 0 else 0
            hi = 513 if tt == T // 512 - 1 else 514
            if lo == 1:
                self.memset("pool", u, u[:, :, 0:1], 0.0)
            if hi == 513:
                self.memset("pool", u, u[:, :, 513:514], 0.0)
            self.dma(u[:, :, lo:hi], self.s_u[:, t0 - 1 + lo:t0 - 1 + hi].rearrange("(a p) t -> p a t", p=128), rd=[su], wr=[u])
            self.dma(bt[:], self.s_b[:, t0:t0 + 512].rearrange("(a p) t -> p a t", p=128), rd=[sb_], wr=[bt])
            ps = pb[tt % 2]
            for ci in range(2):
                cw = PV["cw"] + 3 * ci
                self.ts("pool", c1, c1[:], u[:, ci, 1:513], pv[:, cw + 1:cw + 2], ALU.mult, [u, pv])
                self.stt(c2, c2[:], u[:, ci, 0:512], pv[:, cw:cw + 1], c1[:], ALU.mult, ALU.add, [u, pv, c1])
                self.stt(c1, c1[:], u[:, ci, 2:514], pv[:, cw + 2:cw + 3], c2[:], ALU.mult, ALU.add, [u, pv, c2])
                self.tt("pool", yb, yb[:, ci, :], c1[:], bt[:, ci, :], ALU.mult, [c1, bt])
                self.act(sq, sq[:, ci, :], yb[:, ci, :], AF.Square, [yb])
                self.mm(ps, ps[:], self.onesb[:], sq[:, ci, :], [self.onesb, sq], start=(ci == 0), stop=(ci == 1))
            self.act(rs, rs[:], ps[:], AF.Sqrt, [ps, self.misc], bias=self.eps_rms, scale=1.0 / 256)
            self.recip(rs, rs[:], rs[:], [rs])
            for ci in range(2):
                self.tt("dve" if ci == 0 else "pool", yo, yo[:, ci, :], yb[:, ci, :], rs[:], ALU.mult, [yb, rs])
            self.dma(self.s_mix[384:640, t0:t0 + 512].rearrange("(a p) t -> p a t", p=128), yo[:], rd=[yo], wr=[smix])

    def phase_r1(self, T):
        A, pb, PV, pv = self.A, self.pb, self.PV, self.pv
        nt = T // 512
        F = lambda: A.alloc([512], F32)
        zc = A.alloc([10, 514], F32)
        xs = A.alloc([10, 512], F32)
        t1 = [F() for _ in range(2)]
        t2 = [F() for _ in range(2)]
        kk2, rn, tmp, tk = t1[0], t1[1], t2[0], t2[1]
        tw = F()
        kk, kkn = F(), F()
        sg, aa, clw, E1, E2, enl, ta, bd = (F() for _ in range(8))
        lw = sg
        kd = [F(), F()]
        gst = F()
        bst = E1
        yst = [E2, enl]
        tot = A.alloc([8], F32)
        gC = [A.alloc([8], F32) for _ in range(2)]
        ARx = [A.alloc([2, 2, 512], BF16) for _ in range(2)]
        BT = [A.alloc([512], BF16) for _ in range(2)]
        KT = [A.alloc([512], BF16) for _ in range(2)]
        BG = A.alloc([512], BF16)
        KG = A.alloc([512], BF16)
        VT = A.alloc([512], BF16)
        tmA = [A.alloc([4, 128], BF16) for _ in range(2)]
        gx = [A.alloc([2, 4, 2, 128], BF16) for _ in range(2)]
        vtm = A.alloc([4, 128], BF16)
        atz = [A.alloc([128], BF16) for _ in range(2)]
        atk = [[[A.alloc([2, 128], BF16) for _ in range(2)] for _ in range(2)] for _ in range(4)]
        AW = [[[A.alloc([128], BF16) for _ in range(2)] for _ in range(2)] for _ in range(4)]
        Zs = [[A.alloc([128], F32) for _ in range(6)] for _ in range(2)]
        Zp = [[A.alloc([128], F32) for _ in range(5)] for _ in range(2)]
        Rk = [[A.alloc([128], F32) for _ in range(2)] for _ in range(2)]
        RHf = A.alloc([512], BF16)
        RHb = A.alloc([3, 512], BF16)
        GTf = A.alloc([8, 128], BF16)
        Hf = A.alloc([8, 128], F32)
        GTb = A.alloc([3, 8, 128], BF16)
        Hb = A.alloc([3, 8, 128], F32)
        Pst = [A.alloc([9, 128], BF16) for _ in range(3)]
        for t in ARx + gx + [GTf, GTb, Hb] + Pst:
            self.memset("pool", t, t[:], 0.0)
        self.memset("pool", Hf, Hf[:], 0.0)
        ps_at = [pb[0], pb[1]]
        ps_x = [Sub(pb[2 + i // 4], self.psum[:, 1024 + i * 128:1024 + (i + 1) * 128]) for i in range(8)]
        ps_r = [pb[4], pb[5]]
        ps_g1 = [Sub(pb[6], self.psum[hh * 64:(hh + 1) * 64, 3072:3200]) for hh in range(2)]
        ps_g2 = [[Sub(pb[6], self.psum[hh * 64:(hh + 1) * 64, 3200 + c * 64:3264 + c * 64]) for c in range(2)] for hh in range(2)]
        ps_g3 = [Sub(pb[6], self.psum[hh * 64:(hh + 1) * 64, 3328:3456]) for hh in range(2)]
        ps_c = Sub(pb[6], self.psum[:, 3456:3584])
        ps_y = pb[7]
        ps_m = ps_at
        szc, sg_, sbon, syp, srh, sgt, shb = (Tile(self.s_zc), Tile(self.s_g), Tile(self.s_bon), Tile(self.s_yp),
                                              Tile(self.s_rhb), Tile(self.s_gtb), Tile(self.s_hb))
        lr = self.lr
        c3 = lambda t: t[:].rearrange("p (a b) -> p a b", a=8)
        xi = 0
        for tt in range(nt):
            t0 = tt * 512
            lo = 1 if tt == 0 else 0
            hi = 513 if tt == nt - 1 else 514
            if lo == 1:
                self.memset("pool", zc, zc[:, :, 0:1], 0.0)
            if hi == 513:
                self.memset("pool", zc, zc[:, :, 513:514], 0.0)
            self.dma(zc[:, :, lo:hi], self.s_zc[:, t0 - 1 + lo:t0 - 1 + hi].rearrange("(a p) t -> p a t", p=128), rd=[szc], wr=[zc])
            for cc in range(10):
                a1, a2 = t1[cc % 2], t2[cc % 2]
                self.tt("pool", a1, a1[:], zc[:, cc, 0:512], zc[:, cc, 2:514], ALU.add, [zc])
                self.act(a2, a2[:], a1[:], AF.Copy, [a1, pv], scale=pv[:, PV["m2"] + cc:PV["m2"] + cc + 1])
                self.stt(xs, xs[:, cc, :], zc[:, cc, 1:513], pv[:, PV["m1"] + cc:PV["m1"] + cc + 1], a2[:], ALU.mult, ALU.add, [zc, pv, a2])
            self.act(tw, tw[0:32, :], xs[0:32, 9, :], AF.Tanh, [xs])
            self.cp("dve", tw, tw[32:64, :], xs[32:64, 9, :], [xs])
            self.act(tw, tw[64:128, :], xs[64:128, 9, :], AF.Sigmoid, [xs])
            RL = 9
            for hp in range(3):
                if RL < 1:
                    continue
                r_, k_, v_ = xs[:, hp, :], xs[:, 3 + hp, :], xs[:, 6 + hp, :]
                hc = slice(hp * 128, (hp + 1) * 128)
                pg = ps_m[0]
                self.mm(pg, pg[:], lr[:, 4, hc], tw[:], [lr, tw])
                self.evac(gst, gst[:], pg[:], [pg])
                self.dma(self.s_g[hc, t0:t0 + 512], gst[:], rd=[gst], wr=[sg_])
                self.act(kk, kk[:], k_, AF.Copy, [xs, pv], scale=pv[:, PV["kk"] + hp:PV["kk"] + hp + 1])
                self.tt("pool", kk2, kk2[:], kk[:], kk[:], ALU.mult, [kk])
                pn = ps_m[1]
                self.mm(pn, pn[:], self.blk[:], kk2[:], [self.blk, kk2])
                self.act(rn, rn[:], pn[:], AF.Sqrt, [pn])
                self.ts("dve", rn, rn[:], rn[:], 1e-12, ALU.max, [rn])
                self.recip(rn, rn[:], rn[:], [rn])
                self.tt("pool", kkn, kkn[:], kk[:], rn[:], ALU.mult, [kk, rn])
                self.cp("act", VT, VT[:], v_, [xs])
                pt = ps_m[0]
                ptv = pt.ap.bitcast(BF16)
                for np_ in range(4):
                    self.tr(pt, ptv[:, np_ * 128:(np_ + 1) * 128], VT[:, np_ * 128:(np_ + 1) * 128], [VT])
                self.evac(vtm, vtm[:].rearrange("p a b -> p (a b)"), ptv[:, 0:512], [pt])
                for d in range(2):
                    if RL < 2:
                        continue
                    arx, bt_, kt_ = ARx[d], BT[d], KT[d]
                    pw = ps_m[0]
                    self.mm(pw, pw[:], lr[:, d, hc], tw[:], [lr, tw])
                    self.act(sg, sg[:], pw[:], AF.Sigmoid, [pw, pv], bias=pv[:, PV["w0"] + 3 * d + hp:PV["w0"] + 3 * d + hp + 1])
                    pa = ps_m[1]
                    self.mm(pa, pa[:], lr[:, 2 + d, hc], tw[:], [lr, tw])
                    self.act(aa, aa[:], pa[:], AF.Sigmoid, [pa, pv], bias=pv[:, PV["a0"] + 3 * d + hp:PV["a0"] + 3 * d + hp + 1])
                    self.ts("pool", lw, lw[:], sg[:], WSCALE, ALU.mult, [sg])
                    self.scan(clw, clw[:], self.seg[:], lw[:], [self.seg, lw])
                    self.cp("dve", tot, tot[:], c3(clw)[:, :, 63], [clw])
                    if d == 1:
                        self.tt("dve", tmp, tmp[:], lw[:], clw[:], ALU.subtract, [lw, clw])
                        self.tt("dve", clw, c3(clw), c3(tmp), tot[:].unsqueeze(2).to_broadcast([128, 8, 64]), ALU.add, [tmp, tot])
                    self.act(E1, E1[:], clw[:], AF.Exp, [clw])
                    self.act(E2, E2[:], clw[:], AF.Exp, [clw], scale=-1.0)
                    self.act(enl, enl[:], lw[:], AF.Exp, [lw], scale=-1.0)
                    self.act(gC[d], gC[d][:], tot[:], AF.Exp, [tot])
                    self.tt("pool", ta, ta[:], kkn[:], enl[:], ALU.mult, [kkn, enl])
                    for hh in range(2):
                        bs = slice(hh * 64, hh * 64 + 64)
                        self.stt(arx, arx[bs, hh, 0, :], ta[bs, :], -1.0, E1[bs, :], ALU.mult, ALU.mult, [ta, E1])
                        self.tt("pool", arx, arx[bs, hh, 1, :], xs[bs, hp, :], E1[bs, :], ALU.mult, [xs, E1])
                    self.ts("dve", tk, tk[:], aa[:], pv[:, PV["ka"] + hp:PV["ka"] + hp + 1], ALU.mult, [aa, pv],
                            pv[:, PV["oka"] + hp:PV["oka"] + hp + 1], ALU.add)
                    self.tt("pool", kd[d], kd[d][:], k_, tk[:], ALU.mult, [xs, tk])
                    self.tt("pool", bd, bd[:], kkn[:], aa[:], ALU.mult, [kkn, aa])
                    self.tt("dve", bt_, bt_[:], bd[:], E2[:], ALU.mult, [bd, E2])
                    self.tt("dve", kt_, kt_[:], kd[d][:], E2[:], ALU.mult, [kd[d], E2])
                    gcb = gC[d][:].unsqueeze(2).to_broadcast([128, 8, 64])
                    self.tt("dve", BG, c3(BG), c3(bt_), gcb, ALU.mult, [bt_, gC[d]])
                    self.tt("dve", KG, c3(KG), c3(kt_), gcb, ALU.mult, [kt_, gC[d]])
                    if RL < 3:
                        continue
                    p1, p2 = ps_m[0], ps_m[1]
                    p1v, p2v = p1.ap.bitcast(BF16), p2.ap.bitcast(BF16)
                    for np_ in range(4):
                        tsl = slice(np_ * 128, (np_ + 1) * 128)
                        for hh in range(2):
                            pass
                    for np_ in range(4):
                        tsl = slice(np_ * 128, (np_ + 1) * 128)
                        self.tr(p1, p1v[0:128, np_ * 128:(np_ + 1) * 128], arx[0:128, 0, 0, tsl], [arx])
                        self.tr(p2, p2v[0:128, np_ * 128:(np_ + 1) * 128], arx[0:128, 1, 0, tsl], [arx])
                    tmd = tmA[d]
                    self.evac(tmd, tmd[:, :, 0:64], p1v[:, 0:512].rearrange("p (a b) -> p a b", a=4)[:, :, 0:64], [p1])
                    self.evac(tmd, tmd[:, :, 64:128], p2v[:, 0:512].rearrange("p (a b) -> p a b", a=4)[:, :, 64:128], [p2])
                    gxd = gx[d]
                    for q, (srct, pq, pqv) in enumerate(((BG, p1, p1v), (KG, p2, p2v))):
                        for np_ in range(4):
                            self.tr(pq, pqv[:, 512 + np_ * 128:512 + (np_ + 1) * 128], srct[:, np_ * 128:(np_ + 1) * 128], [srct])
                        v4 = pqv[:, 512:1024].rearrange("p (a b) -> p a b", a=4)
                        self.cp("act", gxd, gxd[0:64, q, :, 0, :], v4[0:64], [pq])
                        self.cp("dve", gxd, gxd[64:128, q, :, 1, :], v4[64:128], [pq])
                    def unit(np_, hh, slot):
                        tsl = slice(np_ * 128, (np_ + 1) * 128)
                        tokp = tsl
                        bs = slice(hh * 64, hh * 64 + 64)
                        az, ak, aw = atz[slot], atk[np_][hh][d], AW[np_][hh][d]
                        pat = ps_at[slot]
                        px0 = ps_x[slot * 4]
                        self.mm(pat, pat[:, 0:256], bt_[:, tsl], arx[:, hh, :, tsl], [bt_, arx])
                        self.mm(pat, pat[:, 256:512], kt_[:, tsl], arx[:, hh, :, tsl], [kt_, arx])
                        self.mm(px0, px0[:], arx[:, hh, 0, tsl], bt_[:, tsl], [arx, bt_])
                        p4 = pat[:].rearrange("p (a b) -> p a b", a=4)
                        m4 = self.maskat[d]
                        Z, ZP = Zs[slot], Zp[slot]
                        self.tt("dve", Z[0], Z[0][:], p4[:, 0, :], m4[:, 0, :], ALU.mult, [pat, m4])
                        self.tt("dve", az, az[:], p4[:, 2, :], m4[:, 2, :], ALU.mult, [pat, m4])
                        self.tt("dve", ak, ak[:], p4[:, 1:4:2, :], m4[:, 1:4:2, :], ALU.mult, [pat, m4])
                        self.tt("dve", ZP[0], ZP[0][:], px0[:], self.maskx0[d][:], ALU.mult, [px0, self.maskx0[d]])
                        yield
                        pr = ps_r[slot]
                        R0 = Rk[slot][0]
                        self.mm(pr, pr[:, 0:64], az[:], vtm[:, np_, bs], [az, vtm])
                        self.cp("pool", R0, R0[:, 0:64], tmd[:, np_, bs], [tmd])
                        self.cp("act", R0, R0[:, 64:128], pr[:, 0:64], [pr])
                        yield
                        pz = ps_x[slot * 4 + 1]
                        pz2 = Sub(pat, pat[:, 0:128])
                        for kq in range(6):
                            rk = Rk[slot][kq % 2]
                            rnx = Rk[slot][(kq + 1) % 2] if kq < 5 else aw
                            if kq < 5:
                                self.mm(pz, pz[:], ZP[kq][:], Z[kq][:], [ZP[kq], Z[kq]])
                            if kq < 4:
                                self.mm(pz2, pz2[:], Z[kq][:], ZP[kq][:], [ZP[kq], Z[kq]])
                            self.mm(pr, pr[:, 0:128], Z[kq][:], rk[:], [Z[kq], rk])
                            if kq < 5:
                                self.cp("act", Z[kq + 1], Z[kq + 1][:], pz[:], [pz])
                            if kq < 4:
                                self.cp("act", ZP[kq + 1], ZP[kq + 1][:], pz2[:], [pz2])
                            self.tt("dve", rnx, rnx[:], pr[:, 0:128], rk[:], ALU.add, [pr, rk])
                            yield
                        rb = (4 + slot) * 512
                        pg1 = Sub(pr, self.psum[bs, rb + 128:rb + 256])
                        pg3 = Sub(pr, self.psum[bs, rb + 256:rb + 384])
                        pg2s = [Sub(pr, self.psum[bs, rb + 384 + c * 64:rb + 448 + c * 64]) for c in range(2)]
                        self.mm(pg1, pg1[:], aw[:, 0:64], gxd[:, 0, np_, :, bs], [aw, gxd])
                        self.mm(pg3, pg3[:], aw[:, 0:64], ak[:, 0, :], [aw, ak])
                        for c in range(2):
                            pg2 = pg2s[c]
                            self.mm(pg2, pg2[:], gxd[:, 0, np_, c, bs], aw[:, 64:128], [aw, gxd], start=True, stop=False)
                            self.mm(pg2, pg2[:], gxd[:, 1, np_, c, bs], vtm[:, np_, bs], [gxd, vtm], start=False, stop=True)
                        for c in range(2):
                            ch = np_ * 2 + c
                            if d == 0:
                                gdst_t, gdst = GTf, GTf[bs, ch, bs]
                                hdst_t, hdst = Hf, Hf[bs, ch, bs]
                            else:
                                gdst_t, gdst = GTb, GTb[bs, hp, ch, bs]
                                hdst_t, hdst = Hb, Hb[bs, hp, ch, bs]
                            self.stt(gdst_t, gdst, self.identf[bs, bs], gC[d][bs, ch:ch + 1], pg1[:, c * 64:(c + 1) * 64], ALU.mult, ALU.add,
                                     [self.identf, gC[d], pg1])
                            self.cp("dve", hdst_t, hdst, pg2s[c][:], [pg2s[c]])
                        if d == 0:
                            rdst_t, rdst = RHf, RHf[bs, tokp]
                        else:
                            rdst_t, rdst = RHb, RHb[bs, hp, tokp]
                        self.tt("dve", rdst_t, rdst, pg3[:], arx[bs, hh, 1, tokp], ALU.add, [pg3, arx])

                    for np_ in range(4):
                        if RL < 4:
                            continue
                        gens = [unit(np_, 0, 0), unit(np_, 1, 1)]
                        while gens:
                            for g in list(gens):
                                try:
                                    next(g)
                                except StopIteration:
                                    gens.remove(g)
                if RL < 5:
                    continue
                self.tt("pool", tmp, tmp[:], kd[0][:], kd[1][:], ALU.add, [kd[0], kd[1]])
                self.stt(ta, ta[:], r_, pv[:, PV["rk"] + hp:PV["rk"] + hp + 1], tmp[:], ALU.mult, ALU.mult, [xs, pv, tmp])
                pbn = ps_m[0]
                self.mm(pbn, pbn[:], self.blk[:], ta[:], [self.blk, ta])
                self.tt("dve", bst, bst[:], pbn[:], v_, ALU.mult, [pbn, xs])
                self.dma(self.s_bon[hc, t0:t0 + 512], bst[:], rd=[bst], wr=[sbon])
                Pc = Pst[hp]
                if tt > 0:
                    self.cp("pool", Pc, Pc[:, 0, :], Pc[:, 8, :], [Pc])
                for ch in range(8):
                    self.mm(ps_c, ps_c[:], GTf[:, ch, :], Pc[:, ch, :], [GTf, Pc])
                    self.tt("dve", Pc, Pc[:, ch + 1, :], ps_c[:], Hf[:, ch, :], ALU.add, [ps_c, Hf])
                for np_ in range(4):
                    tokp = slice(np_ * 128, (np_ + 1) * 128)
                    for c in range(2):
                        ch = np_ * 2 + c
                        tokc = slice(ch * 64, (ch + 1) * 64)
                        self.mm(ps_y, ps_y[:, tokc], Pc[:, ch, :], RHf[:, tokc], [Pc, RHf], start=(c == 0), stop=False)
                    for hh in range(2):
                        bs = slice(hh * 64, hh * 64 + 64)
                        for d in range(2):
                            ak, aw = atk[np_][hh][d], AW[np_][hh][d]
                            self.mm(ps_y, ps_y[bs, tokp], aw[:, 64:128], ak[:, 0, :], [aw, ak], start=False, stop=False)
                            self.mm(ps_y, ps_y[bs, tokp], vtm[:, np_, bs], ak[:, 1, :], [vtm, ak], start=False, stop=(d == 1))
                ys = yst[hp % 2]
                self.evac(ys, ys[:], ps_y[:], [ps_y])
                self.dma(self.s_yp[hc, t0:t0 + 512], ys[:], rd=[ys], wr=[syp])
            self.dma(self.s_gtb[tt], GTb[:].rearrange("p a b c -> p (a b c)"), rd=[GTb], wr=[sgt])
            self.dma(self.s_hb[tt], Hb[:].rearrange("p a b c -> p (a b c)"), rd=[Hb], wr=[shb])
            self.dma(self.s_rhb[:, t0:t0 + 512].rearrange("(a p) t -> p a t", p=128), RHb[:], rd=[RHb], wr=[srh])

    def phase_r2(self, T):
        A, pb, PV, pv = self.A, self.pb, self.PV, self.pv
        nt = T // 512
        gtb = [A.alloc([3, 8, 128], BF16) for _ in range(2)]
        hb = [A.alloc([3, 8, 128], F32) for _ in range(2)]
        rhb = [A.alloc([3, 512], BF16) for _ in range(2)]
        yp = [A.alloc([3, 512], F32) for _ in range(2)]
        bon = [A.alloc([3, 512], F32) for _ in range(2)]
        gg = [A.alloc([3, 512], F32) for _ in range(2)]
        Pst = [A.alloc([9, 128], BF16) for _ in range(3)]
        y = A.alloc([512], F32)
        yc = A.alloc([512], F32)
        sq = A.alloc([512], F32)
        sd = A.alloc([512], F32)
        yo = [A.alloc([3, 512], BF16) for _ in range(2)]
        ps_c = Sub(pb[6], self.psum[:, 3072:3200])
        sg_, sbon, syp, srh, sgt, shb, smix = (Tile(self.s_g), Tile(self.s_bon), Tile(self.s_yp),
                                               Tile(self.s_rhb), Tile(self.s_gtb), Tile(self.s_hb), Tile(self.s_mix))
        for hp in range(3):
            self.memset("pool", Pst[hp], Pst[hp][:, 8, :], 0.0)
        for it, tt in enumerate(range(nt - 1, -1, -1)):
            t0 = tt * 512
            par = it % 2
            G, H, RH, YP, BO, GG, YO = gtb[par], hb[par], rhb[par], yp[par], bon[par], gg[par], yo[par]
            self.dma(G[:].rearrange("p a b c -> p (a b c)"), self.s_gtb[tt], rd=[sgt], wr=[G])
            self.dma(H[:].rearrange("p a b c -> p (a b c)"), self.s_hb[tt], rd=[shb], wr=[H])
            fm = lambda s_: s_[0:384, t0:t0 + 512].rearrange("(a p) t -> p a t", p=128)
            self.dma(RH[:], fm(self.s_rhb), rd=[srh], wr=[RH])
            self.dma(YP[:], fm(self.s_yp), rd=[syp], wr=[YP])
            self.dma(BO[:], fm(self.s_bon), rd=[sbon], wr=[BO])
            self.dma(GG[:], fm(self.s_g), rd=[sg_], wr=[GG])
            for hp in range(3):
                Pc = Pst[hp]
                if it > 0:
                    self.cp("pool", Pc, Pc[:, 8, :], Pc[:, 0, :], [Pc])
                for ch in range(7, -1, -1):
                    self.mm(ps_c, ps_c[:], G[:, hp, ch, :], Pc[:, ch + 1, :], [G, Pc])
                    self.tt("dve", Pc, Pc[:, ch, :], ps_c[:], H[:, hp, ch, :], ALU.add, [ps_c, H])
                py = pb[hp % 2]
                for ch in range(8):
                    tokc = slice(ch * 64, (ch + 1) * 64)
                    self.mm(py, py[:, tokc], Pc[:, ch + 1, :], RH[:, hp, tokc], [Pc, RH])
                self.tt("dve", y, y[:], py[:], YP[:, hp, :], ALU.add, [py, YP])
                pm = pb[2 + hp % 2]
                self.mm(pm, pm[:], self.blk64[:], y[:], [self.blk64, y])
                self.tt("dve", yc, yc[:], y[:], pm[:], ALU.subtract, [y, pm])
                self.act(sq, sq[:], yc[:], AF.Square, [yc])
                pvr = pb[4 + hp % 2]
                self.mm(pvr, pvr[:], self.blk64[:], sq[:], [self.blk64, sq])
                self.act(sd, sd[:], pvr[:], AF.Sqrt, [pvr, self.misc], bias=self.eps_ln)
                self.recip(sd, sd[:], sd[:], [sd])
                self.tt("pool", yc, yc[:], yc[:], sd[:], ALU.mult, [yc, sd])
                self.ts("dve", yc, yc[:], yc[:], pv[:, PV["lw"] + hp:PV["lw"] + hp + 1], ALU.mult, [yc, pv],
                        pv[:, PV["lb"] + hp:PV["lb"] + hp + 1], ALU.add)
                self.tt("pool", yc, yc[:], yc[:], BO[:, hp, :], ALU.add, [yc, BO])
                self.tt("dve", YO, YO[:, hp, :], yc[:], GG[:, hp, :], ALU.mult, [yc, GG])
            self.dma(self.s_mix[640:1024, t0:t0 + 512].rearrange("(a p) t -> p a t", p=128), YO[:], rd=[YO], wr=[smix])

    def phase_c(self, T, l, src, dst):
        A, pb = self.A, self.pb
        nt = T // 512
        mt = [A.alloc([8, 512], BF16) for _ in range(2)]
        xt = [A.alloc([4, 1024], F32) for _ in range(2)]
        mo = A.alloc([4, 1024], F32)
        ht = A.alloc([4, 1024], BF16)
        hT = A.alloc([8, 512], BF16)
        f1 = A.alloc([32, 512], BF16)
        rl = [A.alloc([512], BF16) for _ in range(2)]
        wq = [A.alloc([8, 512], BF16) for _ in range(3)]
        junk = A.alloc([1024], BF16)
        ss = A.alloc([4], F32)
        rs = A.alloc([4], F32)
        tmp = A.alloc([1024], F32)
        smix = Tile(self.s_mix)
        dT = Tile(dst)
        wi = 0
        gp = A.alloc([2, 1024], F32)
        self.dma(gp[:, 0, :], self.rawap(self.w["norm_mix_post"], l * D, [[0, 128], [1, D]]), wr=[gp])
        self.dma(gp[:, 1, :], self.rawap(self.w["norm_ffn_post"], l * D, [[0, 128], [1, D]]), wr=[gp])

        def post_norm_add(x, which):
            for j in range(4):
                self.act(junk, junk[:], mo[:, j, :], AF.Square, [mo], accum=ss[:, j:j + 1], wr=[ss])
            self.act(rs, rs[:], ss[:], AF.Sqrt, [ss, self.misc], bias=self.eps_rms, scale=1.0 / D)
            self.recip(rs, rs[:], rs[:], [rs])
            for j in range(4):
                self.stt(tmp, tmp[:], mo[:, j, :], rs[:, j:j + 1], gp[:, which, :], ALU.mult, ALU.mult, [mo, rs, gp])
                self.tt("pool", x, x[:, j, :], x[:, j, :], tmp[:], ALU.add, [x, tmp])

        for tt in range(nt):
            t0 = tt * 512
            m, x = mt[tt % 2], xt[tt % 2]
            self.dma(m[:], self.s_mix[:, t0:t0 + 512].rearrange("(a p) t -> p a t", p=128), rd=[smix], wr=[m])
            self.dma(x[:], src[t0:t0 + 512, :].rearrange("(j p) d -> p j d", p=128), wr=[x])
            for dh in range(2):
                wt = wq[wi % 3]
                wi += 1
                self.dma(wt[:], self.woutb[l][:, dh * 512:(dh + 1) * 512].rearrange("(kc p) c -> p kc c", p=128), wr=[wt])
                for j in range(4):
                    ps = pb[j]
                    for kc in range(8):
                        self.mm(ps, ps[:], m[:, kc, j * 128:(j + 1) * 128], wt[:, kc, :], [m, wt], start=(kc == 0), stop=(kc == 7))
                    self.evac(mo, mo[:, j, dh * 512:(dh + 1) * 512], ps[:], [ps])
            post_norm_add(x, 0)
            self.norm_transpose(x, ht, hT, ss, rs, junk, [pb[0], pb[1]])
            for fp in range(8):
                wt = wq[wi % 3]
                wi += 1
                self.dma(wt[:], self.w1b[l][:, fp * 512:(fp + 1) * 512].rearrange("(kc p) c -> p kc c", p=128), wr=[wt])
                for fc in range(4):
                    ps = pb[fc]
                    for kc in range(8):
                        self.mm(ps, ps[:], wt[:, kc, fc * 128:(fc + 1) * 128], hT[:, kc, :], [wt, hT], start=(kc == 0), stop=(kc == 7))
                    r = rl[fc % 2]
                    self.act(r, r[:], ps[:], AF.Relu, [ps])
                    self.tt("pool", f1, f1[:, fp * 4 + fc, :], r[:], r[:], ALU.mult, [r])
            for dh in range(2):
                for kp in range(4):
                    wt = wq[wi % 3]
                    wi += 1
                    self.dma(wt[:], self.w2b[l][kp * 1024:(kp + 1) * 1024, dh * 512:(dh + 1) * 512].rearrange("(kc p) c -> p kc c", p=128), wr=[wt])
                    for j in range(4):
                        ps = pb[4 + j]
                        for kc in range(8):
                            kg = kp * 8 + kc
                            self.mm(ps, ps[:], f1[:, kg, j * 128:(j + 1) * 128], wt[:, kc, :], [f1, wt], start=(kg == 0), stop=(kg == 31))
                for j in range(4):
                    self.evac(mo, mo[:, j, dh * 512:(dh + 1) * 512], pb[4 + j][:], [pb[4 + j]])
            post_norm_add(x, 1)
            self.dma(dst[t0:t0 + 512, :].rearrange("(j p) d -> p j d", p=128), x[:], rd=[x], wr=[dT])


_WNAMES = [n for n, _ in WEIGHT_SPECS]


def kernel(**inputs):
    xp = np.ascontiguousarray(np.asarray(inputs["x_prompt"], dtype=np.float32))
    xs = np.ascontiguousarray(np.asarray(inputs["x_sample"], dtype=np.float32))
    ncores = 8
    npp, nsp = xp.shape[0] // ncores, xs.shape[0] // ncores
    seq_lens = [xp.shape[1]] * npp + [xs.shape[1]] * nsp
    b = Builder(seq_lens)
    nc = b.build()
    consts = _consts()
    shared = {n: np.ascontiguousarray(np.asarray(inputs[n], dtype=np.float32)) for n in _WNAMES}
    shared.update(consts)
    in_maps = []
    for c in range(ncores):
        m = dict(shared)
        for i in range(npp):
            m[f"x{i}"] = xp[c * npp + i]
        for i in range(nsp):
            m[f"x{npp + i}"] = xs[c * nsp + i]
        in_maps.append(m)
    res = run_bass_kernel_spmd(nc, in_maps, core_ids=list(range(ncores)))
    yp = np.empty_like(xp)
    ys = np.empty_like(xs)
    for c in range(ncores):
        r = res.results[c]
        for i in range(npp):
            yp[c * npp + i] = r[f"y{i}"]
        for i in range(nsp):
            ys[c * nsp + i] = r[f"y{npp + i}"]
    return (yp, ys)
```
